# Optimizing a Trainium2 kernel written in Bass

```python
import math
import jax, jax.numpy as jnp
from jax import lax
import numpy as np

D_MODEL = 1024
BATCH = 2
SEQ = 8192
DEPTH = 1
DEC_BATCH = 16
DEC_SEQ = 16
PAST_LEN = 2048

CHUNK = 64
Q_BLOCK = 128
EPS = 1e-6
D_SSD = 512
SSD_HEAD_DIM = 64
N_SSD_HEADS = D_SSD // SSD_HEAD_DIM
N_SSD_GROUPS = 2
SSD_HEADS_PER_GROUP = N_SSD_HEADS // N_SSD_GROUPS
D_STATE = 128
SSD_CONV_W = 4
SSD_CONV_DIM = D_SSD + 2 * N_SSD_GROUPS * D_STATE
D_FOX = 512
FOX_HEAD_DIM = 64
N_FOX_HEADS = D_FOX // FOX_HEAD_DIM
FOX_SCALE = FOX_HEAD_DIM ** -0.5
D_MIX = D_SSD + D_FOX
IN_COLS = D_SSD + SSD_CONV_DIM + N_SSD_HEADS + 3 * D_FOX + N_FOX_HEADS
D_FF = 2816
FFN_CONV_W = 3

kernel_name = 'hymba_ssd_fox_convffn_stream_step'


def rmsnorm(x, g):
    xf = x.astype(jnp.float32)
    y = xf * lax.rsqrt(jnp.mean(xf * xf, axis=-1, keepdims=True) + EPS)
    return (y * g.astype(jnp.float32)).astype(x.dtype)


def causal_dwconv(x, prev, w, b):
    L = x.shape[1]
    xp = jnp.concatenate([prev.astype(x.dtype), x], axis=1)
    y = b.astype(x.dtype)
    for j in range(w.shape[0]):
        y = y + xp[:, j:j + L] * w[j].astype(x.dtype)
    return y, xp[:, L:]


def split_columns(u):
    sizes = (D_SSD, SSD_CONV_DIM, N_SSD_HEADS, D_FOX, D_FOX, D_FOX, N_FOX_HEADS)
    out, off = [], 0
    for s in sizes:
        out.append(u[..., off:off + s])
        off += s
    return out


def ssd_scan(xs, dt, a, bm, cm, s0):
    b, L, h, p = xs.shape
    n = bm.shape[-1]
    q = min(CHUNK, L)
    c = L // q
    f32 = jnp.float32
    xdt = (xs.astype(f32) * dt[..., None]).reshape(b, c, q, h, p)
    bc = bm.astype(f32).reshape(b, c, q, h, n)
    cc = cm.astype(f32).reshape(b, c, q, h, n)
    a_cs = jnp.cumsum((dt * a).reshape(b, c, q, h), axis=2)
    seg = a_cs[:, :, :, None, :] - a_cs[:, :, None, :, :]
    causal = (jnp.arange(q)[:, None] >= jnp.arange(q)[None, :])[None, None, :, :, None]
    decay = jnp.exp(jnp.where(causal, seg, -jnp.inf))
    scores = jnp.einsum('bclhn,bcshn->bclsh', cc, bc) * decay
    y_diag = jnp.einsum('bclsh,bcshp->bclhp', scores, xdt)
    to_end = jnp.exp(a_cs[:, :, -1:, :] - a_cs)
    chunk_states = jnp.einsum('bclhn,bclh,bclhp->bchpn', bc, to_end, xdt)
    chunk_decay = jnp.exp(a_cs[:, :, -1, :])

    def step(s, inp):
        st, dec = inp
        return s * dec[:, :, None, None] + st, s

    s_final, s_prev = lax.scan(step, s0.astype(f32),
                               (jnp.moveaxis(chunk_states, 1, 0), jnp.moveaxis(chunk_decay, 1, 0)))
    s_prev = jnp.moveaxis(s_prev, 0, 1)
    y_off = jnp.einsum('bclhn,bchpn,bclh->bclhp', cc, s_prev, jnp.exp(a_cs))
    return (y_diag + y_off).reshape(b, L, h, p), s_final


def ssd_mixer(z, xbc, dt_raw, conv_prev, s0, conv_w, conv_b, dt_bias, a_log, d_skip, norm_g):
    b, L, _ = z.shape
    f32 = jnp.float32
    xbc, conv_new = causal_dwconv(xbc, conv_prev, conv_w, conv_b)
    xbc = jax.nn.silu(xbc)
    gn = N_SSD_GROUPS * D_STATE
    xs = xbc[..., :D_SSD].reshape(b, L, N_SSD_HEADS, SSD_HEAD_DIM)
    bm = jnp.repeat(xbc[..., D_SSD:D_SSD + gn].reshape(b, L, N_SSD_GROUPS, D_STATE), SSD_HEADS_PER_GROUP, axis=2)
    cm = jnp.repeat(xbc[..., D_SSD + gn:].reshape(b, L, N_SSD_GROUPS, D_STATE), SSD_HEADS_PER_GROUP, axis=2)
    dt = jax.nn.softplus(dt_raw.astype(f32) + dt_bias.astype(f32))
    a = -jnp.exp(a_log.astype(f32))
    y, s_final = ssd_scan(xs, dt, a, bm, cm, s0)
    y = y + xs.astype(f32) * d_skip.astype(f32)[:, None]
    y = (y.reshape(b, L, D_SSD) * jax.nn.silu(z.astype(f32))).reshape(b, L, N_SSD_GROUPS, D_SSD // N_SSD_GROUPS)
    y = y * lax.rsqrt(jnp.mean(y * y, axis=-1, keepdims=True) + EPS)
    y = y.reshape(b, L, D_SSD) * norm_g.astype(f32)
    return y.astype(z.dtype), conv_new, s_final.astype(s0.dtype)


def fox_block(q, k, v, fq, fk, q_pos, k_pos):
    s = jnp.einsum('bqhd,bkhd->bhqk', q, k).astype(jnp.float32) * FOX_SCALE
    s = s + (jnp.swapaxes(fq, 1, 2)[:, :, :, None] - jnp.swapaxes(fk, 1, 2)[:, :, None, :])
    s = jnp.where((k_pos[None, :] <= q_pos[:, None])[None, None], s, -jnp.inf)
    p = jax.nn.softmax(s, axis=-1)
    return jnp.einsum('bhqk,bkhd->bqhd', p.astype(v.dtype), v)


def fox_prompt(q, k, v, logf):
    b, L, h, d = q.shape
    F = jnp.cumsum(logf, axis=1)
    nb = L // Q_BLOCK
    qb = jnp.moveaxis(q.reshape(b, nb, Q_BLOCK, h, d), 1, 0)
    fb = jnp.moveaxis(F.reshape(b, nb, Q_BLOCK, h), 1, 0)
    pos = jnp.arange(L, dtype=jnp.int32)
    pb = pos.reshape(nb, Q_BLOCK)
    out = lax.map(lambda blk: fox_block(blk[0], k, v, blk[1], F, blk[2], pos), (qb, fb, pb))
    return jnp.moveaxis(out, 0, 1).reshape(b, L, h * d)


def fox_sample(q, k, v, logf, ck, cv, clogf):
    b, T, h, d = q.shape
    P = ck.shape[1]
    k_all = jnp.concatenate([ck.astype(k.dtype), k], axis=1)
    v_all = jnp.concatenate([cv.astype(v.dtype), v], axis=1)
    F = jnp.cumsum(jnp.concatenate([clogf.astype(jnp.float32), logf], axis=1), axis=1)
    q_pos = P + jnp.arange(T, dtype=jnp.int32)
    k_pos = jnp.arange(P + T, dtype=jnp.int32)
    out = fox_block(q, k_all, v_all, F[:, P:], F, q_pos, k_pos)
    return out.reshape(b, T, h * d)


def conv_ffn(n, prev, w_up, cw, cb, w_down):
    u, new = causal_dwconv(n @ w_up, prev, cw, cb)
    g, v = u[..., :D_FF], u[..., D_FF:]
    return (jax.nn.silu(g) * v) @ w_down, new


def setup_inputs(seed: int = 0) -> dict:
    key = jax.random.key(seed)
    ks = jax.random.split(key, 24)
    f32 = jnp.float32
    nrm = lambda k, shape, s=1.0: jax.random.normal(k, shape, f32) * s
    dt0 = jnp.exp(jax.random.uniform(ks[12], (DEPTH, N_SSD_HEADS), f32) * (math.log(0.1) - math.log(0.001)) + math.log(0.001))
    return {
        'x_prompt': nrm(ks[0], (BATCH, SEQ, D_MODEL)),
        'x_sample': nrm(ks[1], (DEC_BATCH, DEC_SEQ, D_MODEL)),
        'cache_fox_k': nrm(ks[2], (DEPTH, DEC_BATCH, PAST_LEN, N_FOX_HEADS, FOX_HEAD_DIM)),
        'cache_fox_v': nrm(ks[3], (DEPTH, DEC_BATCH, PAST_LEN, N_FOX_HEADS, FOX_HEAD_DIM)),
        'cache_fox_logf': jax.nn.log_sigmoid(nrm(ks[4], (DEPTH, DEC_BATCH, PAST_LEN, N_FOX_HEADS)) + 3.0),
        'state_ssd': nrm(ks[5], (DEPTH, DEC_BATCH, N_SSD_HEADS, SSD_HEAD_DIM, D_STATE), 0.5),
        'state_ssd_conv': nrm(ks[6], (DEPTH, DEC_BATCH, SSD_CONV_W - 1, SSD_CONV_DIM)),
        'state_ffn_conv': nrm(ks[7], (DEPTH, DEC_BATCH, FFN_CONV_W - 1, 2 * D_FF)),
        'norm1_g': 1.0 + nrm(ks[8], (DEPTH, D_MODEL), 0.02),
        'w_in': nrm(ks[9], (DEPTH, D_MODEL, IN_COLS), D_MODEL ** -0.5),
        'ssd_conv_w': nrm(ks[10], (DEPTH, SSD_CONV_W, SSD_CONV_DIM), SSD_CONV_W ** -0.5),
        'ssd_conv_b': nrm(ks[11], (DEPTH, SSD_CONV_DIM), 0.02),
        'ssd_dt_bias': dt0 + jnp.log(-jnp.expm1(-dt0)),
        'ssd_a_log': jnp.log(jax.random.uniform(ks[13], (DEPTH, N_SSD_HEADS), f32, 1.0, 16.0)),
        'ssd_d': 1.0 + nrm(ks[14], (DEPTH, N_SSD_HEADS), 0.1),
        'ssd_norm_g': 1.0 + nrm(ks[15], (DEPTH, D_SSD), 0.02),
        'fox_f_bias': jax.random.uniform(ks[16], (DEPTH, N_FOX_HEADS), f32, 1.0, 5.0),
        'w_out': nrm(ks[17], (DEPTH, D_MIX, D_MODEL), D_MIX ** -0.5),
        'norm2_g': 1.0 + nrm(ks[18], (DEPTH, D_MODEL), 0.02),
        'w_up': nrm(ks[19], (DEPTH, D_MODEL, 2 * D_FF), D_MODEL ** -0.5),
        'ffn_conv_w': nrm(ks[20], (DEPTH, FFN_CONV_W, 2 * D_FF), FFN_CONV_W ** -0.5),
        'ffn_conv_b': nrm(ks[21], (DEPTH, 2 * D_FF), 0.02),
        'w_down': nrm(ks[22], (DEPTH, D_FF, D_MODEL), D_FF ** -0.5),
        'final_norm_g': 1.0 + nrm(ks[23], (D_MODEL,), 0.02),
    }


def reference(x_prompt, x_sample, cache_fox_k, cache_fox_v, cache_fox_logf, state_ssd, state_ssd_conv,
              state_ffn_conv, norm1_g, w_in, ssd_conv_w, ssd_conv_b, ssd_dt_bias, ssd_a_log, ssd_d,
              ssd_norm_g, fox_f_bias, w_out, norm2_g, w_up, ffn_conv_w, ffn_conv_b, w_down, final_norm_g):

    def trunk(x, fox_cache, ssd_state, ssd_conv_state, ffn_conv_state):
        b, L, _ = x.shape
        ks_, vs_, lfs_, sss_, scs_, fcs_ = [], [], [], [], [], []
        for l in range(DEPTH):
            n = rmsnorm(x, norm1_g[l])
            z, xbc, dt_raw, q, k, v, f_raw = split_columns(n @ w_in[l])
            y_ssd, sc_new, ss_new = ssd_mixer(z, xbc, dt_raw, ssd_conv_state[l], ssd_state[l], ssd_conv_w[l],
                                              ssd_conv_b[l], ssd_dt_bias[l], ssd_a_log[l], ssd_d[l], ssd_norm_g[l])
            q = q.reshape(b, L, N_FOX_HEADS, FOX_HEAD_DIM)
            k = k.reshape(b, L, N_FOX_HEADS, FOX_HEAD_DIM)
            v = v.reshape(b, L, N_FOX_HEADS, FOX_HEAD_DIM)
            logf = jax.nn.log_sigmoid(f_raw.astype(jnp.float32) + fox_f_bias[l].astype(jnp.float32))
            if fox_cache is None:
                y_fox = fox_prompt(q, k, v, logf)
            else:
                y_fox = fox_sample(q, k, v, logf, fox_cache[0][l], fox_cache[1][l], fox_cache[2][l])
            h = x + jnp.concatenate([y_ssd, y_fox.astype(x.dtype)], axis=-1) @ w_out[l]
            y_ffn, fc_new = conv_ffn(rmsnorm(h, norm2_g[l]), ffn_conv_state[l], w_up[l], ffn_conv_w[l],
                                     ffn_conv_b[l], w_down[l])
            x = h + y_ffn
            ks_.append(k)
            vs_.append(v)
            lfs_.append(logf.astype(x.dtype))
            sss_.append(ss_new)
            scs_.append(sc_new)
            fcs_.append(fc_new)
        return (rmsnorm(x, final_norm_g), jnp.stack(ks_), jnp.stack(vs_), jnp.stack(lfs_),
                jnp.stack(sss_), jnp.stack(scs_), jnp.stack(fcs_))

    dt_p = x_prompt.dtype
    bp = x_prompt.shape[0]
    (y_prompt, p_fox_k, p_fox_v, p_fox_logf, p_ssd, p_ssd_conv, p_ffn_conv) = trunk(
        x_prompt, None,
        jnp.zeros((DEPTH, bp, N_SSD_HEADS, SSD_HEAD_DIM, D_STATE), dt_p),
        jnp.zeros((DEPTH, bp, SSD_CONV_W - 1, SSD_CONV_DIM), dt_p),
        jnp.zeros((DEPTH, bp, FFN_CONV_W - 1, 2 * D_FF), dt_p))
    (y_sample, s_fox_k, s_fox_v, s_fox_logf, s_ssd, s_ssd_conv, s_ffn_conv) = trunk(
        x_sample, (cache_fox_k, cache_fox_v, cache_fox_logf), state_ssd, state_ssd_conv, state_ffn_conv)
    return (y_prompt, y_sample, p_fox_k, p_fox_v, p_fox_logf, p_ssd, p_ssd_conv, p_ffn_conv,
            s_fox_k, s_fox_v, s_fox_logf, s_ssd, s_ssd_conv, s_ffn_conv)
```

```python
import numpy as np
import concourse.bass as bass
import concourse.mybir as mybir
from concourse.bass_utils import run_bass_kernel_spmd
from contextlib import ExitStack

F32 = mybir.dt.float32
BF16 = mybir.dt.bfloat16
AF = mybir.ActivationFunctionType
ALU = mybir.AluOpType

D = 1024
NH = 8
DFF = 2816
INC = 3088
EPS = 1e-6
NEG = -30000.0
ENG = ["sync", "scalar", "vector", "gpsimd", "tensor"]


class T:
    def __init__(self, ap, name):
        self.ap = ap
        self.name = name
        self.lw = None
        self.rd = []
        self.dsem = None
        self.dsem_sw = None
        self.persist = False

    def __getitem__(self, k):
        return self.ap[k]


class TV:
    def __init__(self, parent, ap):
        object.__setattr__(self, "parent", parent)
        object.__setattr__(self, "ap", ap)

    def __getattr__(self, k):
        return getattr(object.__getattribute__(self, "parent"), k)

    def __setattr__(self, k, v):
        setattr(object.__getattribute__(self, "parent"), k, v)


class Slot:
    def __init__(self, sem):
        self.sem = sem
        self.count = 0


class Op:
    __slots__ = ("eng", "fn", "deps", "is_dma", "tile", "needed", "incval", "soft", "gi", "seg", "cost", "fence")

    def __init__(self, eng, fn):
        self.eng = eng
        self.fn = fn
        self.deps = []
        self.is_dma = False
        self.tile = None
        self.needed = False
        self.incval = 0
        self.soft = []
        self.gi = 0
        self.seg = 0
        self.cost = None
        self.fence = False


import os
STOP = int(os.environ.get("KSTOP", "9"))


class Prog:
    def __init__(self, nc, stack):
        self.nc = nc
        self.stack = stack
        self.ops = {e: [] for e in ENG}
        self.esem = {e: stack.enter_context(nc.semaphore("es_" + e)) for e in ENG}
        self.dtiles = []
        self.nsem = 0
        self.free_slots = []
        self.all_slots = []
        self.order = []
        self.seg = 0
        self.last_pe_w = {}

    def get_slot(self):
        if self.free_slots:
            return self.free_slots.pop()
        sl = Slot(self.stack.enter_context(self.nc.semaphore("ds%d" % self.nsem)))
        self.nsem += 1
        self.all_slots.append(sl)
        return sl

    def _tok(self, p, o):
        if p is None or p is o:
            return
        if p.is_dma:
            o.deps.append(("d", p.tile, p.tile.count))
            o.soft.append(p)
        else:
            if p.eng == o.eng and p.eng == "tensor":
                o.soft.append(p)
                return
            p.needed = True
            o.deps.append(("c", p))

    def op(self, eng, fn, r=(), w=(), dma=None, cost=None):
        o = Op(eng, fn)
        o.gi = len(self.order)
        o.seg = self.seg
        o.cost = cost
        self.order.append(o)
        for t in r:
            self._tok(t.lw, o)
        for t in w:
            self._tok(t.lw, o)
            for q in t.rd:
                self._tok(q, o)
        if dma is not None:
            o.is_dma = True
            attr = "dsem_sw" if eng == "gpsimd" else "dsem"
            if getattr(dma, attr) is None:
                setattr(dma, attr, self.get_slot())
                if dma not in self.dtiles:
                    self.dtiles.append(dma)
            sl_ = getattr(dma, attr)
            o.tile = sl_
            if getattr(sl_, "last", None) is not None:
                o.soft.append(sl_.last)
            sl_.last = o
            sl_.count += 1
        for t in r:
            t.rd.append(o)
        for t in w:
            t.lw = o
            t.rd = []
        self.ops[eng].append(o)
        return o

    def barrier(self, bar_src, bar_tile):
        o = Op("sync", lambda e: e.dma_start(out=bar_tile.ap, in_=bar_src))
        o.fence = True
        o.gi = len(self.order)
        o.seg = self.seg
        self.order.append(o)
        for e in ENG:
            if e == "sync":
                continue
            if self.ops[e]:
                last = None
                for q in reversed(self.ops[e]):
                    if q.fn is not None and not q.is_dma:
                        last = q
                        break
                if last is not None:
                    last.needed = True
                    o.deps.append(("c", last))
        for sl in self.all_slots:
            if sl.count:
                o.deps.append(("d", sl, sl.count))
        o.is_dma = True
        if bar_tile.dsem is None:
            bar_tile.dsem = self.get_slot()
            bar_tile.persist = True
        o.tile = bar_tile.dsem
        bar_tile.dsem.count += 1
        self.ops["sync"].append(o)
        for e in ENG:
            if e == "sync":
                continue
            w = Op(e, None)
            w.fence = True
            w.gi = len(self.order)
            w.seg = self.seg
            self.order.append(w)
            w.deps.append(("d", bar_tile.dsem, bar_tile.dsem.count))
            self.ops[e].append(w)
        self.seg += 1
        for sl in self.all_slots:
            sl.last = None
        keep = []
        for t in self.dtiles:
            if getattr(t, "persist", False):
                keep.append(t)
            else:
                for attr in ("dsem", "dsem_sw"):
                    sl_ = getattr(t, attr)
                    if sl_ is not None and sl_ not in self.free_slots:
                        self.free_slots.append(sl_)
                    setattr(t, attr, None)
                t.lw = None
                t.rd = []
        self.dtiles = keep

    def emit(self, ename, eng):
        seen = {}
        for o in self.ops[ename]:
            for d in o.deps:
                if d[0] == "c":
                    p = d[1]
                    sem, val = self.esem[p.eng], p.incval
                else:
                    sem, val = d[1].sem, 16 * d[2]
                key = id(sem)
                if seen.get(key, 0) >= val:
                    continue
                seen[key] = val
                eng.wait_ge(sem, val)
            if o.fn is None:
                continue
            ins = o.fn(eng)
            if o.is_dma:
                ins.then_inc(o.tile.sem, 16)
            elif o.needed:
                ins.then_inc(self.esem[ename], 1)

    def reschedule(self, window=48):
        COST = {"tensor": 0.25, "scalar": 0.6, "vector": 0.55, "gpsimd": 0.8}
        for e in ENG:
            ops = self.ops[e]
            out = []
            i = 0
            while i < len(ops):
                j = i
                while j < len(ops) and not ops[j].fence:
                    j += 1
                out.append((ops[i:j], ops[j] if j < len(ops) else None))
                i = j + 1
            self._segs = getattr(self, "_segs", {})
            self._segs[e] = out
        nseg = max(len(v) for v in self._segs.values())
        fin = {}
        for sidx in range(nseg):
            qs = {}
            for e in ENG:
                if sidx < len(self._segs[e]):
                    qs[e] = list(self._segs[e][sidx][0])
                else:
                    qs[e] = []
            inseg = set()
            for e in ENG:
                for o in qs[e]:
                    inseg.add(id(o))
            t_eng = {e: 0.0 for e in ENG}
            newq = {e: [] for e in ENG}
            remaining = sum(len(q) for q in qs.values())
            heads = {e: 0 for e in ENG}
            done = set()

            def preds(o):
                r = []
                for d in o.deps:
                    if d[0] == "c":
                        r.append(d[1])
                for p in o.soft:
                    r.append(p)
                return r

            while remaining:
                best = None
                for e in ENG:
                    q = qs[e]
                    cnt = 0
                    for k in range(len(q)):
                        o = q[k]
                        if o is None:
                            continue
                        cnt += 1
                        if cnt > window:
                            break
                        ok = True
                        rt = 0.0
                        for p in preds(o):
                            if id(p) in inseg:
                                if id(p) not in done:
                                    ok = False
                                    break
                                rt = max(rt, fin[id(p)])
                        if not ok:
                            continue
                        st = max(rt, t_eng[e])
                        key = (st, o.gi)
                        if best is None or key < best[0]:
                            best = (key, e, k, o, st)
                        if rt <= t_eng[e]:
                            break
                assert best is not None, "scheduler deadlock"
                _, e, k, o, st = best
                qs[e][k] = None
                while qs[e] and qs[e][0] is None:
                    qs[e].pop(0)
                if o.is_dma:
                    dur = o.cost if o.cost is not None else 3.0
                    t_eng[e] = st + 0.15
                else:
                    dur = o.cost if o.cost is not None else COST.get(e, 0.5)
                    t_eng[e] = st + dur
                fin[id(o)] = st + dur
                done.add(id(o))
                newq[e].append(o)
                remaining -= 1
            for e in ENG:
                if sidx < len(self._segs[e]):
                    self._segs[e][sidx] = (newq[e], self._segs[e][sidx][1])
            sf = self._segs["sync"][sidx][1] if sidx < len(self._segs["sync"]) else None
            if sf is not None:
                sf.deps = [d for d in sf.deps if d[0] != "c"]
                for e in ENG:
                    if e == "sync":
                        continue
                    last = None
                    for ss in range(sidx, -1, -1):
                        if ss < len(self._segs[e]):
                            for q in reversed(self._segs[e][ss][0]):
                                if q.fn is not None and not q.is_dma:
                                    last = q
                                    break
                        if last is not None:
                            break
                    if last is not None:
                        last.needed = True
                        sf.deps.append(("c", last))
        for e in ENG:
            flat = []
            for (lst, fence) in self._segs[e]:
                flat.extend(lst)
                if fence is not None:
                    flat.append(fence)
            self.ops[e] = flat

    def finalize(self):
        for e in ENG:
            c = 0
            for o in self.ops[e]:
                if o.needed and not o.is_dma:
                    c += 1
                    o.incval = c


class Arena:
    def __init__(self, ap32, nwords):
        self.ap = ap32
        self.n = nwords
        self.off = 0
        self.k = 0

    def reset(self):
        self.off = 0

    def alloc(self, shape, dt, name=None):
        free = 1
        for s in shape[1:]:
            free *= s
        words = free if dt in (F32, mybir.dt.int32) else (free + 1) // 2
        words = (words + 7) // 8 * 8
        assert self.off + words <= self.n, "arena overflow %d+%d>%d (%s)" % (self.off, words, self.n, name)
        v = self.ap[0:shape[0], self.off:self.off + words]
        self.off += words
        if dt != F32:
            v = v.bitcast(dt)
        v = v[:, 0:free]
        if len(shape) == 3:
            v = v.rearrange("p (a b) -> p a b", a=shape[1])
        self.k += 1
        return T(v, name or ("t%d" % self.k))


def build(LP, n_samp=2, LS=16, LC=2048):
    nc = bass.Bass("TRN2", target_bir_lowering=False)
    seqs = [dict(L=LP, Lc=0)] + [dict(L=LS, Lc=LC) for _ in range(n_samp)]
    NS = len(seqs)

    def din(name, shape):
        return nc.dram_tensor(name, list(shape), F32, kind="ExternalInput").ap()

    def dout(name, shape):
        return nc.dram_tensor(name, list(shape), F32, kind="ExternalOutput").ap()

    def dscr(name, shape, dt=F32):
        return nc.dram_tensor(name, list(shape), dt, kind="Internal").ap()

    w_in = din("w_in", [D, INC])
    w_out = din("w_out", [D, D])
    w_up = din("w_up", [D, 2 * DFF])
    w_down = din("w_down", [DFF, D])
    g1 = din("g1", [128, 8])
    g2 = din("g2", [128, 8])
    gF = din("gF", [128, 8])
    gS = din("gS", [128, 4])
    cw = din("cw", [128, 8, 4])
    cb = din("cb", [128, 8])
    fw = din("fw", [128, 44, 3])
    fb = din("fb", [128, 44])
    dtb = din("dtb", [128, 8])
    alog = din("alog", [128, 8])
    dsk = din("dsk", [128, 8])
    fbias = din("fbias", [128, 8])
    cst = din("cst", [128, 6, 128])
    cvec = din("cvec", [128, 16])
    rmask = din("rmask", [128, 512])
    bar_src = din("bar_src", [1, 16])
    NQ = 4
    qidx = nc.dram_tensor("qidx", [128, 8], mybir.dt.int32, kind="ExternalInput").ap()
    S = []
    for i, sq in enumerate(seqs):
        L, Lc = sq["L"], sq["Lc"]
        TK = Lc + L
        d = dict(L=L, Lc=Lc, TK=TK)
        d["xT"] = din("xT_%d" % i, [D, L])
        d["st0"] = din("st0_%d" % i, [NH, 128, 64])
        d["cprev"] = din("cprev_%d" % i, [D, 3])
        d["fprev"] = din("fprev_%d" % i, [128, 44, 2])
        if Lc:
            d["cKT"] = din("cKT_%d" % i, [NH, 64, Lc])
            d["cV"] = din("cV_%d" % i, [NH, Lc, 64])
            d["cLF"] = din("cLF_%d" % i, [NH, Lc])
        d["yT"] = dout("yT_%d" % i, [D, (L // NQ) if i == 0 else L])
        d["kT"] = dout("kT_%d" % i, [512, L])
        d["vT"] = dout("vT_%d" % i, [512, L])
        d["lf"] = dout("lf_%d" % i, [NH, L])
        d["sst"] = dout("sst_%d" % i, [NH, 128, 64])
        d["scv"] = dout("scv_%d" % i, [D, 3])
        d["fcv"] = dout("fcv_%d" % i, [128, 44, 2])
        d["U"] = dscr("U_%d" % i, [INC, 3 + L])
        d["XA"] = dscr("XA_%d" % i, [D, L], BF16)
        d["ZS"] = dscr("ZS_%d" % i, [512, L], BF16)
        if i == 0:
            d["Yq_t"] = nc.dram_tensor("Yq", [D, NQ * (2 + L // NQ)], BF16)
            d["Yq"] = d["Yq_t"].ap().rearrange("r (j w) -> r j w", j=NQ)
            d["Yqv"] = d["Yq_t"].ap().rearrange("r (j w) -> (r j) w", j=NQ)
            d["xw"] = din("xw", [D, 2 + L // NQ])
            CHd = L // NQ
            d["QQ_t"] = nc.dram_tensor("QQ", [512, NQ * (2 + CHd)], BF16)
            d["KQ_t"] = nc.dram_tensor("KQ", [512, (NQ + 1) * CHd], BF16)
            d["VQ_t"] = nc.dram_tensor("VQ", [512, (NQ + 1) * CHd], BF16)
            d["FAQ_t"] = nc.dram_tensor("FAQ", [NH * 3, NQ * (2 + CHd)], BF16)
            d["FAK_t"] = nc.dram_tensor("FAK", [NH * 3, (NQ + 1) * CHd], BF16)
            d["YF"] = dscr("YF", [512, 2 + CHd], BF16)
            d["YS"] = dscr("YS", [512, 2 + CHd], BF16)
            d["XAq_t"] = nc.dram_tensor("XAq", [D, (NQ + 1) * CHd], BF16)
            d["ZSq_t"] = nc.dram_tensor("ZSq", [512, (NQ + 1) * CHd], BF16)
            d["DTq_t"] = nc.dram_tensor("DTq", [NH, (NQ + 1) * CHd], F32)
            d["dpad"] = din("dpad", [NH, CHd])
            d["sidx"] = nc.dram_tensor("sidx", [128, NH * 20], mybir.dt.int32, kind="ExternalInput").ap()
            d["zsrc"] = din("zsrc", [128, CHd])
            d["fpad"] = din("fpad", [NH * 3, CHd])
            d["gidx"] = nc.dram_tensor("gidx", [128, NH * 10], mybir.dt.int32, kind="ExternalInput").ap()
        else:
            d["Y"] = dscr("Y_%d" % i, [D, L], BF16)
        d["LFA"] = dscr("LFA_%d" % i, [NH, TK])
        d["FA"] = dscr("FA_%d" % i, [NH, 6, TK], BF16)
        S.append(d)
    wob = dscr("wob", [8, 128, 8, 128], BF16)
    wdb = dscr("wdb", [8, 128, 22, 128], BF16)
    wub = dscr("wub", [44, 128, 8, 128], BF16)
    wor = dscr("wor", [D, D], BF16)
    wdr = dscr("wdr", [DFF, D], BF16)
    wur = dscr("wur", [D, 2 * DFF], BF16)
    bar_d = dscr("bar_d", [1, 16])

    def ydst(d, r0, r1, t0, n):
        if "Yq" not in d:
            return [(d["Y"][r0:r1, t0:t0 + n], 0, n)]
        L_ = d["L"]
        CHq = L_ // NQ
        res_ = []
        t = t0
        while t < t0 + n:
            e = min(t0 + n, (t // CHq + 1) * CHq)
            j = t // CHq
            res_.append((d["Yq"][r0:r1, j, 2 + t % CHq:2 + t % CHq + (e - t)], t - t0, e - t0))
            if e % CHq == 0 and e < L_:
                res_.append((d["Yq"][r0:r1, j + 1, 0:2], e - 2 - t0, e - t0))
            t = e
        return res_

    stack = ExitStack()
    with stack:
        P = Prog(nc, stack)
        NW = 47000
        arena_t = stack.enter_context(nc.sbuf_tensor("arena", [128, NW], F32))
        A = Arena(arena_t, NW)
        psf = [T(stack.enter_context(nc.psum_tensor("psf%d" % i, [128, 512], F32)), "psf%d" % i) for i in range(6)]
        psb = [T(stack.enter_context(nc.psum_tensor("psb%d" % i, [128, 1024], BF16)), "psb%d" % i) for i in range(2)]
        bar_t = T(bar_d, "bar")
        bar_t.persist = True
        cslot = P.get_slot()
        pctr = [0, 0]

        def PS():
            pctr[0] += 1
            return psf[pctr[0] % 4]

        acc_ctr = [0]

        def PACC():
            acc_ctr[0] += 1
            return psf[4 + acc_ctr[0] % 2]

        def PB():
            pctr[1] += 1
            return psb[pctr[1] % 2]

        rr = [0]

        def dq():
            return "sync"

        def load(dst, dst_ap, src_ap, q=None, extra_r=(), extra_w=()):
            cast = (dst_ap.dtype != src_ap.dtype)
            e = "gpsimd" if cast else (q or "sync")
            return P.op(e, lambda g, a=dst_ap, b=src_ap: g.dma_start(out=a, in_=b), r=extra_r, w=(dst,) + tuple(extra_w), dma=dst)

        def store(src, dst_ap, src_ap, q=None, extra_w=(), extra_r=()):
            cast = (dst_ap.dtype != src_ap.dtype)
            e = "gpsimd" if cast else (q or "sync")
            return P.op(e, lambda g, a=dst_ap, b=src_ap: g.dma_start(out=a, in_=b), r=(src,) + tuple(extra_r), w=tuple(extra_w), dma=src)

        def d2d(tile, dst_ap, src_ap, r=(), w=()):
            cast = (dst_ap.dtype != src_ap.dtype)
            e = "gpsimd" if cast else "sync"
            return P.op(e, lambda g, a=dst_ap, b=src_ap: g.dma_start(out=a, in_=b), r=r, w=w, dma=tile)

        def barrier():
            P.barrier(bar_src, bar_t)

        C_ident = A.alloc([128, 128], F32, "ident")
        C_mask = A.alloc([128, 128], F32, "maskf")
        C_low = A.alloc([128, 128], F32, "low")
        C_identb = A.alloc([128, 128], BF16, "identb")
        C_maskb = A.alloc([128, 128], BF16, "maskb")
        C_onesb = A.alloc([128, 128], BF16, "onesb")
        C_onesf = A.alloc([128, 128], F32, "onesf")
        C_vec = A.alloc([128, 16], F32, "cvec")
        C_rm = A.alloc([128, 512], F32, "rmask")
        C_g1 = A.alloc([128, 8], F32); C_g2 = A.alloc([128, 8], F32); C_gF = A.alloc([128, 8], F32)
        C_gS = A.alloc([128, 4], F32)
        C_cw = A.alloc([128, 8, 4], F32); C_cb = A.alloc([128, 8], F32)
        C_fw = A.alloc([128, 44, 3], F32); C_fb = A.alloc([128, 44], F32)
        C_dtb = A.alloc([128, 8], F32); C_al = A.alloc([128, 8], F32); C_dsk = A.alloc([128, 8], F32)
        C_fbs = A.alloc([128, 8], F32)
        C_qidx = A.alloc([128, 8], mybir.dt.int32, "qidx")
        C_zero = A.alloc([128, 2], BF16, "zero")
        C_aneg = A.alloc([128, 8], F32)
        C_nfb = A.alloc([128, 8], F32)
        for t_ in (C_ident, C_mask, C_low, C_onesf, C_vec, C_rm, C_g1, C_g2, C_gF, C_gS, C_cw, C_cb, C_fw, C_fb, C_dtb, C_al, C_dsk, C_fbs, C_qidx):
            t_.dsem = cslot
            t_.persist = True
        load(C_ident, C_ident.ap, cst[:, 0, :])
        load(C_mask, C_mask.ap, cst[:, 1, :])
        load(C_low, C_low.ap, cst[:, 2, :])
        load(C_onesf, C_onesf.ap, cst[:, 3, :])
        load(C_vec, C_vec.ap, cvec)
        load(C_rm, C_rm.ap, rmask)
        for t_, s_ in ((C_g1, g1), (C_g2, g2), (C_gF, gF), (C_gS, gS), (C_cw, cw), (C_cb, cb), (C_fw, fw), (C_fb, fb),
                       (C_dtb, dtb), (C_al, alog), (C_dsk, dsk), (C_fbs, fbias), (C_qidx, qidx)):
            load(t_, t_.ap, s_)
        P.op("vector", lambda e: e.memset(C_zero.ap, 0.0), w=(C_zero,))
        for kt_ in range(8):
            store(C_zero, S[0]["Yq"][kt_ * 128:(kt_ + 1) * 128, 0, 0:2], C_zero.ap)
        P.op("vector", lambda e: e.tensor_copy(out=C_identb.ap, in_=C_ident.ap), r=(C_ident,), w=(C_identb,))
        P.op("vector", lambda e: e.tensor_copy(out=C_maskb.ap, in_=C_mask.ap), r=(C_mask,), w=(C_maskb,))
        P.op("vector", lambda e: e.tensor_copy(out=C_onesb.ap, in_=C_onesf.ap), r=(C_onesf,), w=(C_onesb,))
        P.op("scalar", lambda e: e.activation(out=C_aneg.ap, in_=C_al.ap, func=AF.Exp), r=(C_al,), w=(C_aneg,))
        P.op("vector", lambda e: e.tensor_scalar(out=C_aneg.ap, in0=C_aneg.ap, scalar1=-1.0, scalar2=None, op0=ALU.mult), r=(C_aneg,), w=(C_aneg,))
        P.op("vector", lambda e: e.tensor_scalar(out=C_nfb.ap, in0=C_fbs.ap, scalar1=-1.0, scalar2=None, op0=ALU.mult), r=(C_fbs,), w=(C_nfb,))
        wrT = T(None, "wrT")
        wrT.persist = True
        d2d(wrT, wor, w_out, w=(wrT,))
        d2d(wrT, wdr, w_down, w=(wrT,))
        d2d(wrT, wur, w_up, w=(wrT,))
        for m in range(8):
            d2d(bar_t, wob[m], wor[:, m * 128:(m + 1) * 128].rearrange("(k p) c -> p k c", p=128), r=(wrT,))
            d2d(bar_t, wdb[m], wdr[:, m * 128:(m + 1) * 128].rearrange("(k p) c -> p k c", p=128), r=(wrT,))
        for m in range(44):
            d2d(bar_t, wub[m], wur[:, m * 128:(m + 1) * 128].rearrange("(k p) c -> p k c", p=128), r=(wrT,))
        for d in S:
            d2d(bar_t, d["U"][512:1536, 0:3], d["cprev"])
        base0 = A.off

        def vcol(j):
            return C_vec.ap[:, j:j + 1]

        Wb = A.alloc([128, 8, INC], BF16, "Wb")
        wfs = [A.alloc([128, 772], F32, "wf") for _ in range(2)]
        wfi = 0
        for kt in range(8):
            for c0 in range(0, INC, 772):
                wf = wfs[wfi % 2]
                wfi += 1
                load(wf, wf.ap, w_in[kt * 128:(kt + 1) * 128, c0:c0 + 772])
                P.op("vector", lambda e, a=Wb.ap[:, kt, c0:c0 + 772], b=wf.ap, s=C_g1.ap[:, kt:kt + 1]:
                     e.tensor_scalar(out=a, in0=b, scalar1=s, scalar2=None, op0=ALU.mult), r=(wf, C_g1), w=(Wb,))
        base1 = A.off
        mtiles = [(m0, min(128, INC - m0)) for m0 in range(0, INC, 128)]
        for d in S:
            L = d["L"]
            TT = min(512, L)
            nb = 2
            xb = [A.alloc([128, 8, TT], BF16, "xb") for _ in range(nb)]
            xq = [A.alloc([128, 8, TT], BF16, "xq") for _ in range(nb)]
            rs = [A.alloc([128, TT], F32, "rs") for _ in range(nb)]
            ev = [A.alloc([128, TT], F32, "ev") for _ in range(4)]
            for ti, t0 in enumerate(range(0, L, TT)):
                b = ti % nb
                load(xb[b], xb[b].ap, d["xT"][:, t0:t0 + TT].rearrange("(k p) t -> p k t", p=128))
                P.op("gpsimd", lambda e, a=xq[b].ap, x=xb[b].ap: e.tensor_tensor(out=a, in0=x, in1=x, op=ALU.mult), r=(xb[b],), w=(xq[b],))
                ps = PS()
                for kt in range(8):
                    P.op("tensor", lambda e, o=ps.ap[:, 0:TT], l=C_onesb.ap, r_=xq[b].ap[:, kt, :], k=kt:
                         e.matmul(o, lhsT=l, rhs=r_, start=(k == 0), stop=(k == 7)), r=(C_onesb, xq[b]), w=(ps,))
                P.op("vector", lambda e, a=rs[b].ap, p=ps.ap[:, 0:TT]: e.tensor_scalar(out=a, in0=p, scalar1=1.0 / D, scalar2=EPS, op0=ALU.mult, op1=ALU.add), r=(ps,), w=(rs[b],))
                P.op("scalar", lambda e, a=rs[b].ap: e.activation(out=a, in_=a, func=AF.Ln), r=(rs[b],), w=(rs[b],))
                P.op("scalar", lambda e, a=rs[b].ap: e.activation(out=a, in_=a, func=AF.Exp, scale=-0.5), r=(rs[b],), w=(rs[b],))
                for mi, (m0, mw) in enumerate(mtiles):
                    ps = PS()
                    for kt in range(8):
                        P.op("tensor", lambda e, o=ps.ap[0:mw, 0:TT], l=Wb.ap[:, kt, m0:m0 + mw], r_=xb[b].ap[:, kt, :], k=kt:
                             e.matmul(o, lhsT=l, rhs=r_, start=(k == 0), stop=(k == 7)), r=(Wb, xb[b]), w=(ps,))
                    et = ev[mi % 4]
                    P.op("vector", lambda e, a=et.ap[0:mw, :], p=ps.ap[0:mw, 0:TT], r_=rs[b].ap[0:mw, :]:
                         e.tensor_tensor(out=a, in0=p, in1=r_, op=ALU.mult), r=(ps, rs[b]), w=(et,))
                    store(et, d["U"][m0:m0 + mw, 3 + t0:3 + t0 + TT], et.ap[0:mw, :])
                    if 2056 <= m0 < 2568:
                        pass
        barrier()
        A.off = base0
        QOFF = 512 + 1024 + 8
        KOFF = QOFF + 512
        VOFF = KOFF + 512
        FOFF = VOFF + 512
        DTOFF = 1536
        d0 = S[0]
        CHq = d0["L"] // NQ
        QQ3 = d0["QQ_t"].ap().rearrange("r (j w) -> r j w", j=NQ)
        KQ3 = d0["KQ_t"].ap().rearrange("r (j w) -> r j w", j=NQ + 1)
        VQ3 = d0["VQ_t"].ap().rearrange("r (j w) -> r j w", j=NQ + 1)
        FAQ3 = d0["FAQ_t"].ap().rearrange("r (j w) -> r j w", j=NQ)
        FAK3 = d0["FAK_t"].ap().rearrange("r (j w) -> r j w", j=NQ + 1)
        for j in range(NQ):
            c0 = 3 + j * CHq
            d2d(bar_t, QQ3[:, j, 2:2 + CHq], d0["U"][QOFF:QOFF + 512, c0:c0 + CHq])
            if j > 0:
                d2d(bar_t, QQ3[:, j, 0:2], d0["U"][QOFF:QOFF + 512, c0 - 2:c0])
            d2d(bar_t, KQ3[:, j, :], d0["U"][KOFF:KOFF + 512, c0:c0 + CHq])
            d2d(bar_t, VQ3[:, j, :], d0["U"][VOFF:VOFF + 512, c0:c0 + CHq])
        for k4 in range(4):
            d2d(bar_t, KQ3[k4 * 128:(k4 + 1) * 128, NQ, :], d0["zsrc"])
            d2d(bar_t, VQ3[k4 * 128:(k4 + 1) * 128, NQ, :], d0["zsrc"])
            d2d(bar_t, QQ3[k4 * 128:(k4 + 1) * 128, 0, 0:2], d0["zsrc"][:, 0:2])
        d2d(bar_t, FAK3[:, NQ, :], d0["fpad"])
        XAq3 = d0["XAq_t"].ap().rearrange("r (j w) -> r j w", j=NQ + 1)
        ZSq3 = d0["ZSq_t"].ap().rearrange("r (j w) -> r j w", j=NQ + 1)
        DTq3 = d0["DTq_t"].ap().rearrange("r (j w) -> r j w", j=NQ + 1)
        for j in range(NQ):
            d2d(bar_t, DTq3[:, j, :], d0["U"][DTOFF:DTOFF + NH, 3 + j * CHq:3 + (j + 1) * CHq])
        for k8 in range(8):
            d2d(bar_t, XAq3[k8 * 128:(k8 + 1) * 128, NQ, :], d0["zsrc"])
        for k4 in range(4):
            d2d(bar_t, ZSq3[k4 * 128:(k4 + 1) * 128, NQ, :], d0["zsrc"])
        d2d(bar_t, DTq3[:, NQ, :], d0["dpad"])
        d2d(bar_t, FAQ3[:, 0, 0:2], d0["zsrc"][0:NH * 3, 0:2])
        for d in S:
            L = d["L"]
            d2d(bar_t, d["kT"], d["U"][KOFF:KOFF + 512, 3:3 + L])
            d2d(bar_t, d["vT"], d["U"][VOFF:VOFF + 512, 3:3 + L])
            d2d(bar_t, d["scv"], d["U"][512:1536, L:L + 3])

        for d in (S if STOP >= 2 else []):
            L = d["L"]
            TT = min(512, L)
            if "Yq" in d:
                TT = min(TT, L // NQ)
            NB2 = 4
            ut = [A.alloc([128, TT + 3], F32, "cu") for _ in range(NB2)]
            ca = [A.alloc([128, TT], F32, "ca") for _ in range(NB2)]
            co = [A.alloc([128, TT], BF16, "co") for _ in range(NB2)]
            units = [("c", ct, t0) for ct in range(8) for t0 in range(0, L, TT)] + [("z", zt, t0) for zt in range(4) for t0 in range(0, L, TT)]

            def p2_load(i):
                kind, ct, t0 = units[i]
                u = ut[i % NB2]
                if kind == "c":
                    r0 = 512 + ct * 128
                    load(u, u.ap, d["U"][r0:r0 + 128, t0:t0 + TT + 3])
                else:
                    load(u, u.ap[:, 0:TT], d["U"][ct * 128:(ct + 1) * 128, 3 + t0:3 + t0 + TT])

            PF = 2
            for i in range(min(PF, len(units))):
                p2_load(i)
            for i, (kind, ct, t0) in enumerate(units):
                u, a, o = ut[i % NB2], ca[i % NB2], co[i % NB2]
                if kind == "c":
                    P.op("scalar", lambda e, a_=a.ap, u_=u.ap[:, 3:3 + TT], s=C_cw.ap[:, ct, 3:4], b_=C_cb.ap[:, ct:ct + 1]:
                         e.activation(out=a_, in_=u_, func=AF.Identity, bias=b_, scale=s), r=(u, C_cw, C_cb), w=(a,))
                    for j in range(3):
                        P.op("vector", lambda e, a_=a.ap, u_=u.ap[:, j:j + TT], s=C_cw.ap[:, ct, j:j + 1]:
                             e.scalar_tensor_tensor(out=a_, in0=u_, scalar=s, in1=a_, op0=ALU.mult, op1=ALU.add), r=(u, C_cw, a), w=(a,))
                    P.op("scalar", lambda e, o_=o.ap, a_=a.ap: e.activation(out=o_, in_=a_, func=AF.Silu), r=(a,), w=(o,))
                else:
                    P.op("scalar", lambda e, o_=o.ap, a_=u.ap[:, 0:TT]: e.activation(out=o_, in_=a_, func=AF.Silu), r=(u,), w=(o,))
                if i + PF < len(units):
                    p2_load(i + PF)
                if "Yq" in d:
                    CH2 = L // NQ
                    assert TT <= CH2
                    dstw = (XAq3 if kind == "c" else ZSq3)[ct * 128:(ct + 1) * 128, t0 // CH2, t0 % CH2:t0 % CH2 + TT]
                    store(o, dstw, o.ap)
                elif kind == "c":
                    store(o, d["XA"][ct * 128:(ct + 1) * 128, t0:t0 + TT], o.ap)
                else:
                    store(o, d["ZS"][ct * 128:(ct + 1) * 128, t0:t0 + TT], o.ap)
        barrier()
        A.off = base0
        def phase3(d):
            L = d["L"]
            TT = min(512, L)
            if "Yq" in d:
                TT = min(TT, L // NQ)
            Q = min(128, L)
            NCK = TT // Q
            XX = [A.alloc([128, TT], BF16, "XX") for _ in range(2)]
            BT = [A.alloc([128, TT], BF16, "BT") for _ in range(2)]
            CT = [A.alloc([128, TT], BF16, "CT") for _ in range(2)]
            DR = [A.alloc([128, TT], F32, "DR") for _ in range(2)]
            ZS = [A.alloc([64, TT], BF16, "ZS") for _ in range(2)]
            DTt2 = [A.alloc([128, TT], F32, "DT") for _ in range(2)]
            Ab2 = [A.alloc([128, TT], F32, "Ab") for _ in range(2)]
            Eb2 = [A.alloc([128, TT], F32, "Eb") for _ in range(2)]
            CTs2 = [A.alloc([128, TT], BF16, "CTs") for _ in range(2)]
            XD2 = [A.alloc([128, TT], F32, "XD") for _ in range(2)]
            Wf2 = [A.alloc([128, TT], F32, "Wf") for _ in range(2)]
            XDb2 = [A.alloc([128, TT], BF16, "XDb") for _ in range(2)]
            AA2 = [A.alloc([64, TT], F32, "AA") for _ in range(2)]
            BB2 = [A.alloc([64, TT], F32, "BB") for _ in range(2)]
            XDtok = [A.alloc([128, 128], BF16, "XDtok") for _ in range(3)]
            Btok = [A.alloc([128, 128], BF16, "Btok") for _ in range(3)]
            LT = [A.alloc([128, 128], F32, "LT") for _ in range(3)]
            STt = [A.alloc([128, 128], BF16, "ST") for _ in range(3)]
            yv = [A.alloc([64, 128], F32, "yv") for _ in range(2)]
            YG = [A.alloc([64, TT], BF16, "YG") for _ in range(2)]
            Sf = A.alloc([128, 64], F32, "Sf")
            Sb = A.alloc([128, 64], BF16, "Sb")
            for Wf in Wf2:
                P.op("vector", lambda e, a=Wf.ap[0:64, :]: e.memset(a, 1.0), w=(Wf,))
            it = [0]

            quarter = "Yq" in d
            if quarter:
                CHq_ = L // NQ
                XS = [A.alloc([128, CHq_], BF16, "XS") for _ in range(2)]
                BS = [A.alloc([128, CHq_], BF16, "BS") for _ in range(2)]
                CS = [A.alloc([128, CHq_], BF16, "CS") for _ in range(2)]
                ZQ = [A.alloc([64, CHq_], BF16, "ZQ") for _ in range(2)]
                DS = [A.alloc([128, CHq_], F32, "DS") for _ in range(2)]
                Cs = A.alloc([128, NH * 20], mybir.dt.int32, "sidx")
                load(Cs, Cs.ap, d["sidx"])
                XAqv = d["XAq_t"].ap().rearrange("r (j w) -> (r j) w", j=NQ + 1)
                ZSqv = d["ZSq_t"].ap().rearrange("r (j w) -> (r j) w", j=NQ + 1)
                DTqv = d["DTq_t"].ap().rearrange("r (j w) -> (r j) w", j=NQ + 1)
                sbuf_of = {}
                sctr = [0]

                def gath3(tile, src_v, col, npart):
                    P.op("gpsimd", lambda g_, a=tile.ap[0:npart, :], s_=src_v, ix=Cs.ap[0:npart, col:col + 1]:
                         g_.indirect_dma_start(out=a, out_offset=None, in_=s_, in_offset=bass.IndirectOffsetOnAxis(ap=ix, axis=0)),
                         r=(Cs,), w=(tile,), dma=tile)

                def slot_bufs(h, slot):
                    if (h, slot) not in sbuf_of:
                        sb = sctr[0] % 2
                        sctr[0] += 1
                        cb_ = h * 20 + slot * 5
                        gath3(XS[sb], XAqv, cb_ + 0, 128)
                        gath3(BS[sb], XAqv, cb_ + 1, 128)
                        gath3(CS[sb], XAqv, cb_ + 2, 128)
                        gath3(ZQ[sb], ZSqv, cb_ + 3, 64)
                        gath3(DS[sb], DTqv, cb_ + 4, 128)
                        sbuf_of[(h, slot)] = sb
                    return sbuf_of[(h, slot)]

            def prep(h, t0):
                g = h // 4
                b = it[0] % 2
                it[0] += 1
                DTt, Ab, Eb, CTs, XD, Wf, XDb, AA, BB = DTt2[b], Ab2[b], Eb2[b], CTs2[b], XD2[b], Wf2[b], XDb2[b], AA2[b], BB2[b]
                if quarter:
                    slot, tl = t0 // CHq_, t0 % CHq_
                    sb = slot_bufs(h, slot)
                    xx = TV(XS[sb], XS[sb].ap[:, tl:tl + TT])
                    bt = TV(BS[sb], BS[sb].ap[:, tl:tl + TT])
                    ct_ = TV(CS[sb], CS[sb].ap[:, tl:tl + TT])
                    zs = TV(ZQ[sb], ZQ[sb].ap[:, tl:tl + TT])
                    dr = TV(DS[sb], DS[sb].ap[:, tl:tl + TT])
                else:
                    xx, bt, ct_, dr, zs = XX[b], BT[b], CT[b], DR[b], ZS[b]
                    load(xx, xx.ap[0:64, :], d["XA"][h * 64:(h + 1) * 64, t0:t0 + TT])
                    load(xx, xx.ap[64:128, :], d["XA"][h * 64:(h + 1) * 64, t0:t0 + TT])
                    load(bt, bt.ap, d["XA"][512 + g * 128:512 + (g + 1) * 128, t0:t0 + TT])
                    load(ct_, ct_.ap, d["XA"][768 + g * 128:768 + (g + 1) * 128, t0:t0 + TT])
                    load(dr, dr.ap, d["U"][DTOFF + h:DTOFF + h + 1, 3 + t0:3 + t0 + TT].partition_broadcast(128))
                    load(zs, zs.ap, d["ZS"][h * 64:(h + 1) * 64, t0:t0 + TT])
                P.op("scalar", lambda e, a=DTt.ap, i_=dr.ap, b_=C_dtb.ap[:, h:h + 1]: e.activation(out=a, in_=i_, func=AF.Exp, bias=b_), r=(dr, C_dtb), w=(DTt,))
                P.op("scalar", lambda e, a=DTt.ap: e.activation(out=a, in_=a, func=AF.Ln, bias=1.0), r=(DTt,), w=(DTt,))
                P.op("vector", lambda e, a=Eb.ap, i_=DTt.ap, s=C_aneg.ap[:, h:h + 1]: e.tensor_scalar(out=a, in0=i_, scalar1=s, scalar2=None, op0=ALU.mult), r=(DTt, C_aneg), w=(Eb,))
                moff = 0 if Q == 128 else 1
                P.op("vector", lambda e, a=Ab.ap, m=C_rm.ap[:, moff:moff + TT], x=Eb.ap: e.tensor_tensor_scan(out=a, data0=m, data1=x, initial=0.0, op0=ALU.mult, op1=ALU.add), r=(Eb, C_rm), w=(Ab,))
                P.op("scalar", lambda e, a=Eb.ap, i_=Ab.ap: e.activation(out=a, in_=i_, func=AF.Exp), r=(Ab,), w=(Eb,))
                P.op("gpsimd", lambda e, a=CTs.ap, x=ct_.ap, y=Eb.ap: e.tensor_tensor(out=a, in0=x, in1=y, op=ALU.mult), r=(ct_, Eb), w=(CTs,))
                P.op("vector", lambda e, a=XD.ap, x=xx.ap, y=DTt.ap: e.tensor_tensor(out=a, in0=x, in1=y, op=ALU.mult), r=(xx, DTt), w=(XD,))
                for c in range(NCK):
                    c0 = c * Q
                    P.op("scalar", lambda e, a=Wf.ap[64:128, c0:c0 + Q], i_=Ab.ap[64:128, c0:c0 + Q], b_=Ab.ap[64:128, c0 + Q - 1:c0 + Q]:
                         e.activation(out=a, in_=i_, func=AF.Exp, bias=b_, scale=-1.0), r=(Ab,), w=(Wf,))
                P.op("vector", lambda e, a=XDb.ap, x=XD.ap, y=Wf.ap: e.tensor_tensor(out=a, in0=x, in1=y, op=ALU.mult), r=(XD, Wf), w=(XDb,))
                if any(is_full(t0, c_) for c_ in range(NCK)):
                    P.op("vector", lambda e, a=AA.ap, i_=Ab.ap[0:64, :]: e.tensor_scalar(out=a, in0=i_, scalar1=C_vec.ap[0:64, 1:2], scalar2=C_vec.ap[0:64, 0:1], op0=ALU.mult, op1=ALU.add), r=(Ab, C_vec), w=(AA,))
                    P.op("vector", lambda e, a=BB.ap, i_=Ab.ap[0:64, :]: e.tensor_scalar(out=a, in0=i_, scalar1=C_vec.ap[0:64, 0:1], scalar2=C_vec.ap[0:64, 2:3], op0=ALU.mult, op1=ALU.add), r=(Ab, C_vec), w=(BB,))
                return dict(xx=xx, bt=bt, ct=ct_, zs=zs, Eb=Eb, CTs=CTs, XDb=XDb, AA=AA, BB=BB, yg=YG[b], t0=t0)

            def is_full(t0, c):
                if not quarter:
                    return True
                slot, tl = t0 // CHq_, t0 % CHq_
                return slot == NQ - 1 or (slot == NQ - 2 and tl + TT == CHq_ and c == NCK - 1)

            def stageA(k, tb, c, full=True):
                c0 = c * Q
                xdt_, btk, lt, st = XDtok[k % 3], Btok[k % 3], LT[k % 3], STt[k % 3]
                XDb, bt, ct_, AA, BB = tb["XDb"], tb["bt"], tb["ct"], tb["AA"], tb["BB"]
                pb = PB()
                P.op("tensor", lambda e, o=pb.ap[0:Q, 0:128], i_=XDb.ap[:, c0:c0 + Q]: e.transpose(o, i_, C_identb.ap), r=(XDb, C_identb), w=(pb,))
                P.op("vector", lambda e, a=xdt_.ap[0:Q, :], p=pb.ap[0:Q, 0:128]: e.tensor_copy(out=a, in_=p), r=(pb,), w=(xdt_,))
                pb2 = PB()
                P.op("tensor", lambda e, o=pb2.ap[0:Q, 0:128], i_=bt.ap[:, c0:c0 + Q]: e.transpose(o, i_, C_identb.ap), r=(bt, C_identb), w=(pb2,))
                P.op("scalar", lambda e, a=btk.ap[0:Q, :], p=pb2.ap[0:Q, 0:128]: e.copy(out=a, in_=p), r=(pb2,), w=(btk,))
                if not full:
                    return
                ps = PS()
                P.op("tensor", lambda e, o=ps.ap[0:Q, 0:Q], l=AA.ap[:, c0:c0 + Q], r_=BB.ap[:, c0:c0 + Q]: e.matmul(o, lhsT=l, rhs=r_, start=True, stop=False), r=(AA, BB), w=(ps,))
                P.op("tensor", lambda e, o=ps.ap[0:Q, 0:Q], l=C_ident.ap[0:Q, 0:Q], r_=C_mask.ap[0:Q, 0:Q]: e.matmul(o, lhsT=l, rhs=r_, start=False, stop=True), r=(C_ident, C_mask), w=(ps,))
                P.op("scalar", lambda e, a=lt.ap[0:Q, 0:Q], p=ps.ap[0:Q, 0:Q]: e.activation(out=a, in_=p, func=AF.Exp), r=(ps,), w=(lt,))
                ps2 = PS()
                P.op("tensor", lambda e, o=ps2.ap[0:Q, 0:Q], l=bt.ap[:, c0:c0 + Q], r_=ct_.ap[:, c0:c0 + Q]: e.matmul(o, lhsT=l, rhs=r_, start=True, stop=True), r=(bt, ct_), w=(ps2,))
                P.op("vector", lambda e, a=st.ap[0:Q, 0:Q], p=ps2.ap[0:Q, 0:Q], l=lt.ap[0:Q, 0:Q]: e.tensor_tensor(out=a, in0=p, in1=l, op=ALU.mult), r=(ps2, lt), w=(st,))

            def stageB(k, tb, c, h, full=True):
                c0 = c * Q
                xdt_, btk, st = XDtok[k % 3], Btok[k % 3], STt[k % 3]
                yv_ = yv[k % 2]
                xx, zs, Eb, CTs, yg = tb["xx"], tb["zs"], tb["Eb"], tb["CTs"], tb["yg"]
                if full:
                    ps3 = PACC()
                    P.op("tensor", lambda e, o=ps3.ap[0:64, 0:Q], l=xdt_.ap[0:Q, 0:64], r_=st.ap[0:Q, 0:Q]: e.matmul(o, lhsT=l, rhs=r_, start=True, stop=False), r=(xdt_, st), w=(ps3,))
                    P.op("tensor", lambda e, o=ps3.ap[0:64, 0:Q], l=Sb.ap, r_=CTs.ap[:, c0:c0 + Q]: e.matmul(o, lhsT=l, rhs=r_, start=False, stop=True), r=(Sb, CTs), w=(ps3,))
                    P.op("vector", lambda e, a=yv_.ap[:, 0:Q], x=xx.ap[0:64, c0:c0 + Q], s=C_dsk.ap[0:64, h:h + 1], p=ps3.ap[0:64, 0:Q]:
                         e.scalar_tensor_tensor(out=a, in0=x, scalar=s, in1=p, op0=ALU.mult, op1=ALU.add), r=(xx, C_dsk, ps3), w=(yv_,))
                    P.op("gpsimd", lambda e, a=yg.ap[:, c0:c0 + Q], x=yv_.ap[:, 0:Q], z=zs.ap[:, c0:c0 + Q]: e.tensor_tensor(out=a, in0=x, in1=z, op=ALU.mult), r=(yv_, zs), w=(yg,))
                ps4 = PACC()
                P.op("tensor", lambda e, o=ps4.ap[:, 0:64], l=btk.ap[0:Q, :], r_=xdt_.ap[0:Q, 64:128]: e.matmul(o, lhsT=l, rhs=r_, start=True, stop=True), r=(btk, xdt_), w=(ps4,))
                P.op("vector", lambda e, a=Sf.ap, s=Eb.ap[:, c0 + Q - 1:c0 + Q], p=ps4.ap[:, 0:64]:
                     e.scalar_tensor_tensor(out=a, in0=a, scalar=s, in1=p, op0=ALU.mult, op1=ALU.add), r=(Sf, Eb, ps4), w=(Sf,))
                P.op("scalar", lambda e: e.copy(out=Sb.ap, in_=Sf.ap), r=(Sf,), w=(Sb,))
                if c == NCK - 1 and quarter:
                    slot, tl = tb["t0"] // CHq_, tb["t0"] % CHq_
                    if slot == NQ - 1:
                        store(yg, d["YS"][h * 64:(h + 1) * 64, 2 + tl:2 + tl + TT], yg.ap)
                    elif full:
                        store(yg, d["YS"][h * 64:(h + 1) * 64, 0:2], yg.ap[:, TT - 2:TT])
                elif c == NCK - 1:
                    for (dst_, c0_, c1_) in ydst(d, h * 64, (h + 1) * 64, tb["t0"], TT):
                        store(yg, dst_, yg.ap[:, c0_:c1_])

            chunks = [(h, t0, c) for h in range(NH) for t0 in range(0, L, TT) for c in range(NCK)]
            tiles_ = [(h, t0) for h in range(NH) for t0 in range(0, L, TT)]
            tidx = {t: i for i, t in enumerate(tiles_)}
            tbs = {}
            tbs[tiles_[0]] = prep(*tiles_[0])
            nxt = 1
            doneB = -1
            for i in range(len(chunks) + 1):
                curA = -1
                if i < len(chunks):
                    h, t0, c = chunks[i]
                    curA = tidx[(h, t0)]
                    stageA(i, tbs[(h, t0)], c, is_full(t0, c))
                if i >= 1:
                    h, t0, c = chunks[i - 1]
                    if t0 == 0 and c == 0:
                        load(Sf, Sf.ap, d["st0"][h])
                        P.op("vector", lambda e: e.tensor_copy(out=Sb.ap, in_=Sf.ap), r=(Sf,), w=(Sb,))
                    stageB(i - 1, tbs[(h, t0)], c, h, is_full(t0, c))
                    if c == NCK - 1:
                        doneB = tidx[(h, t0)]
                    if t0 + TT >= L and c == NCK - 1:
                        store(Sf, d["sst"][h], Sf.ap)
                while nxt < len(tiles_) and nxt - 2 <= doneB and nxt <= curA + 1:
                    tbs[tiles_[nxt]] = prep(*tiles_[nxt])
                    nxt += 1
        for d in (S if STOP >= 3 else []):
            phase3(d)
        barrier()
        A.off = base0

        def phase4(si, d):
            L, Lc, TK = d["L"], d["Lc"], d["TK"]
            base = A.off
            PL = min(128, L)
            JL = L // PL
            NKB = (TK + 127) // 128
            JK = NKB
            TKP = NKB * 128
            QT = min(512, L)
            fr = A.alloc([128, JL], F32, "fr")
            lfa = A.alloc([128, JK], F32, "lfa")
            Fc = A.alloc([128, JK], F32, "Fc")
            onesJ = A.alloc([128, JK], F32, "onesJ")
            hb = A.alloc([128, JK], BF16, "hb"); hf = A.alloc([128, JK], F32, "hf")
            r1 = A.alloc([128, JK], F32, "r1"); mb = A.alloc([128, JK], BF16, "mb"); mf = A.alloc([128, JK], F32, "mf")
            r2 = A.alloc([128, JK], F32, "r2"); lb = A.alloc([128, JK], BF16, "lb")
            sc6 = [A.alloc([128, JK], BF16, "sc6") for _ in range(6)]
            quarter = "Yq" in d
            CHq = L // NQ
            QW = (2 + CHq) if quarter else L
            Qa = A.alloc([70, QW], BF16, "Qa")
            Ka = A.alloc([70, TKP], BF16, "Ka")
            if quarter:
                G6 = A.alloc([3, QW], BF16, "G6")
                GK = A.alloc([3, TKP], BF16, "GK")
                Cg = A.alloc([128, NH * 10], mybir.dt.int32, "gidx")
                load(Cg, Cg.ap, d["gidx"])
                QQv = d["QQ_t"].ap().rearrange("r (j w) -> (r j) w", j=NQ)
                KQv = d["KQ_t"].ap().rearrange("r (j w) -> (r j) w", j=NQ + 1)
                VQv = d["VQ_t"].ap().rearrange("r (j w) -> (r j) w", j=NQ + 1)
                FAQv = d["FAQ_t"].ap().rearrange("r (j w) -> (r j) w", j=NQ)
                FAKv = d["FAK_t"].ap().rearrange("r (j w) -> (r j) w", j=NQ + 1)
                FAQ3 = d["FAQ_t"].ap().rearrange("r (j w) -> r j w", j=NQ)
                FAKf = d["FAK_t"].ap()

                def gath(tile, out_ap, src_v, col, npart, extra_r=()):
                    P.op("gpsimd", lambda g, a=out_ap, s_=src_v, ix=Cg.ap[0:npart, col:col + 1]:
                         g.indirect_dma_start(out=a, out_offset=None, in_=s_, in_offset=bass.IndirectOffsetOnAxis(ap=ix, axis=0)),
                         r=(Cg,) + tuple(extra_r), w=(tile,), dma=tile)
            Va = A.alloc([128, NKB, 65], BF16, "Va")
            vT = A.alloc([64, L], BF16, "vT")
            PT = [A.alloc([128, QT], BF16, "PT") for _ in range(3)]
            Lrow = A.alloc([65, QT], F32, "Lrow")
            rcp = A.alloc([64, QT], F32, "rcp")
            Yo = [A.alloc([64, QT], BF16, "Yo") for _ in range(2)]
            lfd = T(d["LFA"], "LFA%d" % si)
            fad = T(d["FA"], "FA%d" % si)
            P.op("vector", lambda e: e.memset(onesJ.ap, 1.0), w=(onesJ,))
            for h in range(NH):
                load(fr, fr.ap[0:PL, :], d["U"][FOFF + h, 3:3 + L].rearrange("(p j) -> p j", p=PL))
                P.op("scalar", lambda e, a=fr.ap[0:PL, :], b_=C_nfb.ap[0:PL, h:h + 1]: e.activation(out=a, in_=a, func=AF.Exp, bias=b_, scale=-1.0), r=(fr, C_nfb), w=(fr,))
                P.op("scalar", lambda e, a=fr.ap[0:PL, :]: e.activation(out=a, in_=a, func=AF.Ln, bias=1.0), r=(fr,), w=(fr,))
                P.op("vector", lambda e, a=fr.ap[0:PL, :]: e.tensor_scalar(out=a, in0=a, scalar1=-1.0, scalar2=None, op0=ALU.mult), r=(fr,), w=(fr,))
                store(fr, d["lf"][h, :].rearrange("(p j) -> p j", p=PL), fr.ap[0:PL, :])
                store(fr, d["LFA"][h, Lc:Lc + L].rearrange("(p j) -> p j", p=PL), fr.ap[0:PL, :], extra_w=(lfd,))
                if Lc:
                    d2d(lfd, d["LFA"][h, 0:Lc], d["cLF"][h, :], w=(lfd,))
                P.op("vector", lambda e: e.memset(lfa.ap, 0.0), w=(lfa,))
                full = TK // JK
                rem = TK - full * JK
                load(lfa, lfa.ap[0:full, :], d["LFA"][h, 0:full * JK].rearrange("(p j) -> p j", j=JK), extra_r=(lfd,))
                if rem:
                    load(lfa, lfa.ap[full:full + 1, 0:rem], d["LFA"][h:h + 1, full * JK:TK], extra_r=(lfd,))
                P.op("vector", lambda e: e.tensor_tensor_scan(out=Fc.ap, data0=onesJ.ap, data1=lfa.ap, initial=0.0, op0=ALU.mult, op1=ALU.add), r=(lfa, onesJ), w=(Fc,))
                ps = PS()
                P.op("tensor", lambda e, o=ps.ap[:, 0:2], r_=Fc.ap[:, JK - 1:JK]: e.matmul(o[:, 0:1], lhsT=C_low.ap, rhs=r_, start=True, stop=True), r=(C_low, Fc), w=(ps,))
                P.op("vector", lambda e, p=ps.ap[:, 0:1]: e.tensor_scalar(out=Fc.ap, in0=Fc.ap, scalar1=p, scalar2=None, op0=ALU.add), r=(ps, Fc), w=(Fc,))
                P.op("vector", lambda e: e.tensor_copy(out=hb.ap, in_=Fc.ap), r=(Fc,), w=(hb,))
                P.op("vector", lambda e: e.tensor_copy(out=hf.ap, in_=hb.ap), r=(hb,), w=(hf,))
                P.op("vector", lambda e: e.tensor_tensor(out=r1.ap, in0=Fc.ap, in1=hf.ap, op=ALU.subtract), r=(Fc, hf), w=(r1,))
                P.op("vector", lambda e: e.tensor_copy(out=mb.ap, in_=r1.ap), r=(r1,), w=(mb,))
                P.op("vector", lambda e: e.tensor_copy(out=mf.ap, in_=mb.ap), r=(mb,), w=(mf,))
                P.op("vector", lambda e: e.tensor_tensor(out=r2.ap, in0=r1.ap, in1=mf.ap, op=ALU.subtract), r=(r1, mf), w=(r2,))
                P.op("vector", lambda e: e.tensor_copy(out=lb.ap, in_=r2.ap), r=(r2,), w=(lb,))
                for i6, (src, sgn) in enumerate(((hb, 8.0), (mb, 8.0), (lb, 8.0), (hb, -8.0), (mb, -8.0), (lb, -8.0))):
                    P.op("vector", lambda e, a=sc6[i6].ap, s_=src.ap, v=sgn: e.tensor_scalar(out=a, in0=s_, scalar1=v, scalar2=None, op0=ALU.mult), r=(src,), w=(sc6[i6],))
                    if quarter:
                        PQ = 128 // NQ
                        if i6 < 3:
                            for jq in range(NQ):
                                store(sc6[i6], FAQ3[h * 3 + i6, jq, 2:2 + CHq].rearrange("(p j) -> p j", j=JK), sc6[i6].ap[jq * PQ:(jq + 1) * PQ, :], extra_w=(fad,))
                                if jq > 0:
                                    store(sc6[i6], FAQ3[h * 3 + i6:h * 3 + i6 + 1, jq, 0:2], sc6[i6].ap[jq * PQ - 1:jq * PQ, JK - 2:JK], extra_w=(fad,))
                        else:
                            store(sc6[i6], FAKf[h * 3 + i6 - 3, 0:TK].rearrange("(p j) -> p j", j=JK), sc6[i6].ap[0:full, :], extra_w=(fad,))
                        continue
                    store(sc6[i6], d["FA"][h, i6, 0:full * JK].rearrange("(p j) -> p j", j=JK), sc6[i6].ap[0:full, :], extra_w=(fad,))
                    if rem:
                        store(sc6[i6], d["FA"][h, i6:i6 + 1, full * JK:TK], sc6[i6].ap[full:full + 1, 0:rem], extra_w=(fad,))
                P.op("vector", lambda e: e.memset(Qa.ap[64:70, :], 1.0), w=(Qa,))
                P.op("vector", lambda e: e.memset(Ka.ap[64:70, :], 1.0), w=(Ka,))
                if quarter:
                    gb = h * 10
                    gath(Qa, Qa.ap[0:64, :], QQv, gb + 0, 64)
                    gath(G6, G6.ap[0:3, :], FAQv, gb + 5, 3, extra_r=(fad,))
                    P.op("sync", lambda g: g.dma_start(out=Qa.ap[64:67, :], in_=G6.ap[0:3, :]), r=(G6,), w=(Qa,), dma=Qa)
                    for sq_ in range(NQ):
                        gath(Ka, Ka.ap[0:64, sq_ * CHq:(sq_ + 1) * CHq], KQv, gb + 1 + sq_, 64)
                        gath(vT, vT.ap[0:64, sq_ * CHq:(sq_ + 1) * CHq], VQv, gb + 1 + sq_, 64)
                        gath(GK, GK.ap[0:3, sq_ * CHq:(sq_ + 1) * CHq], FAKv, gb + 6 + sq_, 3, extra_r=(fad,))
                    P.op("sync", lambda g: g.dma_start(out=Ka.ap[67:70, 0:TK], in_=GK.ap[0:3, 0:TK]), r=(GK,), w=(Ka,), dma=Ka)
                    P.op("vector", lambda e: e.memset(Va.ap[:, :, 64:65], 1.0), w=(Va,))
                else:
                    load(Qa, Qa.ap[0:64, :], d["U"][QOFF + h * 64:QOFF + (h + 1) * 64, 3:3 + L])
                    load(Qa, Qa.ap[64:67, :], d["FA"][h, 0:3, Lc:Lc + L], extra_r=(fad,))
                    if Lc:
                        load(Ka, Ka.ap[0:64, 0:Lc], d["cKT"][h])
                    load(Ka, Ka.ap[0:64, Lc:TK], d["U"][KOFF + h * 64:KOFF + (h + 1) * 64, 3:3 + L])
                    load(Ka, Ka.ap[67:70, 0:TK], d["FA"][h, 3:6, 0:TK], extra_r=(fad,))
                    P.op("vector", lambda e: e.memset(Va.ap[:, :, 64:65], 1.0), w=(Va,))
                    if Lc:
                        load(Va, Va.ap[:, 0:Lc // 128, 0:64], d["cV"][h].rearrange("(b p) d -> p b d", p=128))
                    load(vT, vT.ap, d["U"][VOFF + h * 64:VOFF + (h + 1) * 64, 3:3 + L])
                for kb in range(Lc // 128, NKB):
                    k0 = kb * 128 - Lc
                    kw = min(128, L - k0)
                    pb = PB()
                    P.op("tensor", lambda e, o=pb.ap[0:kw, 0:64], i_=vT.ap[:, k0:k0 + kw]: e.transpose(o, i_, C_identb.ap[0:64, 0:64]), r=(vT, C_identb), w=(pb,))
                    P.op("vector", lambda e, a=Va.ap[0:kw, kb, 0:64], p=pb.ap[0:kw, 0:64]: e.tensor_copy(out=a, in_=p), r=(pb,), w=(Va,))
                tasks = []
                if quarter:
                    QTq = min(512, CHq)
                    qtiles = [(0, 2, 3 * CHq - 2)] + [(2 + q_, QTq, 3 * CHq + q_) for q_ in range(0, CHq, QTq)]
                else:
                    qtiles = [(q_, QT, Lc + q_) for q_ in range(0, L, QT)]
                for qi, (q0, wq, qa0) in enumerate(qtiles):
                    last_kb = (qa0 + wq - 1) // 128
                    for kb in range(0, last_kb + 1):
                        kpos = kb * 128
                        kw = min(128, TK - kpos)
                        o_ = max(0, kpos - qa0)
                        tasks.append(dict(qi=qi, q0=q0, w=wq, kb=kb, kpos=kpos, kw=kw, o=o_, nq=wq - o_, moff=qa0 + o_ - kpos,
                                          diag=(kpos + kw - 1 > qa0 + o_), first=(kb == 0), last=(kb == last_kb)))
                LA = 2
                pos = {}
                for i in range(len(tasks) + LA):
                    if i < len(tasks):
                        t = tasks[i]
                        ps = PS()
                        t["ps"] = ps
                        kw, o_, kpos, q0, wq, mo = t["kw"], t["o"], t["kpos"], t["q0"], t["w"], t["moff"]
                        P.op("tensor", lambda e, o=ps.ap[0:kw, o_:wq], l=Ka.ap[:, kpos:kpos + kw], r_=Qa.ap[:, q0 + o_:q0 + wq], dg=t["diag"]:
                             e.matmul(o, lhsT=l, rhs=r_, start=True, stop=(not dg)), r=(Ka, Qa), w=(ps,))
                        if t["diag"]:
                            mw_ = min(kw - mo, t["nq"])
                            P.op("tensor", lambda e, o=ps.ap[0:kw, o_:o_ + mw_], l=C_identb.ap[0:kw, 0:kw], r_=C_maskb.ap[0:kw, mo:mo + mw_]:
                                 e.matmul(o, lhsT=l, rhs=r_, start=False, stop=True), r=(C_identb, C_maskb), w=(ps,))
                    j = i - LA
                    if j >= 0:
                        t = tasks[j]
                        ps = t["ps"]
                        kw, o_, kb, q0, qi, wq = t["kw"], t["o"], t["kb"], t["q0"], t["qi"], t["w"]
                        if t["first"]:
                            pos[qi] = PACC()
                        po = pos[qi]
                        pt = PT[j % 3]
                        P.op("scalar", lambda e, a=pt.ap[0:kw, o_:wq], p=ps.ap[0:kw, o_:wq]: e.activation(out=a, in_=p, func=AF.Exp, scale=0.125), r=(ps,), w=(pt,))
                        P.op("tensor", lambda e, o=po.ap[0:65, o_:wq], l=Va.ap[0:kw, kb, :], r_=pt.ap[0:kw, o_:wq], f=t["first"], la=t["last"]:
                             e.matmul(o, lhsT=l, rhs=r_, start=f, stop=la), r=(Va, pt), w=(po,))
                        if t["last"]:
                            P.op("scalar", lambda e, a=Lrow.ap[64:65, 0:wq], p=po.ap[64:65, 0:wq]: e.copy(out=a, in_=p), r=(po,), w=(Lrow,))
                            pl = PS()
                            P.op("tensor", lambda e, o=pl.ap[0:64, 0:wq], l=C_onesf.ap[64:65, 0:64], r_=Lrow.ap[64:65, 0:wq]: e.matmul(o, lhsT=l, rhs=r_, start=True, stop=True), r=(C_onesf, Lrow), w=(pl,))
                            P.op("vector", lambda e, a=rcp.ap[:, 0:wq], p=pl.ap[0:64, 0:wq]: e.tensor_scalar(out=a, in0=p, scalar1=1e-30, scalar2=None, op0=ALU.max), r=(pl,), w=(rcp,))
                            P.op("vector", lambda e, a=rcp.ap[:, 0:wq]: e.reciprocal(out=a, in_=a), r=(rcp,), w=(rcp,))
                            yo = Yo[qi % 2]
                            P.op("vector", lambda e, a=yo.ap[:, 0:wq], p=po.ap[0:64, 0:wq], r_=rcp.ap[:, 0:wq]: e.tensor_tensor(out=a, in0=p, in1=r_, op=ALU.mult), r=(po, rcp), w=(yo,))
                            if quarter:
                                store(yo, d["YF"][h * 64:(h + 1) * 64, q0:q0 + wq], yo.ap[:, 0:wq])
                            else:
                                for (dst_, c0_, c1_) in ydst(d, 512 + h * 64, 512 + (h + 1) * 64, q0, wq):
                                    store(yo, dst_, yo.ap[:, c0_:c1_])
        for si, d in (enumerate(S) if STOP >= 4 else []):
            phase4(si, d)
        barrier()
        A.off = base0

        base = A.off
        Wu = [A.alloc([128, 8, 128], BF16, "Wu") for _ in range(4)]
        Wo = [A.alloc([128, 8, 128], BF16, "Wo") for _ in range(2)]
        Wd = [A.alloc([128, 22, 128], BF16, "Wd") for _ in range(2)]
        NT = 512
        Yt2 = [A.alloc([128, 8, NT], F32, "Yt")] * 2
        Yn2 = [A.alloc([128, 8, NT], BF16, "Yn") for _ in range(2)]
        xt2 = [A.alloc([128, 8, NT], F32, "xt") for _ in range(2)]
        n22 = [A.alloc([128, 8, NT], BF16, "n2") for _ in range(2)]
        rs2 = [A.alloc([128, NT], F32, "rs5") for _ in range(2)]
        tctr = [0]
        mt = A.alloc([128, 22, NT], BF16, "mt")
        ug = [A.alloc([128, NT], F32, "ug") for _ in range(2)]
        uv = [A.alloc([128, NT], F32, "uv") for _ in range(2)]
        cg = [A.alloc([128, NT], F32, "cg")] * 2
        cv = [A.alloc([128, NT], F32, "cv")] * 2
        sg = [A.alloc([128, NT], F32, "sg")] * 2
        fpv = A.alloc([128, 44, 2], F32, "fpv")
        UL = A.alloc([128, 44, 2], F32, "UL")
        wi = [0, 0, 0]

        def rstd_from(ps_ap, out_ap, n, scale, rs):
            P.op("vector", lambda e: e.tensor_scalar(out=out_ap, in0=ps_ap, scalar1=scale, scalar2=EPS, op0=ALU.mult, op1=ALU.add), r=(cur_ps[0],), w=(rs,))
            P.op("scalar", lambda e: e.activation(out=out_ap, in_=out_ap, func=AF.Ln), r=(rs,), w=(rs,))
            P.op("scalar", lambda e: e.activation(out=out_ap, in_=out_ap, func=AF.Exp, scale=-0.5), r=(rs,), w=(rs,))

        cur_ps = [None]
        Yw = None
        for d in (S if STOP >= 5 else []):
            quarter = "Yq" in d
            L = d["L"] // NQ if quarter else d["L"]
            load(fpv, fpv.ap, d["fprev"])
            if quarter:
                Yw = A.alloc([128, 8, 2 + L], BF16, "Yw")
                for kt in range(4):
                    load(Yw, Yw.ap[:, kt, :], d["YS"][kt * 128:(kt + 1) * 128, :])
                for kt in range(4, 8):
                    load(Yw, Yw.ap[:, kt, :], d["YF"][(kt - 4) * 128:(kt - 3) * 128, :])
            step = NT - 2
            tiles = []
            t = 0
            while t < L:
                n = min(step, L - t)
                tiles.append((t, n))
                t += n
            def tile_body(t0, n, Yt, Yn, xt, n2, rs):
                sq = n2
                yo5 = Yt
                W = n + 2
                if quarter:
                    hl = 2
                    c_lo = 0
                    ysrc = lambda kt: Yw.ap[:, kt, t0:t0 + W]
                    ytile = Yw
                    load(xt, xt.ap[:, :, 0:W], d["xw"][:, t0:t0 + W].rearrange("(k p) t -> p k t", p=128))
                else:
                    hl = 2 if t0 > 0 else 0
                    c_lo = 2 - hl
                    if hl == 0:
                        P.op("vector", lambda e: e.memset(Yt.ap[:, :, 0:2], 0.0), w=(Yt,))
                        P.op("vector", lambda e: e.memset(xt.ap[:, :, 0:2], 0.0), w=(xt,))
                    load(Yt, Yt.ap[:, :, c_lo:W], d["Y"][:, t0 - hl:t0 + n].rearrange("(k p) t -> p k t", p=128))
                    load(xt, xt.ap[:, :, c_lo:W], d["xT"][:, t0 - hl:t0 + n].rearrange("(k p) t -> p k t", p=128))
                    ysrc = lambda kt: Yt.ap[:, kt, 0:W]
                    ytile = Yt
                for kt in range(4):
                    P.op("vector", lambda e, a=sq.ap[:, kt, 0:W], x=ysrc(kt): e.tensor_tensor(out=a, in0=x, in1=x, op=ALU.mult), r=(ytile,), w=(sq,))
                for g in range(2):
                    ps = PS()
                    cur_ps[0] = ps
                    for k in range(2):
                        P.op("tensor", lambda e, o=ps.ap[:, 0:W], r_=sq.ap[:, 2 * g + k, 0:W], k_=k: e.matmul(o, lhsT=C_onesb.ap, rhs=r_, start=(k_ == 0), stop=(k_ == 1)), r=(C_onesb, sq), w=(ps,))
                    rstd_from(ps.ap[:, 0:W], rs.ap[:, 0:W], W, 1.0 / 256, rs)
                    for k in range(2):
                        kt = 2 * g + k
                        P.op("vector", lambda e, a=Yn.ap[:, kt, 0:W], x=ysrc(kt), s=C_gS.ap[:, kt:kt + 1], r_=rs.ap[:, 0:W]:
                             e.scalar_tensor_tensor(out=a, in0=x, scalar=s, in1=r_, op0=ALU.mult, op1=ALU.mult), r=(ytile, C_gS, rs), w=(Yn,))
                for kt in range(4, 8):
                    P.op("scalar", lambda e, a=Yn.ap[:, kt, 0:W], x=ysrc(kt): e.copy(out=a, in_=x), r=(ytile,), w=(Yn,))
                for m in range(8):
                    wo = Wo[wi[0] % 2]
                    wi[0] += 1
                    load(wo, wo.ap, wob[m])
                    ps = PS()
                    for kt in range(8):
                        P.op("tensor", lambda e, o=ps.ap[:, 0:W], l=wo.ap[:, kt, :], r_=Yn.ap[:, kt, 0:W], k_=kt: e.matmul(o, lhsT=l, rhs=r_, start=(k_ == 0), stop=(k_ == 7)), r=(wo, Yn), w=(ps,))
                    P.op("vector", lambda e, a=xt.ap[:, m, 0:W], p=ps.ap[:, 0:W]: e.tensor_tensor(out=a, in0=a, in1=p, op=ALU.add), r=(xt, ps), w=(xt,))
                for kt in range(8):
                    P.op("vector", lambda e, a=sq.ap[:, kt, 0:W], x=xt.ap[:, kt, 0:W]: e.tensor_tensor(out=a, in0=x, in1=x, op=ALU.mult), r=(xt,), w=(sq,))
                ps = PS()
                cur_ps[0] = ps
                for kt in range(8):
                    P.op("tensor", lambda e, o=ps.ap[:, 0:W], r_=sq.ap[:, kt, 0:W], k_=kt: e.matmul(o, lhsT=C_onesb.ap, rhs=r_, start=(k_ == 0), stop=(k_ == 7)), r=(C_onesb, sq), w=(ps,))
                rstd_from(ps.ap[:, 0:W], rs.ap[:, 0:W], W, 1.0 / D, rs)
                for kt in range(8):
                    P.op("vector", lambda e, a=n2.ap[:, kt, 0:W], x=xt.ap[:, kt, 0:W], s=C_g2.ap[:, kt:kt + 1], r_=rs.ap[:, 0:W]:
                         e.scalar_tensor_tensor(out=a, in0=x, scalar=s, in1=r_, op0=ALU.mult, op1=ALU.mult), r=(xt, C_g2, rs), w=(n2,))
                for j in range(22):
                    bsel = j % 2
                    res = []
                    for (mi, ub, cbuf) in ((j, ug[bsel], cg[bsel]), (j + 22, uv[bsel], cv[bsel])):
                        wu = Wu[wi[2] % 4]
                        wi[2] += 1
                        load(wu, wu.ap, wub[mi])
                        ps = PS()
                        for kt in range(8):
                            P.op("tensor", lambda e, o=ps.ap[:, 0:W], l=wu.ap[:, kt, :], r_=n2.ap[:, kt, 0:W], k_=kt:
                                 e.matmul(o, lhsT=l, rhs=r_, start=(k_ == 0), stop=(k_ == 7)), r=(wu, n2), w=(ps,))
                        P.op("scalar", lambda e, a=ub.ap[:, 0:W], p=ps.ap[:, 0:W]: e.copy(out=a, in_=p), r=(ps,), w=(ub,))
                        if hl == 0:
                            P.op("gpsimd", lambda e, a=ub.ap[:, 0:2], s_=fpv.ap[:, mi, :]: e.tensor_copy(out=a, in_=s_), r=(fpv,), w=(ub,))
                        if t0 + n == L:
                            P.op("gpsimd", lambda e, a=UL.ap[:, mi, :], s_=ub.ap[:, W - 2:W]: e.tensor_copy(out=a, in_=s_), r=(ub,), w=(UL,))
                        P.op("scalar", lambda e, a=cbuf.ap[:, 0:n], u_=ub.ap[:, 2:W], s=C_fw.ap[:, mi, 2:3], b_=C_fb.ap[:, mi:mi + 1]:
                             e.activation(out=a, in_=u_, func=AF.Identity, bias=b_, scale=s), r=(ub, C_fw, C_fb), w=(cbuf,))
                        P.op("vector", lambda e, a=cbuf.ap[:, 0:n], u_=ub.ap[:, 1:W - 1], s=C_fw.ap[:, mi, 1:2]:
                             e.scalar_tensor_tensor(out=a, in0=u_, scalar=s, in1=a, op0=ALU.mult, op1=ALU.add), r=(ub, C_fw, cbuf), w=(cbuf,))
                        P.op("vector", lambda e, a=cbuf.ap[:, 0:n], u_=ub.ap[:, 0:W - 2], s=C_fw.ap[:, mi, 0:1]:
                             e.scalar_tensor_tensor(out=a, in0=u_, scalar=s, in1=a, op0=ALU.mult, op1=ALU.add), r=(ub, C_fw, cbuf), w=(cbuf,))
                    sgb = sg[bsel]
                    P.op("scalar", lambda e, a=sgb.ap[:, 0:n], i_=cg[bsel].ap[:, 0:n]: e.activation(out=a, in_=i_, func=AF.Silu), r=(cg[bsel],), w=(sgb,))
                    P.op("gpsimd", lambda e, a=mt.ap[:, j, 0:n], x=sgb.ap[:, 0:n], y=cv[bsel].ap[:, 0:n]: e.tensor_tensor(out=a, in0=x, in1=y, op=ALU.mult), r=(sgb, cv[bsel]), w=(mt,))
                for m in range(8):
                    wd = Wd[wi[1] % 2]
                    wi[1] += 1
                    load(wd, wd.ap, wdb[m])
                    ps = PS()
                    for kt in range(22):
                        P.op("tensor", lambda e, o=ps.ap[:, 0:n], l=wd.ap[:, kt, :], r_=mt.ap[:, kt, 0:n], k_=kt: e.matmul(o, lhsT=l, rhs=r_, start=(k_ == 0), stop=(k_ == 21)), r=(wd, mt), w=(ps,))
                    P.op("vector", lambda e, a=xt.ap[:, m, 2:W], p=ps.ap[:, 0:n]: e.tensor_tensor(out=a, in0=a, in1=p, op=ALU.add), r=(xt, ps), w=(xt,))
                for kt in range(8):
                    P.op("vector", lambda e, a=sq.ap[:, kt, 0:n], x=xt.ap[:, kt, 2:W]: e.tensor_tensor(out=a, in0=x, in1=x, op=ALU.mult), r=(xt,), w=(sq,))
                ps = PS()
                cur_ps[0] = ps
                for kt in range(8):
                    P.op("tensor", lambda e, o=ps.ap[:, 0:n], r_=sq.ap[:, kt, 0:n], k_=kt: e.matmul(o, lhsT=C_onesb.ap, rhs=r_, start=(k_ == 0), stop=(k_ == 7)), r=(C_onesb, sq), w=(ps,))
                rstd_from(ps.ap[:, 0:n], rs.ap[:, 0:n], n, 1.0 / D, rs)
                for kt in range(8):
                    P.op("vector", lambda e, a=yo5.ap[:, kt, 0:n], x=xt.ap[:, kt, 2:W], s=C_gF.ap[:, kt:kt + 1], r_=rs.ap[:, 0:n]:
                         e.scalar_tensor_tensor(out=a, in0=x, scalar=s, in1=r_, op0=ALU.mult, op1=ALU.mult), r=(xt, C_gF, rs), w=(yo5,))
                store(yo5, d["yT"][:, t0:t0 + n].rearrange("(k p) t -> p k t", p=128), yo5.ap[:, :, 0:n])
            for ti_, (t0, n) in enumerate(tiles):
                par = tctr[0] % 2
                tctr[0] += 1
                tile_body(t0, n, Yt2[par], Yn2[par], xt2[par], n22[par], rs2[par])
            store(UL, d["fcv"], UL.ap)
        A.off = base
        barrier()

        if os.environ.get("KNOSCHED") is None:
            P.reschedule()
        P.finalize()
        with nc.Block() as block:
            @block.sync
            def _(e):
                P.emit("sync", e)

            @block.scalar
            def _(e):
                P.emit("scalar", e)

            @block.vector
            def _(e):
                P.emit("vector", e)

            @block.gpsimd
            def _(e):
                P.emit("gpsimd", e)

            @block.tensor
            def _(e):
                P.emit("tensor", e)
    return nc, seqs


_CACHE = {}


def _consts():
    cst = np.zeros((128, 6, 128), np.float32)
    cst[:, 0, :] = np.eye(128, dtype=np.float32)
    s = np.arange(128)[:, None]
    l = np.arange(128)[None, :]
    cst[:, 1, :] = np.where(l >= s, 0.0, NEG)
    cst[:, 2, :] = (s < l).astype(np.float32)
    cst[:, 3, :] = 1.0
    cvec = np.zeros((128, 16), np.float32)
    cvec[0, 0] = 1.0
    cvec[1, 1] = 1.0
    cvec[1, 2] = -1.0
    cvec[:, 3] = EPS
    cvec[:, 4] = 1.0
    rmask = np.ones((128, 512), np.float32)
    rmask[:, ::128] = 0.0
    return cst, cvec, rmask


def _pk(v, n):
    return np.ascontiguousarray(np.asarray(v, np.float32).reshape(n, 128).T)


def kernel(x_prompt, x_sample, cache_fox_k, cache_fox_v, cache_fox_logf, state_ssd, state_ssd_conv,
           state_ffn_conv, norm1_g, w_in, ssd_conv_w, ssd_conv_b, ssd_dt_bias, ssd_a_log, ssd_d,
           ssd_norm_g, fox_f_bias, w_out, norm2_g, w_up, ffn_conv_w, ffn_conv_b, w_down, final_norm_g):
    f = lambda a: np.ascontiguousarray(np.asarray(a, dtype=np.float32))
    x_prompt = f(x_prompt); x_sample = f(x_sample)
    B, LP, _ = x_prompt.shape
    NSB, LS, _ = x_sample.shape
    LC = cache_fox_k.shape[2]
    ncore = 8
    nsamp = NSB // ncore
    key = (LP, nsamp, LS, LC)
    if key not in _CACHE:
        _CACHE[key] = build(LP, nsamp, LS, LC)
    nc, seqs = _CACHE[key]
    cst, cvec, rmask = _consts()
    rep = lambda v: np.ascontiguousarray(np.broadcast_to(np.asarray(v, np.float32)[None, :], (128, len(v))))
    common = {
        "w_in": f(w_in[0]), "w_out": f(w_out[0]), "w_up": f(w_up[0]), "w_down": f(w_down[0]),
        "g1": _pk(norm1_g[0], 8), "g2": _pk(norm2_g[0], 8), "gF": _pk(final_norm_g, 8), "gS": _pk(ssd_norm_g[0], 4),
        "cw": np.ascontiguousarray(f(ssd_conv_w[0]).T.reshape(8, 128, 4).transpose(1, 0, 2)),
        "cb": _pk(ssd_conv_b[0], 8),
        "fw": np.ascontiguousarray(f(ffn_conv_w[0]).T.reshape(44, 128, 3).transpose(1, 0, 2)),
        "fb": _pk(ffn_conv_b[0], 44),
        "dtb": rep(ssd_dt_bias[0]), "alog": rep(ssd_a_log[0]), "dsk": rep(ssd_d[0]), "fbias": rep(fox_f_bias[0]),
        "cst": cst, "cvec": cvec, "rmask": rmask, "bar_src": np.zeros((1, 16), np.float32),
    }
    cfk = f(cache_fox_k[0]); cfv = f(cache_fox_v[0]); cfl = f(cache_fox_logf[0])
    sst = f(state_ssd[0]); scv = f(state_ssd_conv[0]); sfc = f(state_ffn_conv[0])
    in_maps = []
    for c in range(ncore):
        m = dict(common)
        b = c % B
        m["xT_0"] = np.ascontiguousarray(x_prompt[b].T)
        rq = c // B
        CHq = LP // 4
        xwm = np.zeros((D, 2 + CHq), np.float32)
        if rq > 0:
            xwm[:, 0:2] = x_prompt[b][rq * CHq - 2:rq * CHq].T
        xwm[:, 2:] = x_prompt[b][rq * CHq:(rq + 1) * CHq].T
        m["xw"] = xwm
        pp = np.arange(128)
        m["qidx"] = np.stack([((kt * 128 + pp) * 4 + rq) for kt in range(8)], axis=1).astype(np.int32)
        m["zsrc"] = np.zeros((128, CHq), np.float32)
        fp_ = np.zeros((NH * 3, CHq), np.float32)
        fp_[0::3, :] = -240000.0
        m["fpad"] = fp_
        gi = np.zeros((128, NH * 10), np.int32)
        p64 = np.minimum(pp, 63)
        p3 = np.minimum(pp, 2)
        for h_ in range(NH):
            gi[:, h_ * 10 + 0] = (h_ * 64 + p64) * 4 + rq
            gi[:, h_ * 10 + 5] = (h_ * 3 + p3) * 4 + rq
            for sq_ in range(4):
                slot = sq_ - (3 - rq)
                if slot < 0:
                    slot = 4
                gi[:, h_ * 10 + 1 + sq_] = (h_ * 64 + p64) * 5 + slot
                gi[:, h_ * 10 + 6 + sq_] = (h_ * 3 + p3) * 5 + slot
        m["gidx"] = gi
        dp_ = np.full((NH, CHq), -100.0, np.float32)
        m["dpad"] = dp_
        si_ = np.zeros((128, NH * 20), np.int32)
        for h_ in range(NH):
            g_ = h_ // 4
            for sq_ in range(4):
                slot = sq_ - (3 - rq)
                if slot < 0:
                    slot = 4
                cb_ = h_ * 20 + sq_ * 5
                si_[:, cb_ + 0] = (h_ * 64 + (pp % 64)) * 5 + slot
                si_[:, cb_ + 1] = (512 + g_ * 128 + pp) * 5 + slot
                si_[:, cb_ + 2] = (768 + g_ * 128 + pp) * 5 + slot
                si_[:, cb_ + 3] = (h_ * 64 + p64) * 5 + slot
                si_[:, cb_ + 4] = h_ * 5 + slot
        m["sidx"] = si_
        m["st0_0"] = np.zeros((NH, 128, 64), np.float32)
        m["cprev_0"] = np.zeros((D, 3), np.float32)
        m["fprev_0"] = np.zeros((128, 44, 2), np.float32)
        for i in range(nsamp):
            s = c * nsamp + i
            n = "_%d" % (i + 1)
            m["xT" + n] = np.ascontiguousarray(x_sample[s].T)
            m["st0" + n] = np.ascontiguousarray(sst[s].transpose(0, 2, 1))
            m["cprev" + n] = np.ascontiguousarray(scv[s].T)
            m["fprev" + n] = np.ascontiguousarray(sfc[s].T.reshape(44, 128, 2).transpose(1, 0, 2))
            m["cKT" + n] = np.ascontiguousarray(cfk[s].transpose(1, 2, 0))
            m["cV" + n] = np.ascontiguousarray(cfv[s].transpose(1, 0, 2))
            m["cLF" + n] = np.ascontiguousarray(cfl[s].T)
        in_maps.append(m)
    res = run_bass_kernel_spmd(nc, in_maps, core_ids=list(range(ncore))).results

    def seq_out(r, i, L):
        n = "_%d" % i
        y = r["yT" + n].T if i > 0 else None
        k = r["kT" + n].T.reshape(L, NH, 64)
        v = r["vT" + n].T.reshape(L, NH, 64)
        lf = r["lf" + n].T
        ss = r["sst" + n].transpose(0, 2, 1)
        sc = r["scv" + n].T
        fc = r["fcv" + n].transpose(1, 0, 2).reshape(2 * DFF, 2).T
        return [np.ascontiguousarray(a, dtype=np.float32) if a is not None else None for a in (y, k, v, lf, ss, sc, fc)]
    pr = [seq_out(res[b], 0, LP) for b in range(B)]
    CHq = LP // 4
    for b in range(B):
        yfull = np.zeros((LP, D), np.float32)
        for rq in range(4):
            c = rq * B + b
            yfull[rq * CHq:(rq + 1) * CHq] = res[c]["yT_0"].T
        pr[b][0] = yfull
        pr[b][6] = np.ascontiguousarray(res[3 * B + b]["fcv_0"].transpose(1, 0, 2).reshape(2 * DFF, 2).T)
        pr[b][4] = np.ascontiguousarray(res[3 * B + b]["sst_0"].transpose(0, 2, 1))
    sm = [seq_out(res[s // nsamp], 1 + s % nsamp, LS) for s in range(NSB)]
    outs_p = [np.stack([p[j] for p in pr])[None] if j > 0 else np.stack([p[j] for p in pr]) for j in range(7)]
    outs_s = [np.stack([p[j] for p in sm])[None] if j > 0 else np.stack([p[j] for p in sm]) for j in range(7)]
    return tuple([outs_p[0], outs_s[0]] + outs_p[1:] + outs_s[1:])
```

```python
import numpy as np
import concourse.bass as bass
import concourse.mybir as mybir
from concourse.bass_utils import run_bass_kernel_spmd
from contextlib import ExitStack

F32 = mybir.dt.float32
BF16 = mybir.dt.bfloat16
AF = mybir.ActivationFunctionType
ALU = mybir.AluOpType

D = 1024
NH = 8
DFF = 2816
INC = 3088
EPS = 1e-6
NEG = -30000.0
ENG = ["sync", "scalar", "vector", "gpsimd", "tensor"]


class T:
    def __init__(self, ap, name):
        self.ap = ap
        self.name = name
        self.lw = None
        self.rd = []
        self.dsem = None
        self.dsem_sw = None
        self.persist = False

    def __getitem__(self, k):
        return self.ap[k]


class TV:
    def __init__(self, parent, ap):
        object.__setattr__(self, "parent", parent)
        object.__setattr__(self, "ap", ap)

    def __getattr__(self, k):
        return getattr(object.__getattribute__(self, "parent"), k)

    def __setattr__(self, k, v):
        setattr(object.__getattribute__(self, "parent"), k, v)


class Slot:
    def __init__(self, sem):
        self.sem = sem
        self.count = 0


class Op:
    __slots__ = ("eng", "fn", "deps", "is_dma", "tile", "needed", "incval", "soft", "gi", "seg", "cost", "fence")

    def __init__(self, eng, fn):
        self.eng = eng
        self.fn = fn
        self.deps = []
        self.is_dma = False
        self.tile = None
        self.needed = False
        self.incval = 0
        self.soft = []
        self.gi = 0
        self.seg = 0
        self.cost = None
        self.fence = False


import os
STOP = int(os.environ.get("KSTOP", "9"))


class Prog:
    def __init__(self, nc, stack):
        self.nc = nc
        self.stack = stack
        self.ops = {e: [] for e in ENG}
        self.esem = {e: stack.enter_context(nc.semaphore("es_" + e)) for e in ENG}
        self.dtiles = []
        self.nsem = 0
        self.free_slots = []
        self.all_slots = []
        self.order = []
        self.seg = 0
        self.last_pe_w = {}

    def get_slot(self):
        if self.free_slots:
            return self.free_slots.pop()
        sl = Slot(self.stack.enter_context(self.nc.semaphore("ds%d" % self.nsem)))
        self.nsem += 1
        self.all_slots.append(sl)
        return sl

    def _tok(self, p, o):
        if p is None or p is o:
            return
        if p.is_dma:
            o.deps.append(("d", p.tile, p.tile.count))
            o.soft.append(p)
        else:
            if p.eng == o.eng and p.eng == "tensor":
                o.soft.append(p)
                return
            p.needed = True
            o.deps.append(("c", p))

    def op(self, eng, fn, r=(), w=(), dma=None, cost=None):
        o = Op(eng, fn)
        o.gi = len(self.order)
        o.seg = self.seg
        o.cost = cost
        self.order.append(o)
        for t in r:
            self._tok(t.lw, o)
        for t in w:
            self._tok(t.lw, o)
            for q in t.rd:
                self._tok(q, o)
        if dma is not None:
            o.is_dma = True
            attr = "dsem_sw" if eng == "gpsimd" else "dsem"
            if getattr(dma, attr) is None:
                setattr(dma, attr, self.get_slot())
                if dma not in self.dtiles:
                    self.dtiles.append(dma)
            sl_ = getattr(dma, attr)
            o.tile = sl_
            if getattr(sl_, "last", None) is not None:
                o.soft.append(sl_.last)
            sl_.last = o
            sl_.count += 1
        for t in r:
            t.rd.append(o)
        for t in w:
            t.lw = o
            t.rd = []
        self.ops[eng].append(o)
        return o

    def barrier(self, bar_src, bar_tile):
        o = Op("sync", lambda e: e.dma_start(out=bar_tile.ap, in_=bar_src))
        o.fence = True
        o.gi = len(self.order)
        o.seg = self.seg
        self.order.append(o)
        for e in ENG:
            if e == "sync":
                continue
            if self.ops[e]:
                last = None
                for q in reversed(self.ops[e]):
                    if q.fn is not None and not q.is_dma:
                        last = q
                        break
                if last is not None:
                    last.needed = True
                    o.deps.append(("c", last))
        for sl in self.all_slots:
            if sl.count:
                o.deps.append(("d", sl, sl.count))
        o.is_dma = True
        if bar_tile.dsem is None:
            bar_tile.dsem = self.get_slot()
            bar_tile.persist = True
        o.tile = bar_tile.dsem
        bar_tile.dsem.count += 1
        self.ops["sync"].append(o)
        for e in ENG:
            w = Op(e, None)
            w.fence = (e != "sync")
            w.gi = len(self.order)
            w.seg = self.seg
            self.order.append(w)
            w.deps.append(("d", bar_tile.dsem, bar_tile.dsem.count))
            self.ops[e].append(w)
        self.seg += 1
        for sl in self.all_slots:
            sl.last = None
        keep = []
        for t in self.dtiles:
            if getattr(t, "persist", False):
                keep.append(t)
            else:
                for attr in ("dsem", "dsem_sw"):
                    sl_ = getattr(t, attr)
                    if sl_ is not None and sl_ not in self.free_slots:
                        self.free_slots.append(sl_)
                    setattr(t, attr, None)
                t.lw = None
                t.rd = []
        self.dtiles = keep

    def emit(self, ename, eng):
        seen = {}
        for o in self.ops[ename]:
            for d in o.deps:
                if d[0] == "c":
                    p = d[1]
                    sem, val = self.esem[p.eng], p.incval
                else:
                    sem, val = d[1].sem, 16 * d[2]
                key = id(sem)
                if seen.get(key, 0) >= val:
                    continue
                seen[key] = val
                eng.wait_ge(sem, val)
            if o.fn is None:
                continue
            ins = o.fn(eng)
            if o.is_dma:
                ins.then_inc(o.tile.sem, 16)
            elif o.needed:
                ins.then_inc(self.esem[ename], 1)

    def reschedule(self, window=48):
        COST = {"tensor": 0.25, "scalar": 0.6, "vector": 0.55, "gpsimd": 0.8}
        for e in ENG:
            ops = self.ops[e]
            out = []
            i = 0
            while i < len(ops):
                j = i
                while j < len(ops) and not ops[j].fence:
                    j += 1
                out.append((ops[i:j], ops[j] if j < len(ops) else None))
                i = j + 1
            self._segs = getattr(self, "_segs", {})
            self._segs[e] = out
        nseg = max(len(v) for v in self._segs.values())
        fin = {}
        for sidx in range(nseg):
            qs = {}
            for e in ENG:
                if sidx < len(self._segs[e]):
                    qs[e] = list(self._segs[e][sidx][0])
                else:
                    qs[e] = []
            inseg = set()
            for e in ENG:
                for o in qs[e]:
                    inseg.add(id(o))
            t_eng = {e: 0.0 for e in ENG}
            newq = {e: [] for e in ENG}
            remaining = sum(len(q) for q in qs.values())
            heads = {e: 0 for e in ENG}
            done = set()

            def preds(o):
                r = []
                for d in o.deps:
                    if d[0] == "c":
                        r.append(d[1])
                for p in o.soft:
                    r.append(p)
                return r

            while remaining:
                best = None
                for e in ENG:
                    q = qs[e]
                    cnt = 0
                    for k in range(len(q)):
                        o = q[k]
                        if o is None:
                            continue
                        cnt += 1
                        if cnt > window:
                            break
                        ok = True
                        rt = 0.0
                        for p in preds(o):
                            if id(p) in inseg:
                                if id(p) not in done:
                                    ok = False
                                    break
                                rt = max(rt, fin[id(p)])
                        if not ok:
                            continue
                        st = max(rt, t_eng[e])
                        key = (st, o.gi)
                        if best is None or key < best[0]:
                            best = (key, e, k, o, st)
                        if rt <= t_eng[e]:
                            break
                assert best is not None, "scheduler deadlock"
                _, e, k, o, st = best
                qs[e][k] = None
                while qs[e] and qs[e][0] is None:
                    qs[e].pop(0)
                if o.is_dma:
                    dur = o.cost if o.cost is not None else 3.0
                    t_eng[e] = st + 0.15
                else:
                    dur = o.cost if o.cost is not None else COST.get(e, 0.5)
                    t_eng[e] = st + dur
                fin[id(o)] = st + dur
                done.add(id(o))
                newq[e].append(o)
                remaining -= 1
            for e in ENG:
                if sidx < len(self._segs[e]):
                    self._segs[e][sidx] = (newq[e], self._segs[e][sidx][1])
            sf = self._segs["sync"][sidx][1] if sidx < len(self._segs["sync"]) else None
            if sf is not None:
                sf.deps = [d for d in sf.deps if d[0] != "c"]
                for e in ENG:
                    if e == "sync":
                        continue
                    last = None
                    for ss in range(sidx, -1, -1):
                        if ss < len(self._segs[e]):
                            for q in reversed(self._segs[e][ss][0]):
                                if q.fn is not None and not q.is_dma:
                                    last = q
                                    break
                        if last is not None:
                            break
                    if last is not None:
                        last.needed = True
                        sf.deps.append(("c", last))
        for e in ENG:
            flat = []
            for (lst, fence) in self._segs[e]:
                flat.extend(lst)
                if fence is not None:
                    flat.append(fence)
            self.ops[e] = flat

    def finalize(self):
        for e in ENG:
            c = 0
            for o in self.ops[e]:
                if o.needed and not o.is_dma:
                    c += 1
                    o.incval = c


class Arena:
    def __init__(self, ap32, nwords):
        self.ap = ap32
        self.n = nwords
        self.off = 0
        self.k = 0

    def reset(self):
        self.off = 0

    def alloc(self, shape, dt, name=None):
        free = 1
        for s in shape[1:]:
            free *= s
        words = free if dt in (F32, mybir.dt.int32) else (free + 1) // 2
        words = (words + 7) // 8 * 8
        assert self.off + words <= self.n, "arena overflow %d+%d>%d (%s)" % (self.off, words, self.n, name)
        v = self.ap[0:shape[0], self.off:self.off + words]
        self.off += words
        if dt != F32:
            v = v.bitcast(dt)
        v = v[:, 0:free]
        if len(shape) == 3:
            v = v.rearrange("p (a b) -> p a b", a=shape[1])
        self.k += 1
        return T(v, name or ("t%d" % self.k))


def build(LP, n_samp=2, LS=16, LC=2048):
    nc = bass.Bass("TRN2", target_bir_lowering=False)
    seqs = [dict(L=LP, Lc=0)] + [dict(L=LS, Lc=LC) for _ in range(n_samp)]
    NS = len(seqs)

    def din(name, shape):
        return nc.dram_tensor(name, list(shape), F32, kind="ExternalInput").ap()

    def dout(name, shape):
        return nc.dram_tensor(name, list(shape), F32, kind="ExternalOutput").ap()

    def dscr(name, shape, dt=F32):
        return nc.dram_tensor(name, list(shape), dt, kind="Internal").ap()

    w_in = din("w_in", [D, INC])
    w_out = din("w_out", [D, D])
    w_up = din("w_up", [D, 2 * DFF])
    w_down = din("w_down", [DFF, D])
    g1 = din("g1", [128, 8])
    g2 = din("g2", [128, 8])
    gF = din("gF", [128, 8])
    gS = din("gS", [128, 4])
    cw = din("cw", [128, 8, 4])
    cb = din("cb", [128, 8])
    fw = din("fw", [128, 44, 3])
    fb = din("fb", [128, 44])
    dtb = din("dtb", [128, 8])
    alog = din("alog", [128, 8])
    dsk = din("dsk", [128, 8])
    fbias = din("fbias", [128, 8])
    cst = din("cst", [128, 6, 128])
    cvec = din("cvec", [128, 16])
    rmask = din("rmask", [128, 512])
    bar_src = din("bar_src", [1, 16])
    NQ = 4
    qidx = nc.dram_tensor("qidx", [128, 8], mybir.dt.int32, kind="ExternalInput").ap()
    S = []
    for i, sq in enumerate(seqs):
        L, Lc = sq["L"], sq["Lc"]
        TK = Lc + L
        d = dict(L=L, Lc=Lc, TK=TK)
        d["xT"] = din("xT_%d" % i, [D, L])
        d["st0"] = din("st0_%d" % i, [NH, 128, 64])
        d["cprev"] = din("cprev_%d" % i, [D, 3])
        d["fprev"] = din("fprev_%d" % i, [128, 44, 2])
        if Lc:
            d["cKT"] = din("cKT_%d" % i, [NH, 64, Lc])
            d["cV"] = din("cV_%d" % i, [NH, Lc, 64])
            d["cLF"] = din("cLF_%d" % i, [NH, Lc])
        d["yT"] = dout("yT_%d" % i, [D, (L // NQ) if i == 0 else L])
        d["kT"] = dout("kT_%d" % i, [512, L])
        d["vT"] = dout("vT_%d" % i, [512, L])
        d["lf"] = dout("lf_%d" % i, [NH, L])
        d["sst"] = dout("sst_%d" % i, [NH, 128, 64])
        d["scv"] = dout("scv_%d" % i, [D, 3])
        d["fcv"] = dout("fcv_%d" % i, [128, 44, 2])
        d["U"] = dscr("U_%d" % i, [INC, 3 + L])
        d["XA"] = dscr("XA_%d" % i, [D, L], BF16)
        d["ZS"] = dscr("ZS_%d" % i, [512, L], BF16)
        if i == 0:
            d["Yq_t"] = nc.dram_tensor("Yq", [D, NQ * (2 + L // NQ)], BF16)
            d["Yq"] = d["Yq_t"].ap().rearrange("r (j w) -> r j w", j=NQ)
            d["Yqv"] = d["Yq_t"].ap().rearrange("r (j w) -> (r j) w", j=NQ)
            d["xw"] = din("xw", [D, 2 + L // NQ])
            CHd = L // NQ
            d["QQ_t"] = nc.dram_tensor("QQ", [512, NQ * (2 + CHd)], BF16)
            d["KQ_t"] = nc.dram_tensor("KQ", [512, (NQ + 1) * CHd], BF16)
            d["VQ_t"] = nc.dram_tensor("VQ", [512, (NQ + 1) * CHd], BF16)
            d["FAQ_t"] = nc.dram_tensor("FAQ", [NH * 3, NQ * (2 + CHd)], BF16)
            d["FAK_t"] = nc.dram_tensor("FAK", [NH * 3, (NQ + 1) * CHd], BF16)
            d["YF"] = dscr("YF", [512, 2 + CHd], BF16)
            d["YS"] = dscr("YS", [512, 2 + CHd], BF16)
            d["XAq_t"] = nc.dram_tensor("XAq", [D, (NQ + 1) * CHd], BF16)
            d["ZSq_t"] = nc.dram_tensor("ZSq", [512, (NQ + 1) * CHd], BF16)
            d["DTq_t"] = nc.dram_tensor("DTq", [NH, (NQ + 1) * CHd], F32)
            d["dpad"] = din("dpad", [NH, CHd])
            d["sidx"] = nc.dram_tensor("sidx", [128, NH * 20], mybir.dt.int32, kind="ExternalInput").ap()
            d["zsrc"] = din("zsrc", [128, CHd])
            d["fpad"] = din("fpad", [NH * 3, CHd])
            d["gidx"] = nc.dram_tensor("gidx", [128, NH * 10], mybir.dt.int32, kind="ExternalInput").ap()
        else:
            d["Y"] = dscr("Y_%d" % i, [D, L], BF16)
        d["LFA"] = dscr("LFA_%d" % i, [NH, TK])
        d["FA"] = dscr("FA_%d" % i, [NH, 6, TK], BF16)
        S.append(d)
    wob = dscr("wob", [8, 128, 8, 128], BF16)
    wdb = dscr("wdb", [8, 128, 22, 128], BF16)
    wub = dscr("wub", [44, 128, 8, 128], BF16)
    wor = dscr("wor", [D, D], BF16)
    wdr = dscr("wdr", [DFF, D], BF16)
    wur = dscr("wur", [D, 2 * DFF], BF16)
    bar_d = dscr("bar_d", [1, 16])

    def ydst(d, r0, r1, t0, n):
        if "Yq" not in d:
            return [(d["Y"][r0:r1, t0:t0 + n], 0, n)]
        L_ = d["L"]
        CHq = L_ // NQ
        res_ = []
        t = t0
        while t < t0 + n:
            e = min(t0 + n, (t // CHq + 1) * CHq)
            j = t // CHq
            res_.append((d["Yq"][r0:r1, j, 2 + t % CHq:2 + t % CHq + (e - t)], t - t0, e - t0))
            if e % CHq == 0 and e < L_:
                res_.append((d["Yq"][r0:r1, j + 1, 0:2], e - 2 - t0, e - t0))
            t = e
        return res_

    stack = ExitStack()
    with stack:
        P = Prog(nc, stack)
        NW = 47000
        arena_t = stack.enter_context(nc.sbuf_tensor("arena", [128, NW], F32))
        A = Arena(arena_t, NW)
        psf = [T(stack.enter_context(nc.psum_tensor("psf%d" % i, [128, 512], F32)), "psf%d" % i) for i in range(6)]
        psb = [T(stack.enter_context(nc.psum_tensor("psb%d" % i, [128, 1024], BF16)), "psb%d" % i) for i in range(2)]
        bar_t = T(bar_d, "bar")
        bar_t.persist = True
        cslot = P.get_slot()
        pctr = [0, 0]

        def PS():
            pctr[0] += 1
            return psf[pctr[0] % 4]

        acc_ctr = [0]

        def PACC():
            acc_ctr[0] += 1
            return psf[4 + acc_ctr[0] % 2]

        def PB():
            pctr[1] += 1
            return psb[pctr[1] % 2]

        rr = [0]

        def dq():
            return "sync"

        def load(dst, dst_ap, src_ap, q=None, extra_r=(), extra_w=()):
            cast = (dst_ap.dtype != src_ap.dtype)
            e = "gpsimd" if cast else (q or "sync")
            return P.op(e, lambda g, a=dst_ap, b=src_ap: g.dma_start(out=a, in_=b), r=extra_r, w=(dst,) + tuple(extra_w), dma=dst)

        def store(src, dst_ap, src_ap, q=None, extra_w=(), extra_r=()):
            cast = (dst_ap.dtype != src_ap.dtype)
            e = "gpsimd" if cast else (q or "sync")
            return P.op(e, lambda g, a=dst_ap, b=src_ap: g.dma_start(out=a, in_=b), r=(src,) + tuple(extra_r), w=tuple(extra_w), dma=src)

        def d2d(tile, dst_ap, src_ap, r=(), w=()):
            cast = (dst_ap.dtype != src_ap.dtype)
            e = "gpsimd" if cast else "sync"
            return P.op(e, lambda g, a=dst_ap, b=src_ap: g.dma_start(out=a, in_=b), r=r, w=w, dma=tile)

        def barrier():
            P.barrier(bar_src, bar_t)

        C_ident = A.alloc([128, 128], F32, "ident")
        C_mask = A.alloc([128, 128], F32, "maskf")
        C_low = A.alloc([128, 128], F32, "low")
        C_identb = A.alloc([128, 128], BF16, "identb")
        C_maskb = A.alloc([128, 128], BF16, "maskb")
        C_onesb = A.alloc([128, 128], BF16, "onesb")
        C_onesf = A.alloc([128, 128], F32, "onesf")
        C_vec = A.alloc([128, 16], F32, "cvec")
        C_rm = A.alloc([128, 512], F32, "rmask")
        C_g1 = A.alloc([128, 8], F32); C_g2 = A.alloc([128, 8], F32); C_gF = A.alloc([128, 8], F32)
        C_gS = A.alloc([128, 4], F32)
        C_cw = A.alloc([128, 8, 4], F32); C_cb = A.alloc([128, 8], F32)
        C_fw = A.alloc([128, 44, 3], F32); C_fb = A.alloc([128, 44], F32)
        C_dtb = A.alloc([128, 8], F32); C_al = A.alloc([128, 8], F32); C_dsk = A.alloc([128, 8], F32)
        C_fbs = A.alloc([128, 8], F32)
        C_qidx = A.alloc([128, 8], mybir.dt.int32, "qidx")
        C_zero = A.alloc([128, 2], BF16, "zero")
        C_aneg = A.alloc([128, 8], F32)
        C_nfb = A.alloc([128, 8], F32)
        for t_ in (C_ident, C_mask, C_low, C_onesf, C_vec, C_rm, C_g1, C_g2, C_gF, C_gS, C_cw, C_cb, C_fw, C_fb, C_dtb, C_al, C_dsk, C_fbs, C_qidx):
            t_.dsem = cslot
            t_.persist = True
        load(C_ident, C_ident.ap, cst[:, 0, :])
        load(C_mask, C_mask.ap, cst[:, 1, :])
        load(C_low, C_low.ap, cst[:, 2, :])
        load(C_onesf, C_onesf.ap, cst[:, 3, :])
        load(C_vec, C_vec.ap, cvec)
        load(C_rm, C_rm.ap, rmask)
        for t_, s_ in ((C_g1, g1), (C_g2, g2), (C_gF, gF), (C_gS, gS), (C_cw, cw), (C_cb, cb), (C_fw, fw), (C_fb, fb),
                       (C_dtb, dtb), (C_al, alog), (C_dsk, dsk), (C_fbs, fbias), (C_qidx, qidx)):
            load(t_, t_.ap, s_)
        P.op("vector", lambda e: e.memset(C_zero.ap, 0.0), w=(C_zero,))
        for kt_ in range(8):
            store(C_zero, S[0]["Yq"][kt_ * 128:(kt_ + 1) * 128, 0, 0:2], C_zero.ap)
        P.op("vector", lambda e: e.tensor_copy(out=C_identb.ap, in_=C_ident.ap), r=(C_ident,), w=(C_identb,))
        P.op("vector", lambda e: e.tensor_copy(out=C_maskb.ap, in_=C_mask.ap), r=(C_mask,), w=(C_maskb,))
        P.op("vector", lambda e: e.tensor_copy(out=C_onesb.ap, in_=C_onesf.ap), r=(C_onesf,), w=(C_onesb,))
        P.op("scalar", lambda e: e.activation(out=C_aneg.ap, in_=C_al.ap, func=AF.Exp), r=(C_al,), w=(C_aneg,))
        P.op("vector", lambda e: e.tensor_scalar(out=C_aneg.ap, in0=C_aneg.ap, scalar1=-1.0, scalar2=None, op0=ALU.mult), r=(C_aneg,), w=(C_aneg,))
        P.op("vector", lambda e: e.tensor_scalar(out=C_nfb.ap, in0=C_fbs.ap, scalar1=-1.0, scalar2=None, op0=ALU.mult), r=(C_fbs,), w=(C_nfb,))
        wrT = T(None, "wrT")
        wrT.persist = True
        d2d(wrT, wor, w_out, w=(wrT,))
        d2d(wrT, wdr, w_down, w=(wrT,))
        d2d(wrT, wur, w_up, w=(wrT,))
        for m in range(8):
            d2d(bar_t, wob[m], wor[:, m * 128:(m + 1) * 128].rearrange("(k p) c -> p k c", p=128), r=(wrT,))
            d2d(bar_t, wdb[m], wdr[:, m * 128:(m + 1) * 128].rearrange("(k p) c -> p k c", p=128), r=(wrT,))
        for m in range(44):
            d2d(bar_t, wub[m], wur[:, m * 128:(m + 1) * 128].rearrange("(k p) c -> p k c", p=128), r=(wrT,))
        for d in S:
            d2d(bar_t, d["U"][512:1536, 0:3], d["cprev"])
        base0 = A.off

        def vcol(j):
            return C_vec.ap[:, j:j + 1]

        Wb = A.alloc([128, 8, INC], BF16, "Wb")
        wfs = [A.alloc([128, 772], F32, "wf") for _ in range(2)]
        wfi = 0
        for kt in range(8):
            for c0 in range(0, INC, 772):
                wf = wfs[wfi % 2]
                wfi += 1
                load(wf, wf.ap, w_in[kt * 128:(kt + 1) * 128, c0:c0 + 772])
                P.op("vector", lambda e, a=Wb.ap[:, kt, c0:c0 + 772], b=wf.ap, s=C_g1.ap[:, kt:kt + 1]:
                     e.tensor_scalar(out=a, in0=b, scalar1=s, scalar2=None, op0=ALU.mult), r=(wf, C_g1), w=(Wb,))
        base1 = A.off
        mtiles = [(m0, min(128, INC - m0)) for m0 in range(0, INC, 128)]
        for d in S:
            L = d["L"]
            TT = min(512, L)
            nb = 2
            xb = [A.alloc([128, 8, TT], BF16, "xb") for _ in range(nb)]
            xq = [A.alloc([128, 8, TT], BF16, "xq") for _ in range(nb)]
            rs = [A.alloc([128, TT], F32, "rs") for _ in range(nb)]
            ev = [A.alloc([128, TT], F32, "ev") for _ in range(4)]
            for ti, t0 in enumerate(range(0, L, TT)):
                b = ti % nb
                load(xb[b], xb[b].ap, d["xT"][:, t0:t0 + TT].rearrange("(k p) t -> p k t", p=128))
                P.op("gpsimd", lambda e, a=xq[b].ap, x=xb[b].ap: e.tensor_tensor(out=a, in0=x, in1=x, op=ALU.mult), r=(xb[b],), w=(xq[b],))
                ps = PS()
                for kt in range(8):
                    P.op("tensor", lambda e, o=ps.ap[:, 0:TT], l=C_onesb.ap, r_=xq[b].ap[:, kt, :], k=kt:
                         e.matmul(o, lhsT=l, rhs=r_, start=(k == 0), stop=(k == 7)), r=(C_onesb, xq[b]), w=(ps,))
                P.op("vector", lambda e, a=rs[b].ap, p=ps.ap[:, 0:TT]: e.tensor_scalar(out=a, in0=p, scalar1=1.0 / D, scalar2=EPS, op0=ALU.mult, op1=ALU.add), r=(ps,), w=(rs[b],))
                P.op("scalar", lambda e, a=rs[b].ap: e.activation(out=a, in_=a, func=AF.Ln), r=(rs[b],), w=(rs[b],))
                P.op("scalar", lambda e, a=rs[b].ap: e.activation(out=a, in_=a, func=AF.Exp, scale=-0.5), r=(rs[b],), w=(rs[b],))
                for mi, (m0, mw) in enumerate(mtiles):
                    ps = PS()
                    for kt in range(8):
                        P.op("tensor", lambda e, o=ps.ap[0:mw, 0:TT], l=Wb.ap[:, kt, m0:m0 + mw], r_=xb[b].ap[:, kt, :], k=kt:
                             e.matmul(o, lhsT=l, rhs=r_, start=(k == 0), stop=(k == 7)), r=(Wb, xb[b]), w=(ps,))
                    et = ev[mi % 4]
                    P.op("vector", lambda e, a=et.ap[0:mw, :], p=ps.ap[0:mw, 0:TT], r_=rs[b].ap[0:mw, :]:
                         e.tensor_tensor(out=a, in0=p, in1=r_, op=ALU.mult), r=(ps, rs[b]), w=(et,))
                    store(et, d["U"][m0:m0 + mw, 3 + t0:3 + t0 + TT], et.ap[0:mw, :])
                    if 2056 <= m0 < 2568:
                        pass
        barrier()
        A.off = base0
        QOFF = 512 + 1024 + 8
        KOFF = QOFF + 512
        VOFF = KOFF + 512
        FOFF = VOFF + 512
        DTOFF = 1536
        d0 = S[0]
        CHq = d0["L"] // NQ
        QQ3 = d0["QQ_t"].ap().rearrange("r (j w) -> r j w", j=NQ)
        KQ3 = d0["KQ_t"].ap().rearrange("r (j w) -> r j w", j=NQ + 1)
        VQ3 = d0["VQ_t"].ap().rearrange("r (j w) -> r j w", j=NQ + 1)
        FAQ3 = d0["FAQ_t"].ap().rearrange("r (j w) -> r j w", j=NQ)
        FAK3 = d0["FAK_t"].ap().rearrange("r (j w) -> r j w", j=NQ + 1)
        for j in range(NQ):
            c0 = 3 + j * CHq
            d2d(bar_t, QQ3[:, j, 2:2 + CHq], d0["U"][QOFF:QOFF + 512, c0:c0 + CHq])
            if j > 0:
                d2d(bar_t, QQ3[:, j, 0:2], d0["U"][QOFF:QOFF + 512, c0 - 2:c0])
            d2d(bar_t, KQ3[:, j, :], d0["U"][KOFF:KOFF + 512, c0:c0 + CHq])
            d2d(bar_t, VQ3[:, j, :], d0["U"][VOFF:VOFF + 512, c0:c0 + CHq])
        for k4 in range(4):
            d2d(bar_t, KQ3[k4 * 128:(k4 + 1) * 128, NQ, :], d0["zsrc"])
            d2d(bar_t, VQ3[k4 * 128:(k4 + 1) * 128, NQ, :], d0["zsrc"])
            d2d(bar_t, QQ3[k4 * 128:(k4 + 1) * 128, 0, 0:2], d0["zsrc"][:, 0:2])
        d2d(bar_t, FAK3[:, NQ, :], d0["fpad"])
        XAq3 = d0["XAq_t"].ap().rearrange("r (j w) -> r j w", j=NQ + 1)
        ZSq3 = d0["ZSq_t"].ap().rearrange("r (j w) -> r j w", j=NQ + 1)
        DTq3 = d0["DTq_t"].ap().rearrange("r (j w) -> r j w", j=NQ + 1)
        for j in range(NQ):
            d2d(bar_t, DTq3[:, j, :], d0["U"][DTOFF:DTOFF + NH, 3 + j * CHq:3 + (j + 1) * CHq])
        for k8 in range(8):
            d2d(bar_t, XAq3[k8 * 128:(k8 + 1) * 128, NQ, :], d0["zsrc"])
        for k4 in range(4):
            d2d(bar_t, ZSq3[k4 * 128:(k4 + 1) * 128, NQ, :], d0["zsrc"])
        d2d(bar_t, DTq3[:, NQ, :], d0["dpad"])
        d2d(bar_t, FAQ3[:, 0, 0:2], d0["zsrc"][0:NH * 3, 0:2])
        for d in S:
            L = d["L"]
            d2d(bar_t, d["kT"], d["U"][KOFF:KOFF + 512, 3:3 + L])
            d2d(bar_t, d["vT"], d["U"][VOFF:VOFF + 512, 3:3 + L])
            d2d(bar_t, d["scv"], d["U"][512:1536, L:L + 3])

        for d in (S if STOP >= 2 else []):
            L = d["L"]
            TT = min(512, L)
            if "Yq" in d:
                TT = min(TT, L // NQ)
            NB2 = 4
            ut = [A.alloc([128, TT + 3], F32, "cu") for _ in range(NB2)]
            ca = [A.alloc([128, TT], F32, "ca") for _ in range(NB2)]
            co = [A.alloc([128, TT], BF16, "co") for _ in range(NB2)]
            units = [("c", ct, t0) for ct in range(8) for t0 in range(0, L, TT)] + [("z", zt, t0) for zt in range(4) for t0 in range(0, L, TT)]

            def p2_load(i):
                kind, ct, t0 = units[i]
                u = ut[i % NB2]
                if kind == "c":
                    r0 = 512 + ct * 128
                    load(u, u.ap, d["U"][r0:r0 + 128, t0:t0 + TT + 3])
                else:
                    load(u, u.ap[:, 0:TT], d["U"][ct * 128:(ct + 1) * 128, 3 + t0:3 + t0 + TT])

            PF = 2
            for i in range(min(PF, len(units))):
                p2_load(i)
            for i, (kind, ct, t0) in enumerate(units):
                u, a, o = ut[i % NB2], ca[i % NB2], co[i % NB2]
                if kind == "c":
                    P.op("scalar", lambda e, a_=a.ap, u_=u.ap[:, 3:3 + TT], s=C_cw.ap[:, ct, 3:4], b_=C_cb.ap[:, ct:ct + 1]:
                         e.activation(out=a_, in_=u_, func=AF.Identity, bias=b_, scale=s), r=(u, C_cw, C_cb), w=(a,))
                    for j in range(3):
                        P.op("vector", lambda e, a_=a.ap, u_=u.ap[:, j:j + TT], s=C_cw.ap[:, ct, j:j + 1]:
                             e.scalar_tensor_tensor(out=a_, in0=u_, scalar=s, in1=a_, op0=ALU.mult, op1=ALU.add), r=(u, C_cw, a), w=(a,))
                    P.op("scalar", lambda e, o_=o.ap, a_=a.ap: e.activation(out=o_, in_=a_, func=AF.Silu), r=(a,), w=(o,))
                else:
                    P.op("scalar", lambda e, o_=o.ap, a_=u.ap[:, 0:TT]: e.activation(out=o_, in_=a_, func=AF.Silu), r=(u,), w=(o,))
                if i + PF < len(units):
                    p2_load(i + PF)
                if "Yq" in d:
                    CH2 = L // NQ
                    assert TT <= CH2
                    dstw = (XAq3 if kind == "c" else ZSq3)[ct * 128:(ct + 1) * 128, t0 // CH2, t0 % CH2:t0 % CH2 + TT]
                    store(o, dstw, o.ap)
                elif kind == "c":
                    store(o, d["XA"][ct * 128:(ct + 1) * 128, t0:t0 + TT], o.ap)
                else:
                    store(o, d["ZS"][ct * 128:(ct + 1) * 128, t0:t0 + TT], o.ap)
        barrier()
        A.off = base0
        def phase3(d):
            L = d["L"]
            TT = min(512, L)
            if "Yq" in d:
                TT = min(TT, L // NQ)
            Q = min(128, L)
            NCK = TT // Q
            XX = [A.alloc([128, TT], BF16, "XX") for _ in range(2)]
            BT = [A.alloc([128, TT], BF16, "BT") for _ in range(2)]
            CT = [A.alloc([128, TT], BF16, "CT") for _ in range(2)]
            DR = [A.alloc([128, TT], F32, "DR") for _ in range(2)]
            ZS = [A.alloc([64, TT], BF16, "ZS") for _ in range(2)]
            DTt2 = [A.alloc([128, TT], F32, "DT") for _ in range(2)]
            Ab2 = [A.alloc([128, TT], F32, "Ab") for _ in range(2)]
            Eb2 = [A.alloc([128, TT], F32, "Eb") for _ in range(2)]
            CTs2 = [A.alloc([128, TT], BF16, "CTs") for _ in range(2)]
            XD2 = [A.alloc([128, TT], F32, "XD") for _ in range(2)]
            Wf2 = [A.alloc([128, TT], F32, "Wf") for _ in range(2)]
            XDb2 = [A.alloc([128, TT], BF16, "XDb") for _ in range(2)]
            AA2 = [A.alloc([64, TT], F32, "AA") for _ in range(2)]
            BB2 = [A.alloc([64, TT], F32, "BB") for _ in range(2)]
            XDtok = [A.alloc([128, 128], BF16, "XDtok") for _ in range(3)]
            Btok = [A.alloc([128, 128], BF16, "Btok") for _ in range(3)]
            LT = [A.alloc([128, 128], F32, "LT") for _ in range(3)]
            STt = [A.alloc([128, 128], BF16, "ST") for _ in range(3)]
            yv = [A.alloc([64, 128], F32, "yv") for _ in range(2)]
            YG = [A.alloc([64, TT], BF16, "YG") for _ in range(2)]
            Sf = A.alloc([128, 64], F32, "Sf")
            Sb = A.alloc([128, 64], BF16, "Sb")
            for Wf in Wf2:
                P.op("vector", lambda e, a=Wf.ap[0:64, :]: e.memset(a, 1.0), w=(Wf,))
            it = [0]

            quarter = "Yq" in d
            if quarter:
                CHq_ = L // NQ
                XS = [A.alloc([128, CHq_], BF16, "XS") for _ in range(2)]
                BS = [A.alloc([128, CHq_], BF16, "BS") for _ in range(2)]
                CS = [A.alloc([128, CHq_], BF16, "CS") for _ in range(2)]
                ZQ = [A.alloc([64, CHq_], BF16, "ZQ") for _ in range(2)]
                DS = [A.alloc([128, CHq_], F32, "DS") for _ in range(2)]
                Cs = A.alloc([128, NH * 20], mybir.dt.int32, "sidx")
                load(Cs, Cs.ap, d["sidx"])
                XAqv = d["XAq_t"].ap().rearrange("r (j w) -> (r j) w", j=NQ + 1)
                ZSqv = d["ZSq_t"].ap().rearrange("r (j w) -> (r j) w", j=NQ + 1)
                DTqv = d["DTq_t"].ap().rearrange("r (j w) -> (r j) w", j=NQ + 1)
                sbuf_of = {}
                sctr = [0]

                def gath3(tile, src_v, col, npart):
                    P.op("gpsimd", lambda g_, a=tile.ap[0:npart, :], s_=src_v, ix=Cs.ap[0:npart, col:col + 1]:
                         g_.indirect_dma_start(out=a, out_offset=None, in_=s_, in_offset=bass.IndirectOffsetOnAxis(ap=ix, axis=0)),
                         r=(Cs,), w=(tile,), dma=tile)

                def slot_bufs(h, slot):
                    if (h, slot) not in sbuf_of:
                        sb = sctr[0] % 2
                        sctr[0] += 1
                        cb_ = h * 20 + slot * 5
                        gath3(XS[sb], XAqv, cb_ + 0, 128)
                        gath3(BS[sb], XAqv, cb_ + 1, 128)
                        gath3(CS[sb], XAqv, cb_ + 2, 128)
                        gath3(ZQ[sb], ZSqv, cb_ + 3, 64)
                        gath3(DS[sb], DTqv, cb_ + 4, 128)
                        sbuf_of[(h, slot)] = sb
                    return sbuf_of[(h, slot)]

            def prep(h, t0):
                g = h // 4
                b = it[0] % 2
                it[0] += 1
                DTt, Ab, Eb, CTs, XD, Wf, XDb, AA, BB = DTt2[b], Ab2[b], Eb2[b], CTs2[b], XD2[b], Wf2[b], XDb2[b], AA2[b], BB2[b]
                if quarter:
                    slot, tl = t0 // CHq_, t0 % CHq_
                    sb = slot_bufs(h, slot)
                    xx = TV(XS[sb], XS[sb].ap[:, tl:tl + TT])
                    bt = TV(BS[sb], BS[sb].ap[:, tl:tl + TT])
                    ct_ = TV(CS[sb], CS[sb].ap[:, tl:tl + TT])
                    zs = TV(ZQ[sb], ZQ[sb].ap[:, tl:tl + TT])
                    dr = TV(DS[sb], DS[sb].ap[:, tl:tl + TT])
                else:
                    xx, bt, ct_, dr, zs = XX[b], BT[b], CT[b], DR[b], ZS[b]
                    load(xx, xx.ap[0:64, :], d["XA"][h * 64:(h + 1) * 64, t0:t0 + TT])
                    load(xx, xx.ap[64:128, :], d["XA"][h * 64:(h + 1) * 64, t0:t0 + TT])
                    load(bt, bt.ap, d["XA"][512 + g * 128:512 + (g + 1) * 128, t0:t0 + TT])
                    load(ct_, ct_.ap, d["XA"][768 + g * 128:768 + (g + 1) * 128, t0:t0 + TT])
                    load(dr, dr.ap, d["U"][DTOFF + h:DTOFF + h + 1, 3 + t0:3 + t0 + TT].partition_broadcast(128))
                    load(zs, zs.ap, d["ZS"][h * 64:(h + 1) * 64, t0:t0 + TT])
                P.op("scalar", lambda e, a=DTt.ap, i_=dr.ap, b_=C_dtb.ap[:, h:h + 1]: e.activation(out=a, in_=i_, func=AF.Exp, bias=b_), r=(dr, C_dtb), w=(DTt,))
                P.op("scalar", lambda e, a=DTt.ap: e.activation(out=a, in_=a, func=AF.Ln, bias=1.0), r=(DTt,), w=(DTt,))
                P.op("vector", lambda e, a=Eb.ap, i_=DTt.ap, s=C_aneg.ap[:, h:h + 1]: e.tensor_scalar(out=a, in0=i_, scalar1=s, scalar2=None, op0=ALU.mult), r=(DTt, C_aneg), w=(Eb,))
                moff = 0 if Q == 128 else 1
                P.op("vector", lambda e, a=Ab.ap, m=C_rm.ap[:, moff:moff + TT], x=Eb.ap: e.tensor_tensor_scan(out=a, data0=m, data1=x, initial=0.0, op0=ALU.mult, op1=ALU.add), r=(Eb, C_rm), w=(Ab,))
                P.op("scalar", lambda e, a=Eb.ap, i_=Ab.ap: e.activation(out=a, in_=i_, func=AF.Exp), r=(Ab,), w=(Eb,))
                P.op("gpsimd", lambda e, a=CTs.ap, x=ct_.ap, y=Eb.ap: e.tensor_tensor(out=a, in0=x, in1=y, op=ALU.mult), r=(ct_, Eb), w=(CTs,))
                P.op("vector", lambda e, a=XD.ap, x=xx.ap, y=DTt.ap: e.tensor_tensor(out=a, in0=x, in1=y, op=ALU.mult), r=(xx, DTt), w=(XD,))
                for c in range(NCK):
                    c0 = c * Q
                    P.op("scalar", lambda e, a=Wf.ap[64:128, c0:c0 + Q], i_=Ab.ap[64:128, c0:c0 + Q], b_=Ab.ap[64:128, c0 + Q - 1:c0 + Q]:
                         e.activation(out=a, in_=i_, func=AF.Exp, bias=b_, scale=-1.0), r=(Ab,), w=(Wf,))
                P.op("vector", lambda e, a=XDb.ap, x=XD.ap, y=Wf.ap: e.tensor_tensor(out=a, in0=x, in1=y, op=ALU.mult), r=(XD, Wf), w=(XDb,))
                if any(is_full(t0, c_) for c_ in range(NCK)):
                    P.op("vector", lambda e, a=AA.ap, i_=Ab.ap[0:64, :]: e.tensor_scalar(out=a, in0=i_, scalar1=C_vec.ap[0:64, 1:2], scalar2=C_vec.ap[0:64, 0:1], op0=ALU.mult, op1=ALU.add), r=(Ab, C_vec), w=(AA,))
                    P.op("vector", lambda e, a=BB.ap, i_=Ab.ap[0:64, :]: e.tensor_scalar(out=a, in0=i_, scalar1=C_vec.ap[0:64, 0:1], scalar2=C_vec.ap[0:64, 2:3], op0=ALU.mult, op1=ALU.add), r=(Ab, C_vec), w=(BB,))
                return dict(xx=xx, bt=bt, ct=ct_, zs=zs, Eb=Eb, CTs=CTs, XDb=XDb, AA=AA, BB=BB, yg=YG[b], t0=t0)

            def is_full(t0, c):
                if not quarter:
                    return True
                slot, tl = t0 // CHq_, t0 % CHq_
                return slot == NQ - 1 or (slot == NQ - 2 and tl + TT == CHq_ and c == NCK - 1)

            def stageA(k, tb, c, full=True):
                c0 = c * Q
                xdt_, btk, lt, st = XDtok[k % 3], Btok[k % 3], LT[k % 3], STt[k % 3]
                XDb, bt, ct_, AA, BB = tb["XDb"], tb["bt"], tb["ct"], tb["AA"], tb["BB"]
                pb = PB()
                P.op("tensor", lambda e, o=pb.ap[0:Q, 0:128], i_=XDb.ap[:, c0:c0 + Q]: e.transpose(o, i_, C_identb.ap), r=(XDb, C_identb), w=(pb,))
                P.op("vector", lambda e, a=xdt_.ap[0:Q, :], p=pb.ap[0:Q, 0:128]: e.tensor_copy(out=a, in_=p), r=(pb,), w=(xdt_,))
                pb2 = PB()
                P.op("tensor", lambda e, o=pb2.ap[0:Q, 0:128], i_=bt.ap[:, c0:c0 + Q]: e.transpose(o, i_, C_identb.ap), r=(bt, C_identb), w=(pb2,))
                P.op("scalar", lambda e, a=btk.ap[0:Q, :], p=pb2.ap[0:Q, 0:128]: e.copy(out=a, in_=p), r=(pb2,), w=(btk,))
                if not full:
                    return
                ps = PS()
                P.op("tensor", lambda e, o=ps.ap[0:Q, 0:Q], l=AA.ap[:, c0:c0 + Q], r_=BB.ap[:, c0:c0 + Q]: e.matmul(o, lhsT=l, rhs=r_, start=True, stop=False), r=(AA, BB), w=(ps,))
                P.op("tensor", lambda e, o=ps.ap[0:Q, 0:Q], l=C_ident.ap[0:Q, 0:Q], r_=C_mask.ap[0:Q, 0:Q]: e.matmul(o, lhsT=l, rhs=r_, start=False, stop=True), r=(C_ident, C_mask), w=(ps,))
                P.op("scalar", lambda e, a=lt.ap[0:Q, 0:Q], p=ps.ap[0:Q, 0:Q]: e.activation(out=a, in_=p, func=AF.Exp), r=(ps,), w=(lt,))
                ps2 = PS()
                P.op("tensor", lambda e, o=ps2.ap[0:Q, 0:Q], l=bt.ap[:, c0:c0 + Q], r_=ct_.ap[:, c0:c0 + Q]: e.matmul(o, lhsT=l, rhs=r_, start=True, stop=True), r=(bt, ct_), w=(ps2,))
                P.op("vector", lambda e, a=st.ap[0:Q, 0:Q], p=ps2.ap[0:Q, 0:Q], l=lt.ap[0:Q, 0:Q]: e.tensor_tensor(out=a, in0=p, in1=l, op=ALU.mult), r=(ps2, lt), w=(st,))

            def stageB(k, tb, c, h, full=True):
                c0 = c * Q
                xdt_, btk, st = XDtok[k % 3], Btok[k % 3], STt[k % 3]
                yv_ = yv[k % 2]
                xx, zs, Eb, CTs, yg = tb["xx"], tb["zs"], tb["Eb"], tb["CTs"], tb["yg"]
                if full:
                    ps3 = PACC()
                    P.op("tensor", lambda e, o=ps3.ap[0:64, 0:Q], l=xdt_.ap[0:Q, 0:64], r_=st.ap[0:Q, 0:Q]: e.matmul(o, lhsT=l, rhs=r_, start=True, stop=False), r=(xdt_, st), w=(ps3,))
                    P.op("tensor", lambda e, o=ps3.ap[0:64, 0:Q], l=Sb.ap, r_=CTs.ap[:, c0:c0 + Q]: e.matmul(o, lhsT=l, rhs=r_, start=False, stop=True), r=(Sb, CTs), w=(ps3,))
                    P.op("vector", lambda e, a=yv_.ap[:, 0:Q], x=xx.ap[0:64, c0:c0 + Q], s=C_dsk.ap[0:64, h:h + 1], p=ps3.ap[0:64, 0:Q]:
                         e.scalar_tensor_tensor(out=a, in0=x, scalar=s, in1=p, op0=ALU.mult, op1=ALU.add), r=(xx, C_dsk, ps3), w=(yv_,))
                    P.op("gpsimd", lambda e, a=yg.ap[:, c0:c0 + Q], x=yv_.ap[:, 0:Q], z=zs.ap[:, c0:c0 + Q]: e.tensor_tensor(out=a, in0=x, in1=z, op=ALU.mult), r=(yv_, zs), w=(yg,))
                ps4 = PACC()
                P.op("tensor", lambda e, o=ps4.ap[:, 0:64], l=btk.ap[0:Q, :], r_=xdt_.ap[0:Q, 64:128]: e.matmul(o, lhsT=l, rhs=r_, start=True, stop=True), r=(btk, xdt_), w=(ps4,))
                P.op("vector", lambda e, a=Sf.ap, s=Eb.ap[:, c0 + Q - 1:c0 + Q], p=ps4.ap[:, 0:64]:
                     e.scalar_tensor_tensor(out=a, in0=a, scalar=s, in1=p, op0=ALU.mult, op1=ALU.add), r=(Sf, Eb, ps4), w=(Sf,))
                P.op("scalar", lambda e: e.copy(out=Sb.ap, in_=Sf.ap), r=(Sf,), w=(Sb,))
                if c == NCK - 1 and quarter:
                    slot, tl = tb["t0"] // CHq_, tb["t0"] % CHq_
                    if slot == NQ - 1:
                        store(yg, d["YS"][h * 64:(h + 1) * 64, 2 + tl:2 + tl + TT], yg.ap)
                    elif full:
                        store(yg, d["YS"][h * 64:(h + 1) * 64, 0:2], yg.ap[:, TT - 2:TT])
                elif c == NCK - 1:
                    for (dst_, c0_, c1_) in ydst(d, h * 64, (h + 1) * 64, tb["t0"], TT):
                        store(yg, dst_, yg.ap[:, c0_:c1_])

            chunks = [(h, t0, c) for h in range(NH) for t0 in range(0, L, TT) for c in range(NCK)]
            tiles_ = [(h, t0) for h in range(NH) for t0 in range(0, L, TT)]
            tidx = {t: i for i, t in enumerate(tiles_)}
            tbs = {}
            tbs[tiles_[0]] = prep(*tiles_[0])
            nxt = 1
            doneB = -1
            for i in range(len(chunks) + 1):
                curA = -1
                if i < len(chunks):
                    h, t0, c = chunks[i]
                    curA = tidx[(h, t0)]
                    stageA(i, tbs[(h, t0)], c, is_full(t0, c))
                if i >= 1:
                    h, t0, c = chunks[i - 1]
                    if t0 == 0 and c == 0:
                        load(Sf, Sf.ap, d["st0"][h])
                        P.op("vector", lambda e: e.tensor_copy(out=Sb.ap, in_=Sf.ap), r=(Sf,), w=(Sb,))
                    stageB(i - 1, tbs[(h, t0)], c, h, is_full(t0, c))
                    if c == NCK - 1:
                        doneB = tidx[(h, t0)]
                    if t0 + TT >= L and c == NCK - 1:
                        store(Sf, d["sst"][h], Sf.ap)
                while nxt < len(tiles_) and nxt - 2 <= doneB and nxt <= curA + 1:
                    tbs[tiles_[nxt]] = prep(*tiles_[nxt])
                    nxt += 1
        for d in (S if STOP >= 3 else []):
            phase3(d)
        barrier()
        A.off = base0

        def phase4(si, d):
            L, Lc, TK = d["L"], d["Lc"], d["TK"]
            base = A.off
            PL = min(128, L)
            JL = L // PL
            NKB = (TK + 127) // 128
            JK = NKB
            TKP = NKB * 128
            QT = min(512, L)
            fr = A.alloc([128, JL], F32, "fr")
            lfa = A.alloc([128, JK], F32, "lfa")
            Fc = A.alloc([128, JK], F32, "Fc")
            onesJ = A.alloc([128, JK], F32, "onesJ")
            hb = A.alloc([128, JK], BF16, "hb"); hf = A.alloc([128, JK], F32, "hf")
            r1 = A.alloc([128, JK], F32, "r1"); mb = A.alloc([128, JK], BF16, "mb"); mf = A.alloc([128, JK], F32, "mf")
            r2 = A.alloc([128, JK], F32, "r2"); lb = A.alloc([128, JK], BF16, "lb")
            sc6 = [A.alloc([128, JK], BF16, "sc6") for _ in range(6)]
            quarter = "Yq" in d
            CHq = L // NQ
            QW = (2 + CHq) if quarter else L
            Qa = A.alloc([70, QW], BF16, "Qa")
            Ka = A.alloc([70, TKP], BF16, "Ka")
            if quarter:
                G6 = A.alloc([3, QW], BF16, "G6")
                GK = A.alloc([3, TKP], BF16, "GK")
                Cg = A.alloc([128, NH * 10], mybir.dt.int32, "gidx")
                load(Cg, Cg.ap, d["gidx"])
                QQv = d["QQ_t"].ap().rearrange("r (j w) -> (r j) w", j=NQ)
                KQv = d["KQ_t"].ap().rearrange("r (j w) -> (r j) w", j=NQ + 1)
                VQv = d["VQ_t"].ap().rearrange("r (j w) -> (r j) w", j=NQ + 1)
                FAQv = d["FAQ_t"].ap().rearrange("r (j w) -> (r j) w", j=NQ)
                FAKv = d["FAK_t"].ap().rearrange("r (j w) -> (r j) w", j=NQ + 1)
                FAQ3 = d["FAQ_t"].ap().rearrange("r (j w) -> r j w", j=NQ)
                FAKf = d["FAK_t"].ap()

                def gath(tile, out_ap, src_v, col, npart, extra_r=()):
                    P.op("gpsimd", lambda g, a=out_ap, s_=src_v, ix=Cg.ap[0:npart, col:col + 1]:
                         g.indirect_dma_start(out=a, out_offset=None, in_=s_, in_offset=bass.IndirectOffsetOnAxis(ap=ix, axis=0)),
                         r=(Cg,) + tuple(extra_r), w=(tile,), dma=tile)
            Va = A.alloc([128, NKB, 65], BF16, "Va")
            vT = A.alloc([64, L], BF16, "vT")
            PT = [A.alloc([128, QT], BF16, "PT") for _ in range(3)]
            Lrow = A.alloc([65, QT], F32, "Lrow")
            rcp = A.alloc([64, QT], F32, "rcp")
            Yo = [A.alloc([64, QT], BF16, "Yo") for _ in range(2)]
            lfd = T(d["LFA"], "LFA%d" % si)
            fad = T(d["FA"], "FA%d" % si)
            P.op("vector", lambda e: e.memset(onesJ.ap, 1.0), w=(onesJ,))
            for h in range(NH):
                load(fr, fr.ap[0:PL, :], d["U"][FOFF + h, 3:3 + L].rearrange("(p j) -> p j", p=PL))
                P.op("scalar", lambda e, a=fr.ap[0:PL, :], b_=C_nfb.ap[0:PL, h:h + 1]: e.activation(out=a, in_=a, func=AF.Exp, bias=b_, scale=-1.0), r=(fr, C_nfb), w=(fr,))
                P.op("scalar", lambda e, a=fr.ap[0:PL, :]: e.activation(out=a, in_=a, func=AF.Ln, bias=1.0), r=(fr,), w=(fr,))
                P.op("vector", lambda e, a=fr.ap[0:PL, :]: e.tensor_scalar(out=a, in0=a, scalar1=-1.0, scalar2=None, op0=ALU.mult), r=(fr,), w=(fr,))
                store(fr, d["lf"][h, :].rearrange("(p j) -> p j", p=PL), fr.ap[0:PL, :])
                store(fr, d["LFA"][h, Lc:Lc + L].rearrange("(p j) -> p j", p=PL), fr.ap[0:PL, :], extra_w=(lfd,))
                if Lc:
                    d2d(lfd, d["LFA"][h, 0:Lc], d["cLF"][h, :], w=(lfd,))
                P.op("vector", lambda e: e.memset(lfa.ap, 0.0), w=(lfa,))
                full = TK // JK
                rem = TK - full * JK
                load(lfa, lfa.ap[0:full, :], d["LFA"][h, 0:full * JK].rearrange("(p j) -> p j", j=JK), extra_r=(lfd,))
                if rem:
                    load(lfa, lfa.ap[full:full + 1, 0:rem], d["LFA"][h:h + 1, full * JK:TK], extra_r=(lfd,))
                P.op("vector", lambda e: e.tensor_tensor_scan(out=Fc.ap, data0=onesJ.ap, data1=lfa.ap, initial=0.0, op0=ALU.mult, op1=ALU.add), r=(lfa, onesJ), w=(Fc,))
                ps = PS()
                P.op("tensor", lambda e, o=ps.ap[:, 0:2], r_=Fc.ap[:, JK - 1:JK]: e.matmul(o[:, 0:1], lhsT=C_low.ap, rhs=r_, start=True, stop=True), r=(C_low, Fc), w=(ps,))
                P.op("vector", lambda e, p=ps.ap[:, 0:1]: e.tensor_scalar(out=Fc.ap, in0=Fc.ap, scalar1=p, scalar2=None, op0=ALU.add), r=(ps, Fc), w=(Fc,))
                P.op("vector", lambda e: e.tensor_copy(out=hb.ap, in_=Fc.ap), r=(Fc,), w=(hb,))
                P.op("vector", lambda e: e.tensor_copy(out=hf.ap, in_=hb.ap), r=(hb,), w=(hf,))
                P.op("vector", lambda e: e.tensor_tensor(out=r1.ap, in0=Fc.ap, in1=hf.ap, op=ALU.subtract), r=(Fc, hf), w=(r1,))
                P.op("vector", lambda e: e.tensor_copy(out=mb.ap, in_=r1.ap), r=(r1,), w=(mb,))
                P.op("vector", lambda e: e.tensor_copy(out=mf.ap, in_=mb.ap), r=(mb,), w=(mf,))
                P.op("vector", lambda e: e.tensor_tensor(out=r2.ap, in0=r1.ap, in1=mf.ap, op=ALU.subtract), r=(r1, mf), w=(r2,))
                P.op("vector", lambda e: e.tensor_copy(out=lb.ap, in_=r2.ap), r=(r2,), w=(lb,))
                for i6, (src, sgn) in enumerate(((hb, 8.0), (mb, 8.0), (lb, 8.0), (hb, -8.0), (mb, -8.0), (lb, -8.0))):
                    P.op("vector", lambda e, a=sc6[i6].ap, s_=src.ap, v=sgn: e.tensor_scalar(out=a, in0=s_, scalar1=v, scalar2=None, op0=ALU.mult), r=(src,), w=(sc6[i6],))
                    if quarter:
                        PQ = 128 // NQ
                        if i6 < 3:
                            for jq in range(NQ):
                                store(sc6[i6], FAQ3[h * 3 + i6, jq, 2:2 + CHq].rearrange("(p j) -> p j", j=JK), sc6[i6].ap[jq * PQ:(jq + 1) * PQ, :], extra_w=(fad,))
                                if jq > 0:
                                    store(sc6[i6], FAQ3[h * 3 + i6:h * 3 + i6 + 1, jq, 0:2], sc6[i6].ap[jq * PQ - 1:jq * PQ, JK - 2:JK], extra_w=(fad,))
                        else:
                            store(sc6[i6], FAKf[h * 3 + i6 - 3, 0:TK].rearrange("(p j) -> p j", j=JK), sc6[i6].ap[0:full, :], extra_w=(fad,))
                        continue
                    store(sc6[i6], d["FA"][h, i6, 0:full * JK].rearrange("(p j) -> p j", j=JK), sc6[i6].ap[0:full, :], extra_w=(fad,))
                    if rem:
                        store(sc6[i6], d["FA"][h, i6:i6 + 1, full * JK:TK], sc6[i6].ap[full:full + 1, 0:rem], extra_w=(fad,))
                P.op("vector", lambda e: e.memset(Qa.ap[64:70, :], 1.0), w=(Qa,))
                P.op("vector", lambda e: e.memset(Ka.ap[64:70, :], 1.0), w=(Ka,))
                if quarter:
                    gb = h * 10
                    gath(Qa, Qa.ap[0:64, :], QQv, gb + 0, 64)
                    gath(G6, G6.ap[0:3, :], FAQv, gb + 5, 3, extra_r=(fad,))
                    P.op("sync", lambda g: g.dma_start(out=Qa.ap[64:67, :], in_=G6.ap[0:3, :]), r=(G6,), w=(Qa,), dma=Qa)
                    for sq_ in range(NQ):
                        gath(Ka, Ka.ap[0:64, sq_ * CHq:(sq_ + 1) * CHq], KQv, gb + 1 + sq_, 64)
                        gath(vT, vT.ap[0:64, sq_ * CHq:(sq_ + 1) * CHq], VQv, gb + 1 + sq_, 64)
                        gath(GK, GK.ap[0:3, sq_ * CHq:(sq_ + 1) * CHq], FAKv, gb + 6 + sq_, 3, extra_r=(fad,))
                    P.op("sync", lambda g: g.dma_start(out=Ka.ap[67:70, 0:TK], in_=GK.ap[0:3, 0:TK]), r=(GK,), w=(Ka,), dma=Ka)
                    P.op("vector", lambda e: e.memset(Va.ap[:, :, 64:65], 1.0), w=(Va,))
                else:
                    load(Qa, Qa.ap[0:64, :], d["U"][QOFF + h * 64:QOFF + (h + 1) * 64, 3:3 + L])
                    load(Qa, Qa.ap[64:67, :], d["FA"][h, 0:3, Lc:Lc + L], extra_r=(fad,))
                    if Lc:
                        load(Ka, Ka.ap[0:64, 0:Lc], d["cKT"][h])
                    load(Ka, Ka.ap[0:64, Lc:TK], d["U"][KOFF + h * 64:KOFF + (h + 1) * 64, 3:3 + L])
                    load(Ka, Ka.ap[67:70, 0:TK], d["FA"][h, 3:6, 0:TK], extra_r=(fad,))
                    P.op("vector", lambda e: e.memset(Va.ap[:, :, 64:65], 1.0), w=(Va,))
                    if Lc:
                        load(Va, Va.ap[:, 0:Lc // 128, 0:64], d["cV"][h].rearrange("(b p) d -> p b d", p=128))
                    load(vT, vT.ap, d["U"][VOFF + h * 64:VOFF + (h + 1) * 64, 3:3 + L])
                for kb in range(Lc // 128, NKB):
                    k0 = kb * 128 - Lc
                    kw = min(128, L - k0)
                    pb = PB()
                    P.op("tensor", lambda e, o=pb.ap[0:kw, 0:64], i_=vT.ap[:, k0:k0 + kw]: e.transpose(o, i_, C_identb.ap[0:64, 0:64]), r=(vT, C_identb), w=(pb,))
                    P.op("vector", lambda e, a=Va.ap[0:kw, kb, 0:64], p=pb.ap[0:kw, 0:64]: e.tensor_copy(out=a, in_=p), r=(pb,), w=(Va,))
                tasks = []
                if quarter:
                    QTq = min(512, CHq)
                    qtiles = [(0, 2, 3 * CHq - 2)] + [(2 + q_, QTq, 3 * CHq + q_) for q_ in range(0, CHq, QTq)]
                else:
                    qtiles = [(q_, QT, Lc + q_) for q_ in range(0, L, QT)]
                for qi, (q0, wq, qa0) in enumerate(qtiles):
                    last_kb = (qa0 + wq - 1) // 128
                    for kb in range(0, last_kb + 1):
                        kpos = kb * 128
                        kw = min(128, TK - kpos)
                        o_ = max(0, kpos - qa0)
                        tasks.append(dict(qi=qi, q0=q0, w=wq, kb=kb, kpos=kpos, kw=kw, o=o_, nq=wq - o_, moff=qa0 + o_ - kpos,
                                          diag=(kpos + kw - 1 > qa0 + o_), first=(kb == 0), last=(kb == last_kb)))
                LA = 2
                pos = {}
                for i in range(len(tasks) + LA):
                    if i < len(tasks):
                        t = tasks[i]
                        ps = PS()
                        t["ps"] = ps
                        kw, o_, kpos, q0, wq, mo = t["kw"], t["o"], t["kpos"], t["q0"], t["w"], t["moff"]
                        P.op("tensor", lambda e, o=ps.ap[0:kw, o_:wq], l=Ka.ap[:, kpos:kpos + kw], r_=Qa.ap[:, q0 + o_:q0 + wq], dg=t["diag"]:
                             e.matmul(o, lhsT=l, rhs=r_, start=True, stop=(not dg)), r=(Ka, Qa), w=(ps,))
                        if t["diag"]:
                            mw_ = min(kw - mo, t["nq"])
                            P.op("tensor", lambda e, o=ps.ap[0:kw, o_:o_ + mw_], l=C_identb.ap[0:kw, 0:kw], r_=C_maskb.ap[0:kw, mo:mo + mw_]:
                                 e.matmul(o, lhsT=l, rhs=r_, start=False, stop=True), r=(C_identb, C_maskb), w=(ps,))
                    j = i - LA
                    if j >= 0:
                        t = tasks[j]
                        ps = t["ps"]
                        kw, o_, kb, q0, qi, wq = t["kw"], t["o"], t["kb"], t["q0"], t["qi"], t["w"]
                        if t["first"]:
                            pos[qi] = PACC()
                        po = pos[qi]
                        pt = PT[j % 3]
                        P.op("scalar", lambda e, a=pt.ap[0:kw, o_:wq], p=ps.ap[0:kw, o_:wq]: e.activation(out=a, in_=p, func=AF.Exp, scale=0.125), r=(ps,), w=(pt,))
                        P.op("tensor", lambda e, o=po.ap[0:65, o_:wq], l=Va.ap[0:kw, kb, :], r_=pt.ap[0:kw, o_:wq], f=t["first"], la=t["last"]:
                             e.matmul(o, lhsT=l, rhs=r_, start=f, stop=la), r=(Va, pt), w=(po,))
                        if t["last"]:
                            P.op("scalar", lambda e, a=Lrow.ap[64:65, 0:wq], p=po.ap[64:65, 0:wq]: e.copy(out=a, in_=p), r=(po,), w=(Lrow,))
                            pl = PS()
                            P.op("tensor", lambda e, o=pl.ap[0:64, 0:wq], l=C_onesf.ap[64:65, 0:64], r_=Lrow.ap[64:65, 0:wq]: e.matmul(o, lhsT=l, rhs=r_, start=True, stop=True), r=(C_onesf, Lrow), w=(pl,))
                            P.op("vector", lambda e, a=rcp.ap[:, 0:wq], p=pl.ap[0:64, 0:wq]: e.tensor_scalar(out=a, in0=p, scalar1=1e-30, scalar2=None, op0=ALU.max), r=(pl,), w=(rcp,))
                            P.op("vector", lambda e, a=rcp.ap[:, 0:wq]: e.reciprocal(out=a, in_=a), r=(rcp,), w=(rcp,))
                            yo = Yo[qi % 2]
                            P.op("vector", lambda e, a=yo.ap[:, 0:wq], p=po.ap[0:64, 0:wq], r_=rcp.ap[:, 0:wq]: e.tensor_tensor(out=a, in0=p, in1=r_, op=ALU.mult), r=(po, rcp), w=(yo,))
                            if quarter:
                                store(yo, d["YF"][h * 64:(h + 1) * 64, q0:q0 + wq], yo.ap[:, 0:wq])
                            else:
                                for (dst_, c0_, c1_) in ydst(d, 512 + h * 64, 512 + (h + 1) * 64, q0, wq):
                                    store(yo, dst_, yo.ap[:, c0_:c1_])
        for si, d in (enumerate(S) if STOP >= 4 else []):
            phase4(si, d)
        barrier()
        A.off = base0

        base = A.off
        Wu = [A.alloc([128, 8, 128], BF16, "Wu") for _ in range(4)]
        Wo = [A.alloc([128, 8, 128], BF16, "Wo") for _ in range(2)]
        Wd = [A.alloc([128, 22, 128], BF16, "Wd") for _ in range(2)]
        NT = 512
        Yt2 = [A.alloc([128, 8, NT], F32, "Yt")] * 2
        Yn2 = [A.alloc([128, 8, NT], BF16, "Yn") for _ in range(2)]
        xt2 = [A.alloc([128, 8, NT], F32, "xt") for _ in range(2)]
        n22 = [A.alloc([128, 8, NT], BF16, "n2") for _ in range(2)]
        rs2 = [A.alloc([128, NT], F32, "rs5") for _ in range(2)]
        tctr = [0]
        mt = A.alloc([128, 22, NT], BF16, "mt")
        ug = [A.alloc([128, NT], F32, "ug") for _ in range(2)]
        uv = [A.alloc([128, NT], F32, "uv") for _ in range(2)]
        cg = [A.alloc([128, NT], F32, "cg")] * 2
        cv = [A.alloc([128, NT], F32, "cv")] * 2
        sg = [A.alloc([128, NT], F32, "sg")] * 2
        fpv = A.alloc([128, 44, 2], F32, "fpv")
        UL = A.alloc([128, 44, 2], F32, "UL")
        wi = [0, 0, 0]

        def rstd_from(ps_ap, out_ap, n, scale, rs):
            P.op("vector", lambda e: e.tensor_scalar(out=out_ap, in0=ps_ap, scalar1=scale, scalar2=EPS, op0=ALU.mult, op1=ALU.add), r=(cur_ps[0],), w=(rs,))
            P.op("scalar", lambda e: e.activation(out=out_ap, in_=out_ap, func=AF.Ln), r=(rs,), w=(rs,))
            P.op("scalar", lambda e: e.activation(out=out_ap, in_=out_ap, func=AF.Exp, scale=-0.5), r=(rs,), w=(rs,))

        cur_ps = [None]
        Yw = None
        for d in (S if STOP >= 5 else []):
            quarter = "Yq" in d
            L = d["L"] // NQ if quarter else d["L"]
            load(fpv, fpv.ap, d["fprev"])
            if quarter:
                Yw = A.alloc([128, 8, 2 + L], BF16, "Yw")
                for kt in range(4):
                    load(Yw, Yw.ap[:, kt, :], d["YS"][kt * 128:(kt + 1) * 128, :])
                for kt in range(4, 8):
                    load(Yw, Yw.ap[:, kt, :], d["YF"][(kt - 4) * 128:(kt - 3) * 128, :])
            step = NT - 2
            tiles = []
            t = 0
            while t < L:
                n = min(step, L - t)
                tiles.append((t, n))
                t += n
            def tile_body(t0, n, Yt, Yn, xt, n2, rs):
                sq = n2
                yo5 = Yt
                W = n + 2
                if quarter:
                    hl = 2
                    c_lo = 0
                    ysrc = lambda kt: Yw.ap[:, kt, t0:t0 + W]
                    ytile = Yw
                    load(xt, xt.ap[:, :, 0:W], d["xw"][:, t0:t0 + W].rearrange("(k p) t -> p k t", p=128))
                else:
                    hl = 2 if t0 > 0 else 0
                    c_lo = 2 - hl
                    if hl == 0:
                        P.op("vector", lambda e: e.memset(Yt.ap[:, :, 0:2], 0.0), w=(Yt,))
                        P.op("vector", lambda e: e.memset(xt.ap[:, :, 0:2], 0.0), w=(xt,))
                    load(Yt, Yt.ap[:, :, c_lo:W], d["Y"][:, t0 - hl:t0 + n].rearrange("(k p) t -> p k t", p=128))
                    load(xt, xt.ap[:, :, c_lo:W], d["xT"][:, t0 - hl:t0 + n].rearrange("(k p) t -> p k t", p=128))
                    ysrc = lambda kt: Yt.ap[:, kt, 0:W]
                    ytile = Yt
                for kt in range(4):
                    P.op("vector", lambda e, a=sq.ap[:, kt, 0:W], x=ysrc(kt): e.tensor_tensor(out=a, in0=x, in1=x, op=ALU.mult), r=(ytile,), w=(sq,))
                for g in range(2):
                    ps = PS()
                    cur_ps[0] = ps
                    for k in range(2):
                        P.op("tensor", lambda e, o=ps.ap[:, 0:W], r_=sq.ap[:, 2 * g + k, 0:W], k_=k: e.matmul(o, lhsT=C_onesb.ap, rhs=r_, start=(k_ == 0), stop=(k_ == 1)), r=(C_onesb, sq), w=(ps,))
                    rstd_from(ps.ap[:, 0:W], rs.ap[:, 0:W], W, 1.0 / 256, rs)
                    for k in range(2):
                        kt = 2 * g + k
                        P.op("vector", lambda e, a=Yn.ap[:, kt, 0:W], x=ysrc(kt), s=C_gS.ap[:, kt:kt + 1], r_=rs.ap[:, 0:W]:
                             e.scalar_tensor_tensor(out=a, in0=x, scalar=s, in1=r_, op0=ALU.mult, op1=ALU.mult), r=(ytile, C_gS, rs), w=(Yn,))
                for kt in range(4, 8):
                    P.op("scalar", lambda e, a=Yn.ap[:, kt, 0:W], x=ysrc(kt): e.copy(out=a, in_=x), r=(ytile,), w=(Yn,))
                for m in range(8):
                    wo = Wo[wi[0] % 2]
                    wi[0] += 1
                    load(wo, wo.ap, wob[m])
                    ps = PS()
                    for kt in range(8):
                        P.op("tensor", lambda e, o=ps.ap[:, 0:W], l=wo.ap[:, kt, :], r_=Yn.ap[:, kt, 0:W], k_=kt: e.matmul(o, lhsT=l, rhs=r_, start=(k_ == 0), stop=(k_ == 7)), r=(wo, Yn), w=(ps,))
                    P.op("vector", lambda e, a=xt.ap[:, m, 0:W], p=ps.ap[:, 0:W]: e.tensor_tensor(out=a, in0=a, in1=p, op=ALU.add), r=(xt, ps), w=(xt,))
                for kt in range(8):
                    P.op("vector", lambda e, a=sq.ap[:, kt, 0:W], x=xt.ap[:, kt, 0:W]: e.tensor_tensor(out=a, in0=x, in1=x, op=ALU.mult), r=(xt,), w=(sq,))
                ps = PS()
                cur_ps[0] = ps
                for kt in range(8):
                    P.op("tensor", lambda e, o=ps.ap[:, 0:W], r_=sq.ap[:, kt, 0:W], k_=kt: e.matmul(o, lhsT=C_onesb.ap, rhs=r_, start=(k_ == 0), stop=(k_ == 7)), r=(C_onesb, sq), w=(ps,))
                rstd_from(ps.ap[:, 0:W], rs.ap[:, 0:W], W, 1.0 / D, rs)
                for kt in range(8):
                    P.op("vector", lambda e, a=n2.ap[:, kt, 0:W], x=xt.ap[:, kt, 0:W], s=C_g2.ap[:, kt:kt + 1], r_=rs.ap[:, 0:W]:
                         e.scalar_tensor_tensor(out=a, in0=x, scalar=s, in1=r_, op0=ALU.mult, op1=ALU.mult), r=(xt, C_g2, rs), w=(n2,))
                for j in range(22):
                    bsel = j % 2
                    res = []
                    for (mi, ub, cbuf) in ((j, ug[bsel], cg[bsel]), (j + 22, uv[bsel], cv[bsel])):
                        wu = Wu[wi[2] % 4]
                        wi[2] += 1
                        load(wu, wu.ap, wub[mi])
                        ps = PS()
                        for kt in range(8):
                            P.op("tensor", lambda e, o=ps.ap[:, 0:W], l=wu.ap[:, kt, :], r_=n2.ap[:, kt, 0:W], k_=kt:
                                 e.matmul(o, lhsT=l, rhs=r_, start=(k_ == 0), stop=(k_ == 7)), r=(wu, n2), w=(ps,))
                        P.op("scalar", lambda e, a=ub.ap[:, 0:W], p=ps.ap[:, 0:W]: e.copy(out=a, in_=p), r=(ps,), w=(ub,))
                        if hl == 0:
                            P.op("gpsimd", lambda e, a=ub.ap[:, 0:2], s_=fpv.ap[:, mi, :]: e.tensor_copy(out=a, in_=s_), r=(fpv,), w=(ub,))
                        if t0 + n == L:
                            P.op("gpsimd", lambda e, a=UL.ap[:, mi, :], s_=ub.ap[:, W - 2:W]: e.tensor_copy(out=a, in_=s_), r=(ub,), w=(UL,))
                        P.op("scalar", lambda e, a=cbuf.ap[:, 0:n], u_=ub.ap[:, 2:W], s=C_fw.ap[:, mi, 2:3], b_=C_fb.ap[:, mi:mi + 1]:
                             e.activation(out=a, in_=u_, func=AF.Identity, bias=b_, scale=s), r=(ub, C_fw, C_fb), w=(cbuf,))
                        P.op("vector", lambda e, a=cbuf.ap[:, 0:n], u_=ub.ap[:, 1:W - 1], s=C_fw.ap[:, mi, 1:2]:
                             e.scalar_tensor_tensor(out=a, in0=u_, scalar=s, in1=a, op0=ALU.mult, op1=ALU.add), r=(ub, C_fw, cbuf), w=(cbuf,))
                        P.op("vector", lambda e, a=cbuf.ap[:, 0:n], u_=ub.ap[:, 0:W - 2], s=C_fw.ap[:, mi, 0:1]:
                             e.scalar_tensor_tensor(out=a, in0=u_, scalar=s, in1=a, op0=ALU.mult, op1=ALU.add), r=(ub, C_fw, cbuf), w=(cbuf,))
                    sgb = sg[bsel]
                    P.op("scalar", lambda e, a=sgb.ap[:, 0:n], i_=cg[bsel].ap[:, 0:n]: e.activation(out=a, in_=i_, func=AF.Silu), r=(cg[bsel],), w=(sgb,))
                    P.op("gpsimd", lambda e, a=mt.ap[:, j, 0:n], x=sgb.ap[:, 0:n], y=cv[bsel].ap[:, 0:n]: e.tensor_tensor(out=a, in0=x, in1=y, op=ALU.mult), r=(sgb, cv[bsel]), w=(mt,))
                for m in range(8):
                    wd = Wd[wi[1] % 2]
                    wi[1] += 1
                    load(wd, wd.ap, wdb[m])
                    ps = PS()
                    for kt in range(22):
                        P.op("tensor", lambda e, o=ps.ap[:, 0:n], l=wd.ap[:, kt, :], r_=mt.ap[:, kt, 0:n], k_=kt: e.matmul(o, lhsT=l, rhs=r_, start=(k_ == 0), stop=(k_ == 21)), r=(wd, mt), w=(ps,))
                    P.op("vector", lambda e, a=xt.ap[:, m, 2:W], p=ps.ap[:, 0:n]: e.tensor_tensor(out=a, in0=a, in1=p, op=ALU.add), r=(xt, ps), w=(xt,))
                for kt in range(8):
                    P.op("vector", lambda e, a=sq.ap[:, kt, 0:n], x=xt.ap[:, kt, 2:W]: e.tensor_tensor(out=a, in0=x, in1=x, op=ALU.mult), r=(xt,), w=(sq,))
                ps = PS()
                cur_ps[0] = ps
                for kt in range(8):
                    P.op("tensor", lambda e, o=ps.ap[:, 0:n], r_=sq.ap[:, kt, 0:n], k_=kt: e.matmul(o, lhsT=C_onesb.ap, rhs=r_, start=(k_ == 0), stop=(k_ == 7)), r=(C_onesb, sq), w=(ps,))
                rstd_from(ps.ap[:, 0:n], rs.ap[:, 0:n], n, 1.0 / D, rs)
                for kt in range(8):
                    P.op("vector", lambda e, a=yo5.ap[:, kt, 0:n], x=xt.ap[:, kt, 2:W], s=C_gF.ap[:, kt:kt + 1], r_=rs.ap[:, 0:n]:
                         e.scalar_tensor_tensor(out=a, in0=x, scalar=s, in1=r_, op0=ALU.mult, op1=ALU.mult), r=(xt, C_gF, rs), w=(yo5,))
                store(yo5, d["yT"][:, t0:t0 + n].rearrange("(k p) t -> p k t", p=128), yo5.ap[:, :, 0:n])
            for ti_, (t0, n) in enumerate(tiles):
                par = tctr[0] % 2
                tctr[0] += 1
                tile_body(t0, n, Yt2[par], Yn2[par], xt2[par], n22[par], rs2[par])
            store(UL, d["fcv"], UL.ap)
        A.off = base
        barrier()

        if os.environ.get("KNOSCHED") is None:
            P.reschedule()
        P.finalize()
        with nc.Block() as block:
            @block.sync
            def _(e):
                P.emit("sync", e)

            @block.scalar
            def _(e):
                P.emit("scalar", e)

            @block.vector
            def _(e):
                P.emit("vector", e)

            @block.gpsimd
            def _(e):
                P.emit("gpsimd", e)

            @block.tensor
            def _(e):
                P.emit("tensor", e)
    return nc, seqs


_CACHE = {}


def _consts():
    cst = np.zeros((128, 6, 128), np.float32)
    cst[:, 0, :] = np.eye(128, dtype=np.float32)
    s = np.arange(128)[:, None]
    l = np.arange(128)[None, :]
    cst[:, 1, :] = np.where(l >= s, 0.0, NEG)
    cst[:, 2, :] = (s < l).astype(np.float32)
    cst[:, 3, :] = 1.0
    cvec = np.zeros((128, 16), np.float32)
    cvec[0, 0] = 1.0
    cvec[1, 1] = 1.0
    cvec[1, 2] = -1.0
    cvec[:, 3] = EPS
    cvec[:, 4] = 1.0
    rmask = np.ones((128, 512), np.float32)
    rmask[:, ::128] = 0.0
    return cst, cvec, rmask


def _pk(v, n):
    return np.ascontiguousarray(np.asarray(v, np.float32).reshape(n, 128).T)


def kernel(x_prompt, x_sample, cache_fox_k, cache_fox_v, cache_fox_logf, state_ssd, state_ssd_conv,
           state_ffn_conv, norm1_g, w_in, ssd_conv_w, ssd_conv_b, ssd_dt_bias, ssd_a_log, ssd_d,
           ssd_norm_g, fox_f_bias, w_out, norm2_g, w_up, ffn_conv_w, ffn_conv_b, w_down, final_norm_g):
    f = lambda a: np.ascontiguousarray(np.asarray(a, dtype=np.float32))
    x_prompt = f(x_prompt); x_sample = f(x_sample)
    B, LP, _ = x_prompt.shape
    NSB, LS, _ = x_sample.shape
    LC = cache_fox_k.shape[2]
    ncore = 8
    nsamp = NSB // ncore
    key = (LP, nsamp, LS, LC)
    if key not in _CACHE:
        _CACHE[key] = build(LP, nsamp, LS, LC)
    nc, seqs = _CACHE[key]
    cst, cvec, rmask = _consts()
    rep = lambda v: np.ascontiguousarray(np.broadcast_to(np.asarray(v, np.float32)[None, :], (128, len(v))))
    common = {
        "w_in": f(w_in[0]), "w_out": f(w_out[0]), "w_up": f(w_up[0]), "w_down": f(w_down[0]),
        "g1": _pk(norm1_g[0], 8), "g2": _pk(norm2_g[0], 8), "gF": _pk(final_norm_g, 8), "gS": _pk(ssd_norm_g[0], 4),
        "cw": np.ascontiguousarray(f(ssd_conv_w[0]).T.reshape(8, 128, 4).transpose(1, 0, 2)),
        "cb": _pk(ssd_conv_b[0], 8),
        "fw": np.ascontiguousarray(f(ffn_conv_w[0]).T.reshape(44, 128, 3).transpose(1, 0, 2)),
        "fb": _pk(ffn_conv_b[0], 44),
        "dtb": rep(ssd_dt_bias[0]), "alog": rep(ssd_a_log[0]), "dsk": rep(ssd_d[0]), "fbias": rep(fox_f_bias[0]),
        "cst": cst, "cvec": cvec, "rmask": rmask, "bar_src": np.zeros((1, 16), np.float32),
    }
    cfk = f(cache_fox_k[0]); cfv = f(cache_fox_v[0]); cfl = f(cache_fox_logf[0])
    sst = f(state_ssd[0]); scv = f(state_ssd_conv[0]); sfc = f(state_ffn_conv[0])
    in_maps = []
    for c in range(ncore):
        m = dict(common)
        b = c % B
        m["xT_0"] = np.ascontiguousarray(x_prompt[b].T)
        rq = c // B
        CHq = LP // 4
        xwm = np.zeros((D, 2 + CHq), np.float32)
        if rq > 0:
            xwm[:, 0:2] = x_prompt[b][rq * CHq - 2:rq * CHq].T
        xwm[:, 2:] = x_prompt[b][rq * CHq:(rq + 1) * CHq].T
        m["xw"] = xwm
        pp = np.arange(128)
        m["qidx"] = np.stack([((kt * 128 + pp) * 4 + rq) for kt in range(8)], axis=1).astype(np.int32)
        m["zsrc"] = np.zeros((128, CHq), np.float32)
        fp_ = np.zeros((NH * 3, CHq), np.float32)
        fp_[0::3, :] = -240000.0
        m["fpad"] = fp_
        gi = np.zeros((128, NH * 10), np.int32)
        p64 = np.minimum(pp, 63)
        p3 = np.minimum(pp, 2)
        for h_ in range(NH):
            gi[:, h_ * 10 + 0] = (h_ * 64 + p64) * 4 + rq
            gi[:, h_ * 10 + 5] = (h_ * 3 + p3) * 4 + rq
            for sq_ in range(4):
                slot = sq_ - (3 - rq)
                if slot < 0:
                    slot = 4
                gi[:, h_ * 10 + 1 + sq_] = (h_ * 64 + p64) * 5 + slot
                gi[:, h_ * 10 + 6 + sq_] = (h_ * 3 + p3) * 5 + slot
        m["gidx"] = gi
        dp_ = np.full((NH, CHq), -100.0, np.float32)
        m["dpad"] = dp_
        si_ = np.zeros((128, NH * 20), np.int32)
        for h_ in range(NH):
            g_ = h_ // 4
            for sq_ in range(4):
                slot = sq_ - (3 - rq)
                if slot < 0:
                    slot = 4
                cb_ = h_ * 20 + sq_ * 5
                si_[:, cb_ + 0] = (h_ * 64 + (pp % 64)) * 5 + slot
                si_[:, cb_ + 1] = (512 + g_ * 128 + pp) * 5 + slot
                si_[:, cb_ + 2] = (768 + g_ * 128 + pp) * 5 + slot
                si_[:, cb_ + 3] = (h_ * 64 + p64) * 5 + slot
                si_[:, cb_ + 4] = h_ * 5 + slot
        m["sidx"] = si_
        m["st0_0"] = np.zeros((NH, 128, 64), np.float32)
        m["cprev_0"] = np.zeros((D, 3), np.float32)
        m["fprev_0"] = np.zeros((128, 44, 2), np.float32)
        for i in range(nsamp):
            s = c * nsamp + i
            n = "_%d" % (i + 1)
            m["xT" + n] = np.ascontiguousarray(x_sample[s].T)
            m["st0" + n] = np.ascontiguousarray(sst[s].transpose(0, 2, 1))
            m["cprev" + n] = np.ascontiguousarray(scv[s].T)
            m["fprev" + n] = np.ascontiguousarray(sfc[s].T.reshape(44, 128, 2).transpose(1, 0, 2))
            m["cKT" + n] = np.ascontiguousarray(cfk[s].transpose(1, 2, 0))
            m["cV" + n] = np.ascontiguousarray(cfv[s].transpose(1, 0, 2))
            m["cLF" + n] = np.ascontiguousarray(cfl[s].T)
        in_maps.append(m)
    res = run_bass_kernel_spmd(nc, in_maps, core_ids=list(range(ncore))).results

    def seq_out(r, i, L):
        n = "_%d" % i
        y = r["yT" + n].T if i > 0 else None
        k = r["kT" + n].T.reshape(L, NH, 64)
        v = r["vT" + n].T.reshape(L, NH, 64)
        lf = r["lf" + n].T
        ss = r["sst" + n].transpose(0, 2, 1)
        sc = r["scv" + n].T
        fc = r["fcv" + n].transpose(1, 0, 2).reshape(2 * DFF, 2).T
        return [np.ascontiguousarray(a, dtype=np.float32) if a is not None else None for a in (y, k, v, lf, ss, sc, fc)]
    pr = [seq_out(res[b], 0, LP) for b in range(B)]
    CHq = LP // 4
    for b in range(B):
        yfull = np.zeros((LP, D), np.float32)
        for rq in range(4):
            c = rq * B + b
            yfull[rq * CHq:(rq + 1) * CHq] = res[c]["yT_0"].T
        pr[b][0] = yfull
        pr[b][6] = np.ascontiguousarray(res[3 * B + b]["fcv_0"].transpose(1, 0, 2).reshape(2 * DFF, 2).T)
        pr[b][4] = np.ascontiguousarray(res[3 * B + b]["sst_0"].transpose(0, 2, 1))
    sm = [seq_out(res[s // nsamp], 1 + s % nsamp, LS) for s in range(NSB)]
    outs_p = [np.stack([p[j] for p in pr])[None] if j > 0 else np.stack([p[j] for p in pr]) for j in range(7)]
    outs_s = [np.stack([p[j] for p in sm])[None] if j > 0 else np.stack([p[j] for p in sm]) for j in range(7)]
    return tuple([outs_p[0], outs_s[0]] + outs_p[1:] + outs_s[1:])
```

```python
import numpy as np
import concourse.bass as bass
import concourse.mybir as mybir
from concourse.bass_utils import run_bass_kernel_spmd
from contextlib import ExitStack

F32 = mybir.dt.float32
BF16 = mybir.dt.bfloat16
AF = mybir.ActivationFunctionType
ALU = mybir.AluOpType

D = 1024
NH = 8
DFF = 2816
INC = 3088
EPS = 1e-6
NEG = -30000.0
ENG = ["sync", "scalar", "vector", "gpsimd", "tensor"]


class T:
    def __init__(self, ap, name):
        self.ap = ap
        self.name = name
        self.lw = None
        self.rd = []
        self.dsem = None
        self.dsem_sw = None
        self.persist = False

    def __getitem__(self, k):
        return self.ap[k]


class TV:
    def __init__(self, parent, ap):
        object.__setattr__(self, "parent", parent)
        object.__setattr__(self, "ap", ap)

    def __getattr__(self, k):
        return getattr(object.__getattribute__(self, "parent"), k)

    def __setattr__(self, k, v):
        setattr(object.__getattribute__(self, "parent"), k, v)


class Slot:
    def __init__(self, sem):
        self.sem = sem
        self.count = 0


class Op:
    __slots__ = ("eng", "fn", "deps", "is_dma", "tile", "needed", "incval", "soft", "gi", "seg", "cost", "fence")

    def __init__(self, eng, fn):
        self.eng = eng
        self.fn = fn
        self.deps = []
        self.is_dma = False
        self.tile = None
        self.needed = False
        self.incval = 0
        self.soft = []
        self.gi = 0
        self.seg = 0
        self.cost = None
        self.fence = False


import os
STOP = int(os.environ.get("KSTOP", "9"))


class Prog:
    def __init__(self, nc, stack):
        self.nc = nc
        self.stack = stack
        self.ops = {e: [] for e in ENG}
        self.esem = {e: stack.enter_context(nc.semaphore("es_" + e)) for e in ENG}
        self.dtiles = []
        self.nsem = 0
        self.free_slots = []
        self.free_slots_sw = []
        self.all_slots = []
        self.order = []
        self.seg = 0
        self.last_pe_w = {}

    def get_slot(self, sw=False):
        pool = self.free_slots_sw if sw else self.free_slots
        if pool:
            return pool.pop()
        sl = Slot(self.stack.enter_context(self.nc.semaphore("ds%d" % self.nsem)))
        sl.sw = sw
        self.nsem += 1
        self.all_slots.append(sl)
        return sl

    def _tok(self, p, o):
        if p is None or p is o:
            return
        if p.is_dma:
            o.deps.append(("d", p.tile, p.tile.count))
            o.soft.append(p)
        else:
            if p.eng == o.eng and p.eng == "tensor":
                o.soft.append(p)
                return
            p.needed = True
            o.deps.append(("c", p))

    def op(self, eng, fn, r=(), w=(), dma=None, cost=None):
        o = Op(eng, fn)
        o.gi = len(self.order)
        o.seg = self.seg
        o.cost = cost
        self.order.append(o)
        for t in r:
            self._tok(t.lw, o)
        for t in w:
            self._tok(t.lw, o)
            for q in t.rd:
                self._tok(q, o)
        if dma is not None:
            o.is_dma = True
            attr = "dsem_sw" if eng == "gpsimd" else "dsem"
            if getattr(dma, attr) is None:
                setattr(dma, attr, self.get_slot(sw=(attr == "dsem_sw")))
                if dma not in self.dtiles:
                    self.dtiles.append(dma)
            sl_ = getattr(dma, attr)
            o.tile = sl_
            if getattr(sl_, "last", None) is not None:
                o.soft.append(sl_.last)
            sl_.last = o
            sl_.count += 1
        for t in r:
            t.rd.append(o)
        for t in w:
            t.lw = o
            t.rd = []
        self.ops[eng].append(o)
        return o

    def barrier(self, bar_src, bar_tile):
        o = Op("sync", lambda e: e.dma_start(out=bar_tile.ap, in_=bar_src))
        o.fence = True
        o.gi = len(self.order)
        o.seg = self.seg
        self.order.append(o)
        for e in ENG:
            if e == "sync":
                continue
            if self.ops[e]:
                last = None
                for q in reversed(self.ops[e]):
                    if q.fn is not None and not q.is_dma:
                        last = q
                        break
                if last is not None:
                    last.needed = True
                    o.deps.append(("c", last))
        for sl in self.all_slots:
            if sl.count:
                o.deps.append(("d", sl, sl.count))
        o.is_dma = True
        if bar_tile.dsem is None:
            bar_tile.dsem = self.get_slot()
            bar_tile.persist = True
        o.tile = bar_tile.dsem
        bar_tile.dsem.count += 1
        self.ops["sync"].append(o)
        for e in ENG:
            w = Op(e, None)
            w.fence = (e != "sync")
            w.gi = len(self.order)
            w.seg = self.seg
            self.order.append(w)
            w.deps.append(("d", bar_tile.dsem, bar_tile.dsem.count))
            self.ops[e].append(w)
        self.seg += 1
        for sl in self.all_slots:
            sl.last = None
        keep = []
        for t in self.dtiles:
            if getattr(t, "persist", False):
                keep.append(t)
            else:
                for attr in ("dsem", "dsem_sw"):
                    sl_ = getattr(t, attr)
                    pool_ = self.free_slots_sw if attr == "dsem_sw" else self.free_slots
                    if sl_ is not None and sl_ not in pool_:
                        pool_.append(sl_)
                    setattr(t, attr, None)
                t.lw = None
                t.rd = []
        self.dtiles = keep

    def emit(self, ename, eng):
        seen = {}
        for o in self.ops[ename]:
            for d in o.deps:
                if d[0] == "c":
                    p = d[1]
                    sem, val = self.esem[p.eng], p.incval
                else:
                    sem, val = d[1].sem, 16 * d[2]
                key = id(sem)
                if seen.get(key, 0) >= val:
                    continue
                seen[key] = val
                eng.wait_ge(sem, val)
            if o.fn is None:
                continue
            ins = o.fn(eng)
            if o.is_dma:
                ins.then_inc(o.tile.sem, 16)
            elif o.needed:
                ins.then_inc(self.esem[ename], 1)

    def reschedule(self, window=48):
        COST = {"tensor": 0.25, "scalar": 0.6, "vector": 0.55, "gpsimd": 0.8}
        for e in ENG:
            ops = self.ops[e]
            out = []
            i = 0
            while i < len(ops):
                j = i
                while j < len(ops) and not ops[j].fence:
                    j += 1
                out.append((ops[i:j], ops[j] if j < len(ops) else None))
                i = j + 1
            self._segs = getattr(self, "_segs", {})
            self._segs[e] = out
        nseg = max(len(v) for v in self._segs.values())
        fin = {}
        for sidx in range(nseg):
            qs = {}
            for e in ENG:
                if sidx < len(self._segs[e]):
                    qs[e] = list(self._segs[e][sidx][0])
                else:
                    qs[e] = []
            inseg = set()
            for e in ENG:
                for o in qs[e]:
                    inseg.add(id(o))
            t_eng = {e: 0.0 for e in ENG}
            newq = {e: [] for e in ENG}
            remaining = sum(len(q) for q in qs.values())
            heads = {e: 0 for e in ENG}
            done = set()

            def preds(o):
                r = []
                for d in o.deps:
                    if d[0] == "c":
                        r.append(d[1])
                for p in o.soft:
                    r.append(p)
                return r

            while remaining:
                best = None
                for e in ENG:
                    q = qs[e]
                    cnt = 0
                    for k in range(len(q)):
                        o = q[k]
                        if o is None:
                            continue
                        cnt += 1
                        if cnt > window:
                            break
                        ok = True
                        rt = 0.0
                        for p in preds(o):
                            if id(p) in inseg:
                                if id(p) not in done:
                                    ok = False
                                    break
                                rt = max(rt, fin[id(p)])
                        if not ok:
                            continue
                        st = max(rt, t_eng[e])
                        key = (st, o.gi)
                        if best is None or key < best[0]:
                            best = (key, e, k, o, st)
                        if rt <= t_eng[e]:
                            break
                assert best is not None, "scheduler deadlock"
                _, e, k, o, st = best
                qs[e][k] = None
                while qs[e] and qs[e][0] is None:
                    qs[e].pop(0)
                if o.is_dma:
                    dur = o.cost if o.cost is not None else 3.0
                    t_eng[e] = st + 0.15
                else:
                    dur = o.cost if o.cost is not None else COST.get(e, 0.5)
                    t_eng[e] = st + dur
                fin[id(o)] = st + dur
                done.add(id(o))
                newq[e].append(o)
                remaining -= 1
            for e in ENG:
                if sidx < len(self._segs[e]):
                    self._segs[e][sidx] = (newq[e], self._segs[e][sidx][1])
            sf = self._segs["sync"][sidx][1] if sidx < len(self._segs["sync"]) else None
            if sf is not None:
                sf.deps = [d for d in sf.deps if d[0] != "c"]
                for e in ENG:
                    if e == "sync":
                        continue
                    last = None
                    for ss in range(sidx, -1, -1):
                        if ss < len(self._segs[e]):
                            for q in reversed(self._segs[e][ss][0]):
                                if q.fn is not None and not q.is_dma:
                                    last = q
                                    break
                        if last is not None:
                            break
                    if last is not None:
                        last.needed = True
                        sf.deps.append(("c", last))
        for e in ENG:
            flat = []
            for (lst, fence) in self._segs[e]:
                flat.extend(lst)
                if fence is not None:
                    flat.append(fence)
            self.ops[e] = flat

    def finalize(self):
        for e in ENG:
            c = 0
            for o in self.ops[e]:
                if o.needed and not o.is_dma:
                    c += 1
                    o.incval = c


class Arena:
    def __init__(self, ap32, nwords):
        self.ap = ap32
        self.n = nwords
        self.off = 0
        self.k = 0

    def reset(self):
        self.off = 0

    def alloc(self, shape, dt, name=None):
        free = 1
        for s in shape[1:]:
            free *= s
        words = free if dt in (F32, mybir.dt.int32) else (free + 1) // 2
        words = (words + 7) // 8 * 8
        assert self.off + words <= self.n, "arena overflow %d+%d>%d (%s)" % (self.off, words, self.n, name)
        v = self.ap[0:shape[0], self.off:self.off + words]
        self.off += words
        if dt != F32:
            v = v.bitcast(dt)
        v = v[:, 0:free]
        if len(shape) == 3:
            v = v.rearrange("p (a b) -> p a b", a=shape[1])
        self.k += 1
        return T(v, name or ("t%d" % self.k))


def build(LP, n_samp=2, LS=16, LC=2048):
    nc = bass.Bass("TRN2", target_bir_lowering=False)
    seqs = [dict(L=LP, Lc=0)] + [dict(L=LS, Lc=LC) for _ in range(n_samp)]
    NS = len(seqs)

    def din(name, shape):
        return nc.dram_tensor(name, list(shape), F32, kind="ExternalInput").ap()

    def dout(name, shape):
        return nc.dram_tensor(name, list(shape), F32, kind="ExternalOutput").ap()

    def dscr(name, shape, dt=F32):
        return nc.dram_tensor(name, list(shape), dt, kind="Internal").ap()

    w_in = din("w_in", [D, INC])
    w_out = din("w_out", [D, D])
    w_up = din("w_up", [D, 2 * DFF])
    w_down = din("w_down", [DFF, D])
    g1 = din("g1", [128, 8])
    g2 = din("g2", [128, 8])
    gF = din("gF", [128, 8])
    gS = din("gS", [128, 4])
    cw = din("cw", [128, 8, 4])
    cb = din("cb", [128, 8])
    fw = din("fw", [128, 44, 3])
    fb = din("fb", [128, 44])
    dtb = din("dtb", [128, 8])
    alog = din("alog", [128, 8])
    dsk = din("dsk", [128, 8])
    fbias = din("fbias", [128, 8])
    cst = din("cst", [128, 6, 128])
    cvec = din("cvec", [128, 16])
    rmask = din("rmask", [128, 512])
    bar_src = din("bar_src", [1, 16])
    NQ = 4
    qidx = nc.dram_tensor("qidx", [128, 8], mybir.dt.int32, kind="ExternalInput").ap()
    S = []
    for i, sq in enumerate(seqs):
        L, Lc = sq["L"], sq["Lc"]
        TK = Lc + L
        d = dict(L=L, Lc=Lc, TK=TK)
        d["xT"] = din("xT_%d" % i, [D, L])
        d["st0"] = din("st0_%d" % i, [NH, 128, 64])
        d["cprev"] = din("cprev_%d" % i, [D, 3])
        d["fprev"] = din("fprev_%d" % i, [128, 44, 2])
        if Lc:
            d["cKT"] = din("cKT_%d" % i, [NH, 64, Lc])
            d["cV"] = din("cV_%d" % i, [NH, Lc, 64])
            d["cLF"] = din("cLF_%d" % i, [NH, Lc])
        d["yT"] = dout("yT_%d" % i, [D, (L // NQ) if i == 0 else L])
        d["kT"] = dout("kT_%d" % i, [512, L])
        d["vT"] = dout("vT_%d" % i, [512, L])
        d["lf"] = dout("lf_%d" % i, [NH, L])
        d["sst"] = dout("sst_%d" % i, [NH, 128, 64])
        d["scv"] = dout("scv_%d" % i, [D, 3])
        d["fcv"] = dout("fcv_%d" % i, [128, 44, 2])
        d["U"] = dscr("U_%d" % i, [INC, 3 + L])
        d["XA"] = dscr("XA_%d" % i, [D, L], BF16)
        d["ZS"] = dscr("ZS_%d" % i, [512, L], BF16)
        if i == 0:
            d["Yq_t"] = nc.dram_tensor("Yq", [D, NQ * (2 + L // NQ)], BF16)
            d["Yq"] = d["Yq_t"].ap().rearrange("r (j w) -> r j w", j=NQ)
            d["Yqv"] = d["Yq_t"].ap().rearrange("r (j w) -> (r j) w", j=NQ)
            d["xw"] = din("xw", [D, 2 + L // NQ])
            CHd = L // NQ
            d["QQ_t"] = nc.dram_tensor("QQ", [512, NQ * (2 + CHd)], BF16)
            d["KQ_t"] = nc.dram_tensor("KQ", [512, (NQ + 1) * CHd], BF16)
            d["VQ_t"] = nc.dram_tensor("VQ", [512, (NQ + 1) * CHd], BF16)
            d["FAQ_t"] = nc.dram_tensor("FAQ", [NH * 3, NQ * (2 + CHd)], BF16)
            d["FAK_t"] = nc.dram_tensor("FAK", [NH * 3, (NQ + 1) * CHd], BF16)
            d["YF"] = dscr("YF", [512, 2 + CHd], BF16)
            d["YS"] = dscr("YS", [512, 2 + CHd], BF16)
            d["XAq_t"] = nc.dram_tensor("XAq", [D, (NQ + 1) * CHd], BF16)
            d["ZSq_t"] = nc.dram_tensor("ZSq", [512, (NQ + 1) * CHd], BF16)
            d["DTq_t"] = nc.dram_tensor("DTq", [NH, (NQ + 1) * CHd], F32)
            d["dpad"] = din("dpad", [NH, CHd])
            d["sidx"] = nc.dram_tensor("sidx", [128, NH * 20], mybir.dt.int32, kind="ExternalInput").ap()
            d["zsrc"] = din("zsrc", [128, CHd])
            d["fpad"] = din("fpad", [NH * 3, CHd])
            d["gidx"] = nc.dram_tensor("gidx", [128, NH * 10], mybir.dt.int32, kind="ExternalInput").ap()
        else:
            d["Y"] = dscr("Y_%d" % i, [D, L], BF16)
        d["LFA"] = dscr("LFA_%d" % i, [NH, TK])
        d["FA"] = dscr("FA_%d" % i, [NH, 6, TK], BF16)
        S.append(d)
    wob = dscr("wob", [8, 128, 8, 128], BF16)
    wdb = dscr("wdb", [8, 128, 22, 128], BF16)
    wub = dscr("wub", [44, 128, 8, 128], BF16)
    wor = dscr("wor", [D, D], BF16)
    wdr = dscr("wdr", [DFF, D], BF16)
    wur = dscr("wur", [D, 2 * DFF], BF16)
    bar_d = dscr("bar_d", [1, 16])

    def ydst(d, r0, r1, t0, n):
        if "Yq" not in d:
            return [(d["Y"][r0:r1, t0:t0 + n], 0, n)]
        L_ = d["L"]
        CHq = L_ // NQ
        res_ = []
        t = t0
        while t < t0 + n:
            e = min(t0 + n, (t // CHq + 1) * CHq)
            j = t // CHq
            res_.append((d["Yq"][r0:r1, j, 2 + t % CHq:2 + t % CHq + (e - t)], t - t0, e - t0))
            if e % CHq == 0 and e < L_:
                res_.append((d["Yq"][r0:r1, j + 1, 0:2], e - 2 - t0, e - t0))
            t = e
        return res_

    stack = ExitStack()
    with stack:
        P = Prog(nc, stack)
        NW = 47000
        arena_t = stack.enter_context(nc.sbuf_tensor("arena", [128, NW], F32))
        A = Arena(arena_t, NW)
        psf = [T(stack.enter_context(nc.psum_tensor("psf%d" % i, [128, 512], F32)), "psf%d" % i) for i in range(6)]
        psb = [T(stack.enter_context(nc.psum_tensor("psb%d" % i, [128, 1024], BF16)), "psb%d" % i) for i in range(2)]
        bar_t = T(bar_d, "bar")
        bar_t.persist = True
        cslot = P.get_slot()
        pctr = [0, 0]

        def PS():
            pctr[0] += 1
            return psf[pctr[0] % 4]

        acc_ctr = [0]

        def PACC():
            acc_ctr[0] += 1
            return psf[4 + acc_ctr[0] % 2]

        def PB():
            pctr[1] += 1
            return psb[pctr[1] % 2]

        rr = [0]

        def dq():
            return "sync"

        def load(dst, dst_ap, src_ap, q=None, extra_r=(), extra_w=()):
            cast = (dst_ap.dtype != src_ap.dtype)
            e = "gpsimd" if cast else (q or "sync")
            return P.op(e, lambda g, a=dst_ap, b=src_ap: g.dma_start(out=a, in_=b), r=extra_r, w=(dst,) + tuple(extra_w), dma=dst)

        def store(src, dst_ap, src_ap, q=None, extra_w=(), extra_r=()):
            cast = (dst_ap.dtype != src_ap.dtype)
            e = "gpsimd" if cast else (q or "sync")
            return P.op(e, lambda g, a=dst_ap, b=src_ap: g.dma_start(out=a, in_=b), r=(src,) + tuple(extra_r), w=tuple(extra_w), dma=src)

        def d2d(tile, dst_ap, src_ap, r=(), w=()):
            cast = (dst_ap.dtype != src_ap.dtype)
            e = "gpsimd" if cast else "sync"
            return P.op(e, lambda g, a=dst_ap, b=src_ap: g.dma_start(out=a, in_=b), r=r, w=w, dma=tile)

        def barrier():
            P.barrier(bar_src, bar_t)

        C_ident = A.alloc([128, 128], F32, "ident")
        C_mask = A.alloc([128, 128], F32, "maskf")
        C_low = A.alloc([128, 128], F32, "low")
        C_identb = A.alloc([128, 128], BF16, "identb")
        C_maskb = A.alloc([128, 128], BF16, "maskb")
        C_onesb = A.alloc([128, 128], BF16, "onesb")
        C_onesf = A.alloc([128, 128], F32, "onesf")
        C_vec = A.alloc([128, 16], F32, "cvec")
        C_rm = A.alloc([128, 512], F32, "rmask")
        C_g1 = A.alloc([128, 8], F32); C_g2 = A.alloc([128, 8], F32); C_gF = A.alloc([128, 8], F32)
        C_gS = A.alloc([128, 4], F32)
        C_cw = A.alloc([128, 8, 4], F32); C_cb = A.alloc([128, 8], F32)
        C_fw = A.alloc([128, 44, 3], F32); C_fb = A.alloc([128, 44], F32)
        C_dtb = A.alloc([128, 8], F32); C_al = A.alloc([128, 8], F32); C_dsk = A.alloc([128, 8], F32)
        C_fbs = A.alloc([128, 8], F32)
        C_qidx = A.alloc([128, 8], mybir.dt.int32, "qidx")
        C_zero = A.alloc([128, 2], BF16, "zero")
        C_aneg = A.alloc([128, 8], F32)
        C_nfb = A.alloc([128, 8], F32)
        for t_ in (C_ident, C_mask, C_low, C_onesf, C_vec, C_rm, C_g1, C_g2, C_gF, C_gS, C_cw, C_cb, C_fw, C_fb, C_dtb, C_al, C_dsk, C_fbs, C_qidx):
            t_.dsem = cslot
            t_.persist = True
        load(C_ident, C_ident.ap, cst[:, 0, :])
        load(C_mask, C_mask.ap, cst[:, 1, :])
        load(C_low, C_low.ap, cst[:, 2, :])
        load(C_onesf, C_onesf.ap, cst[:, 3, :])
        load(C_vec, C_vec.ap, cvec)
        load(C_rm, C_rm.ap, rmask)
        for t_, s_ in ((C_g1, g1), (C_g2, g2), (C_gF, gF), (C_gS, gS), (C_cw, cw), (C_cb, cb), (C_fw, fw), (C_fb, fb),
                       (C_dtb, dtb), (C_al, alog), (C_dsk, dsk), (C_fbs, fbias), (C_qidx, qidx)):
            load(t_, t_.ap, s_)
        P.op("vector", lambda e: e.memset(C_zero.ap, 0.0), w=(C_zero,))
        for kt_ in range(8):
            store(C_zero, S[0]["Yq"][kt_ * 128:(kt_ + 1) * 128, 0, 0:2], C_zero.ap)
        P.op("vector", lambda e: e.tensor_copy(out=C_identb.ap, in_=C_ident.ap), r=(C_ident,), w=(C_identb,))
        P.op("vector", lambda e: e.tensor_copy(out=C_maskb.ap, in_=C_mask.ap), r=(C_mask,), w=(C_maskb,))
        P.op("vector", lambda e: e.tensor_copy(out=C_onesb.ap, in_=C_onesf.ap), r=(C_onesf,), w=(C_onesb,))
        P.op("scalar", lambda e: e.activation(out=C_aneg.ap, in_=C_al.ap, func=AF.Exp), r=(C_al,), w=(C_aneg,))
        P.op("vector", lambda e: e.tensor_scalar(out=C_aneg.ap, in0=C_aneg.ap, scalar1=-1.0, scalar2=None, op0=ALU.mult), r=(C_aneg,), w=(C_aneg,))
        P.op("vector", lambda e: e.tensor_scalar(out=C_nfb.ap, in0=C_fbs.ap, scalar1=-1.0, scalar2=None, op0=ALU.mult), r=(C_fbs,), w=(C_nfb,))
        wrT = T(None, "wrT")
        wrT.persist = True
        d2d(wrT, wor, w_out, w=(wrT,))
        d2d(wrT, wdr, w_down, w=(wrT,))
        d2d(wrT, wur, w_up, w=(wrT,))
        for m in range(8):
            d2d(bar_t, wob[m], wor[:, m * 128:(m + 1) * 128].rearrange("(k p) c -> p k c", p=128), r=(wrT,))
            d2d(bar_t, wdb[m], wdr[:, m * 128:(m + 1) * 128].rearrange("(k p) c -> p k c", p=128), r=(wrT,))
        for m in range(44):
            d2d(bar_t, wub[m], wur[:, m * 128:(m + 1) * 128].rearrange("(k p) c -> p k c", p=128), r=(wrT,))
        for d in S:
            d2d(bar_t, d["U"][512:1536, 0:3], d["cprev"])
        base0 = A.off

        def vcol(j):
            return C_vec.ap[:, j:j + 1]

        Wb = A.alloc([128, 8, INC], BF16, "Wb")
        wfs = [A.alloc([128, 772], F32, "wf") for _ in range(2)]
        wfi = 0
        for kt in range(8):
            for c0 in range(0, INC, 772):
                wf = wfs[wfi % 2]
                wfi += 1
                load(wf, wf.ap, w_in[kt * 128:(kt + 1) * 128, c0:c0 + 772])
                P.op("vector", lambda e, a=Wb.ap[:, kt, c0:c0 + 772], b=wf.ap, s=C_g1.ap[:, kt:kt + 1]:
                     e.tensor_scalar(out=a, in0=b, scalar1=s, scalar2=None, op0=ALU.mult), r=(wf, C_g1), w=(Wb,))
        base1 = A.off
        mtiles = [(m0, min(128, INC - m0)) for m0 in range(0, INC, 128)]
        for d in S:
            L = d["L"]
            TT = min(512, L)
            nb = 2
            xb = [A.alloc([128, 8, TT], BF16, "xb") for _ in range(nb)]
            xq = [A.alloc([128, 8, TT], BF16, "xq") for _ in range(nb)]
            rs = [A.alloc([128, TT], F32, "rs") for _ in range(nb)]
            ev = [A.alloc([128, TT], F32, "ev") for _ in range(4)]
            for ti, t0 in enumerate(range(0, L, TT)):
                b = ti % nb
                load(xb[b], xb[b].ap, d["xT"][:, t0:t0 + TT].rearrange("(k p) t -> p k t", p=128))
                P.op("gpsimd", lambda e, a=xq[b].ap, x=xb[b].ap: e.tensor_tensor(out=a, in0=x, in1=x, op=ALU.mult), r=(xb[b],), w=(xq[b],))
                ps = PS()
                for kt in range(8):
                    P.op("tensor", lambda e, o=ps.ap[:, 0:TT], l=C_onesb.ap, r_=xq[b].ap[:, kt, :], k=kt:
                         e.matmul(o, lhsT=l, rhs=r_, start=(k == 0), stop=(k == 7)), r=(C_onesb, xq[b]), w=(ps,))
                P.op("vector", lambda e, a=rs[b].ap, p=ps.ap[:, 0:TT]: e.tensor_scalar(out=a, in0=p, scalar1=1.0 / D, scalar2=EPS, op0=ALU.mult, op1=ALU.add), r=(ps,), w=(rs[b],))
                P.op("scalar", lambda e, a=rs[b].ap: e.activation(out=a, in_=a, func=AF.Ln), r=(rs[b],), w=(rs[b],))
                P.op("scalar", lambda e, a=rs[b].ap: e.activation(out=a, in_=a, func=AF.Exp, scale=-0.5), r=(rs[b],), w=(rs[b],))
                for mi, (m0, mw) in enumerate(mtiles):
                    ps = PS()
                    for kt in range(8):
                        P.op("tensor", lambda e, o=ps.ap[0:mw, 0:TT], l=Wb.ap[:, kt, m0:m0 + mw], r_=xb[b].ap[:, kt, :], k=kt:
                             e.matmul(o, lhsT=l, rhs=r_, start=(k == 0), stop=(k == 7)), r=(Wb, xb[b]), w=(ps,))
                    et = ev[mi % 4]
                    P.op("vector", lambda e, a=et.ap[0:mw, :], p=ps.ap[0:mw, 0:TT], r_=rs[b].ap[0:mw, :]:
                         e.tensor_tensor(out=a, in0=p, in1=r_, op=ALU.mult), r=(ps, rs[b]), w=(et,))
                    store(et, d["U"][m0:m0 + mw, 3 + t0:3 + t0 + TT], et.ap[0:mw, :])
                    if 2056 <= m0 < 2568:
                        pass
        barrier()
        A.off = base0
        QOFF = 512 + 1024 + 8
        KOFF = QOFF + 512
        VOFF = KOFF + 512
        FOFF = VOFF + 512
        DTOFF = 1536
        d0 = S[0]
        CHq = d0["L"] // NQ
        QQ3 = d0["QQ_t"].ap().rearrange("r (j w) -> r j w", j=NQ)
        KQ3 = d0["KQ_t"].ap().rearrange("r (j w) -> r j w", j=NQ + 1)
        VQ3 = d0["VQ_t"].ap().rearrange("r (j w) -> r j w", j=NQ + 1)
        FAQ3 = d0["FAQ_t"].ap().rearrange("r (j w) -> r j w", j=NQ)
        FAK3 = d0["FAK_t"].ap().rearrange("r (j w) -> r j w", j=NQ + 1)
        for j in range(NQ):
            c0 = 3 + j * CHq
            d2d(bar_t, QQ3[:, j, 2:2 + CHq], d0["U"][QOFF:QOFF + 512, c0:c0 + CHq])
            if j > 0:
                d2d(bar_t, QQ3[:, j, 0:2], d0["U"][QOFF:QOFF + 512, c0 - 2:c0])
            d2d(bar_t, KQ3[:, j, :], d0["U"][KOFF:KOFF + 512, c0:c0 + CHq])
            d2d(bar_t, VQ3[:, j, :], d0["U"][VOFF:VOFF + 512, c0:c0 + CHq])
        for k4 in range(4):
            d2d(bar_t, KQ3[k4 * 128:(k4 + 1) * 128, NQ, :], d0["zsrc"])
            d2d(bar_t, VQ3[k4 * 128:(k4 + 1) * 128, NQ, :], d0["zsrc"])
            d2d(bar_t, QQ3[k4 * 128:(k4 + 1) * 128, 0, 0:2], d0["zsrc"][:, 0:2])
        d2d(bar_t, FAK3[:, NQ, :], d0["fpad"])
        XAq3 = d0["XAq_t"].ap().rearrange("r (j w) -> r j w", j=NQ + 1)
        ZSq3 = d0["ZSq_t"].ap().rearrange("r (j w) -> r j w", j=NQ + 1)
        DTq3 = d0["DTq_t"].ap().rearrange("r (j w) -> r j w", j=NQ + 1)
        for j in range(NQ):
            d2d(bar_t, DTq3[:, j, :], d0["U"][DTOFF:DTOFF + NH, 3 + j * CHq:3 + (j + 1) * CHq])
        for k8 in range(8):
            d2d(bar_t, XAq3[k8 * 128:(k8 + 1) * 128, NQ, :], d0["zsrc"])
        for k4 in range(4):
            d2d(bar_t, ZSq3[k4 * 128:(k4 + 1) * 128, NQ, :], d0["zsrc"])
        d2d(bar_t, DTq3[:, NQ, :], d0["dpad"])
        d2d(bar_t, FAQ3[:, 0, 0:2], d0["zsrc"][0:NH * 3, 0:2])
        for d in S:
            L = d["L"]
            d2d(bar_t, d["kT"], d["U"][KOFF:KOFF + 512, 3:3 + L])
            d2d(bar_t, d["vT"], d["U"][VOFF:VOFF + 512, 3:3 + L])
            d2d(bar_t, d["scv"], d["U"][512:1536, L:L + 3])

        for d in (S if STOP >= 2 else []):
            L = d["L"]
            TT = min(512, L)
            if "Yq" in d:
                TT = min(TT, L // NQ)
            NB2 = 4
            ut = [A.alloc([128, TT + 3], F32, "cu") for _ in range(NB2)]
            ca = [A.alloc([128, TT], F32, "ca") for _ in range(NB2)]
            co = [A.alloc([128, TT], BF16, "co") for _ in range(NB2)]
            units = [("c", ct, t0) for ct in range(8) for t0 in range(0, L, TT)] + [("z", zt, t0) for zt in range(4) for t0 in range(0, L, TT)]

            def p2_load(i):
                kind, ct, t0 = units[i]
                u = ut[i % NB2]
                if kind == "c":
                    r0 = 512 + ct * 128
                    load(u, u.ap, d["U"][r0:r0 + 128, t0:t0 + TT + 3])
                else:
                    load(u, u.ap[:, 0:TT], d["U"][ct * 128:(ct + 1) * 128, 3 + t0:3 + t0 + TT])

            PF = 2
            for i in range(min(PF, len(units))):
                p2_load(i)
            for i, (kind, ct, t0) in enumerate(units):
                u, a, o = ut[i % NB2], ca[i % NB2], co[i % NB2]
                if kind == "c":
                    P.op("scalar", lambda e, a_=a.ap, u_=u.ap[:, 3:3 + TT], s=C_cw.ap[:, ct, 3:4], b_=C_cb.ap[:, ct:ct + 1]:
                         e.activation(out=a_, in_=u_, func=AF.Identity, bias=b_, scale=s), r=(u, C_cw, C_cb), w=(a,))
                    for j in range(3):
                        P.op("vector", lambda e, a_=a.ap, u_=u.ap[:, j:j + TT], s=C_cw.ap[:, ct, j:j + 1]:
                             e.scalar_tensor_tensor(out=a_, in0=u_, scalar=s, in1=a_, op0=ALU.mult, op1=ALU.add), r=(u, C_cw, a), w=(a,))
                    P.op("scalar", lambda e, o_=o.ap, a_=a.ap: e.activation(out=o_, in_=a_, func=AF.Silu), r=(a,), w=(o,))
                else:
                    P.op("scalar", lambda e, o_=o.ap, a_=u.ap[:, 0:TT]: e.activation(out=o_, in_=a_, func=AF.Silu), r=(u,), w=(o,))
                if i + PF < len(units):
                    p2_load(i + PF)
                if "Yq" in d:
                    CH2 = L // NQ
                    assert TT <= CH2
                    dstw = (XAq3 if kind == "c" else ZSq3)[ct * 128:(ct + 1) * 128, t0 // CH2, t0 % CH2:t0 % CH2 + TT]
                    store(o, dstw, o.ap)
                elif kind == "c":
                    store(o, d["XA"][ct * 128:(ct + 1) * 128, t0:t0 + TT], o.ap)
                else:
                    store(o, d["ZS"][ct * 128:(ct + 1) * 128, t0:t0 + TT], o.ap)
        barrier()
        A.off = base0
        def phase3(d):
            L = d["L"]
            TT = min(512, L)
            if "Yq" in d:
                TT = min(TT, L // NQ)
            Q = min(128, L)
            NCK = TT // Q
            XX = [A.alloc([128, TT], BF16, "XX") for _ in range(2)]
            BT = [A.alloc([128, TT], BF16, "BT") for _ in range(2)]
            CT = [A.alloc([128, TT], BF16, "CT") for _ in range(2)]
            DR = [A.alloc([128, TT], F32, "DR") for _ in range(2)]
            ZS = [A.alloc([64, TT], BF16, "ZS") for _ in range(2)]
            DTt2 = [A.alloc([128, TT], F32, "DT") for _ in range(2)]
            Ab2 = [A.alloc([128, TT], F32, "Ab") for _ in range(2)]
            Eb2 = [A.alloc([128, TT], F32, "Eb") for _ in range(2)]
            CTs2 = [A.alloc([128, TT], BF16, "CTs") for _ in range(2)]
            XD2 = [A.alloc([128, TT], F32, "XD") for _ in range(2)]
            Wf2 = [A.alloc([128, TT], F32, "Wf") for _ in range(2)]
            XDb2 = [A.alloc([128, TT], BF16, "XDb") for _ in range(2)]
            AA2 = [A.alloc([64, TT], F32, "AA") for _ in range(2)]
            BB2 = [A.alloc([64, TT], F32, "BB") for _ in range(2)]
            XDtok = [A.alloc([128, 128], BF16, "XDtok") for _ in range(3)]
            Btok = [A.alloc([128, 128], BF16, "Btok") for _ in range(3)]
            LT = [A.alloc([128, 128], F32, "LT") for _ in range(3)]
            STt = [A.alloc([128, 128], BF16, "ST") for _ in range(3)]
            yv = [A.alloc([64, 128], F32, "yv") for _ in range(2)]
            YG = [A.alloc([64, TT], BF16, "YG") for _ in range(2)]
            Sf = A.alloc([128, 64], F32, "Sf")
            Sb = A.alloc([128, 64], BF16, "Sb")
            for Wf in Wf2:
                P.op("vector", lambda e, a=Wf.ap[0:64, :]: e.memset(a, 1.0), w=(Wf,))
            it = [0]

            quarter = "Yq" in d
            if quarter:
                CHq_ = L // NQ
                XS = [A.alloc([128, CHq_], BF16, "XS") for _ in range(2)]
                BS = [A.alloc([128, CHq_], BF16, "BS") for _ in range(2)]
                CS = [A.alloc([128, CHq_], BF16, "CS") for _ in range(2)]
                ZQ = [A.alloc([64, CHq_], BF16, "ZQ") for _ in range(2)]
                DS = [A.alloc([128, CHq_], F32, "DS") for _ in range(2)]
                Cs = A.alloc([128, NH * 20], mybir.dt.int32, "sidx")
                load(Cs, Cs.ap, d["sidx"])
                XAqv = d["XAq_t"].ap().rearrange("r (j w) -> (r j) w", j=NQ + 1)
                ZSqv = d["ZSq_t"].ap().rearrange("r (j w) -> (r j) w", j=NQ + 1)
                DTqv = d["DTq_t"].ap().rearrange("r (j w) -> (r j) w", j=NQ + 1)
                sbuf_of = {}
                sctr = [0]

                def gath3(tile, src_v, col, npart):
                    P.op("gpsimd", lambda g_, a=tile.ap[0:npart, :], s_=src_v, ix=Cs.ap[0:npart, col:col + 1]:
                         g_.indirect_dma_start(out=a, out_offset=None, in_=s_, in_offset=bass.IndirectOffsetOnAxis(ap=ix, axis=0)),
                         r=(Cs,), w=(tile,), dma=tile)

                def slot_bufs(h, slot):
                    if (h, slot) not in sbuf_of:
                        sb = sctr[0] % 2
                        sctr[0] += 1
                        cb_ = h * 20 + slot * 5
                        gath3(XS[sb], XAqv, cb_ + 0, 128)
                        gath3(BS[sb], XAqv, cb_ + 1, 128)
                        gath3(CS[sb], XAqv, cb_ + 2, 128)
                        gath3(ZQ[sb], ZSqv, cb_ + 3, 64)
                        gath3(DS[sb], DTqv, cb_ + 4, 128)
                        sbuf_of[(h, slot)] = sb
                    return sbuf_of[(h, slot)]

            def prep(h, t0):
                g = h // 4
                b = it[0] % 2
                it[0] += 1
                DTt, Ab, Eb, CTs, XD, Wf, XDb, AA, BB = DTt2[b], Ab2[b], Eb2[b], CTs2[b], XD2[b], Wf2[b], XDb2[b], AA2[b], BB2[b]
                if quarter:
                    slot, tl = t0 // CHq_, t0 % CHq_
                    sb = slot_bufs(h, slot)
                    xx = TV(XS[sb], XS[sb].ap[:, tl:tl + TT])
                    bt = TV(BS[sb], BS[sb].ap[:, tl:tl + TT])
                    ct_ = TV(CS[sb], CS[sb].ap[:, tl:tl + TT])
                    zs = TV(ZQ[sb], ZQ[sb].ap[:, tl:tl + TT])
                    dr = TV(DS[sb], DS[sb].ap[:, tl:tl + TT])
                else:
                    xx, bt, ct_, dr, zs = XX[b], BT[b], CT[b], DR[b], ZS[b]
                    load(xx, xx.ap[0:64, :], d["XA"][h * 64:(h + 1) * 64, t0:t0 + TT])
                    load(xx, xx.ap[64:128, :], d["XA"][h * 64:(h + 1) * 64, t0:t0 + TT])
                    load(bt, bt.ap, d["XA"][512 + g * 128:512 + (g + 1) * 128, t0:t0 + TT])
                    load(ct_, ct_.ap, d["XA"][768 + g * 128:768 + (g + 1) * 128, t0:t0 + TT])
                    load(dr, dr.ap, d["U"][DTOFF + h:DTOFF + h + 1, 3 + t0:3 + t0 + TT].partition_broadcast(128))
                    load(zs, zs.ap, d["ZS"][h * 64:(h + 1) * 64, t0:t0 + TT])
                P.op("scalar", lambda e, a=DTt.ap, i_=dr.ap, b_=C_dtb.ap[:, h:h + 1]: e.activation(out=a, in_=i_, func=AF.Exp, bias=b_), r=(dr, C_dtb), w=(DTt,))
                P.op("scalar", lambda e, a=DTt.ap: e.activation(out=a, in_=a, func=AF.Ln, bias=1.0), r=(DTt,), w=(DTt,))
                P.op("vector", lambda e, a=Eb.ap, i_=DTt.ap, s=C_aneg.ap[:, h:h + 1]: e.tensor_scalar(out=a, in0=i_, scalar1=s, scalar2=None, op0=ALU.mult), r=(DTt, C_aneg), w=(Eb,))
                moff = 0 if Q == 128 else 1
                P.op("vector", lambda e, a=Ab.ap, m=C_rm.ap[:, moff:moff + TT], x=Eb.ap: e.tensor_tensor_scan(out=a, data0=m, data1=x, initial=0.0, op0=ALU.mult, op1=ALU.add), r=(Eb, C_rm), w=(Ab,))
                P.op("scalar", lambda e, a=Eb.ap, i_=Ab.ap: e.activation(out=a, in_=i_, func=AF.Exp), r=(Ab,), w=(Eb,))
                P.op("gpsimd", lambda e, a=CTs.ap, x=ct_.ap, y=Eb.ap: e.tensor_tensor(out=a, in0=x, in1=y, op=ALU.mult), r=(ct_, Eb), w=(CTs,))
                P.op("vector", lambda e, a=XD.ap, x=xx.ap, y=DTt.ap: e.tensor_tensor(out=a, in0=x, in1=y, op=ALU.mult), r=(xx, DTt), w=(XD,))
                for c in range(NCK):
                    c0 = c * Q
                    P.op("scalar", lambda e, a=Wf.ap[64:128, c0:c0 + Q], i_=Ab.ap[64:128, c0:c0 + Q], b_=Ab.ap[64:128, c0 + Q - 1:c0 + Q]:
                         e.activation(out=a, in_=i_, func=AF.Exp, bias=b_, scale=-1.0), r=(Ab,), w=(Wf,))
                P.op("vector", lambda e, a=XDb.ap, x=XD.ap, y=Wf.ap: e.tensor_tensor(out=a, in0=x, in1=y, op=ALU.mult), r=(XD, Wf), w=(XDb,))
                if any(is_full(t0, c_) for c_ in range(NCK)):
                    P.op("vector", lambda e, a=AA.ap, i_=Ab.ap[0:64, :]: e.tensor_scalar(out=a, in0=i_, scalar1=C_vec.ap[0:64, 1:2], scalar2=C_vec.ap[0:64, 0:1], op0=ALU.mult, op1=ALU.add), r=(Ab, C_vec), w=(AA,))
                    P.op("vector", lambda e, a=BB.ap, i_=Ab.ap[0:64, :]: e.tensor_scalar(out=a, in0=i_, scalar1=C_vec.ap[0:64, 0:1], scalar2=C_vec.ap[0:64, 2:3], op0=ALU.mult, op1=ALU.add), r=(Ab, C_vec), w=(BB,))
                return dict(xx=xx, bt=bt, ct=ct_, zs=zs, Eb=Eb, CTs=CTs, XDb=XDb, AA=AA, BB=BB, yg=YG[b], t0=t0)

            def is_full(t0, c):
                if not quarter:
                    return True
                slot, tl = t0 // CHq_, t0 % CHq_
                return slot == NQ - 1 or (slot == NQ - 2 and tl + TT == CHq_ and c == NCK - 1)

            def stageA(k, tb, c, full=True):
                c0 = c * Q
                xdt_, btk, lt, st = XDtok[k % 3], Btok[k % 3], LT[k % 3], STt[k % 3]
                XDb, bt, ct_, AA, BB = tb["XDb"], tb["bt"], tb["ct"], tb["AA"], tb["BB"]
                pb = PB()
                P.op("tensor", lambda e, o=pb.ap[0:Q, 0:128], i_=XDb.ap[:, c0:c0 + Q]: e.transpose(o, i_, C_identb.ap), r=(XDb, C_identb), w=(pb,))
                P.op("vector", lambda e, a=xdt_.ap[0:Q, :], p=pb.ap[0:Q, 0:128]: e.tensor_copy(out=a, in_=p), r=(pb,), w=(xdt_,))
                pb2 = PB()
                P.op("tensor", lambda e, o=pb2.ap[0:Q, 0:128], i_=bt.ap[:, c0:c0 + Q]: e.transpose(o, i_, C_identb.ap), r=(bt, C_identb), w=(pb2,))
                P.op("scalar", lambda e, a=btk.ap[0:Q, :], p=pb2.ap[0:Q, 0:128]: e.copy(out=a, in_=p), r=(pb2,), w=(btk,))
                if not full:
                    return
                ps = PS()
                P.op("tensor", lambda e, o=ps.ap[0:Q, 0:Q], l=AA.ap[:, c0:c0 + Q], r_=BB.ap[:, c0:c0 + Q]: e.matmul(o, lhsT=l, rhs=r_, start=True, stop=False), r=(AA, BB), w=(ps,))
                P.op("tensor", lambda e, o=ps.ap[0:Q, 0:Q], l=C_ident.ap[0:Q, 0:Q], r_=C_mask.ap[0:Q, 0:Q]: e.matmul(o, lhsT=l, rhs=r_, start=False, stop=True), r=(C_ident, C_mask), w=(ps,))
                P.op("scalar", lambda e, a=lt.ap[0:Q, 0:Q], p=ps.ap[0:Q, 0:Q]: e.activation(out=a, in_=p, func=AF.Exp), r=(ps,), w=(lt,))
                ps2 = PS()
                P.op("tensor", lambda e, o=ps2.ap[0:Q, 0:Q], l=bt.ap[:, c0:c0 + Q], r_=ct_.ap[:, c0:c0 + Q]: e.matmul(o, lhsT=l, rhs=r_, start=True, stop=True), r=(bt, ct_), w=(ps2,))
                P.op("vector", lambda e, a=st.ap[0:Q, 0:Q], p=ps2.ap[0:Q, 0:Q], l=lt.ap[0:Q, 0:Q]: e.tensor_tensor(out=a, in0=p, in1=l, op=ALU.mult), r=(ps2, lt), w=(st,))

            def stageB(k, tb, c, h, full=True):
                c0 = c * Q
                xdt_, btk, st = XDtok[k % 3], Btok[k % 3], STt[k % 3]
                yv_ = yv[k % 2]
                xx, zs, Eb, CTs, yg = tb["xx"], tb["zs"], tb["Eb"], tb["CTs"], tb["yg"]
                if full:
                    ps3 = PACC()
                    P.op("tensor", lambda e, o=ps3.ap[0:64, 0:Q], l=xdt_.ap[0:Q, 0:64], r_=st.ap[0:Q, 0:Q]: e.matmul(o, lhsT=l, rhs=r_, start=True, stop=False), r=(xdt_, st), w=(ps3,))
                    P.op("tensor", lambda e, o=ps3.ap[0:64, 0:Q], l=Sb.ap, r_=CTs.ap[:, c0:c0 + Q]: e.matmul(o, lhsT=l, rhs=r_, start=False, stop=True), r=(Sb, CTs), w=(ps3,))
                    P.op("vector", lambda e, a=yv_.ap[:, 0:Q], x=xx.ap[0:64, c0:c0 + Q], s=C_dsk.ap[0:64, h:h + 1], p=ps3.ap[0:64, 0:Q]:
                         e.scalar_tensor_tensor(out=a, in0=x, scalar=s, in1=p, op0=ALU.mult, op1=ALU.add), r=(xx, C_dsk, ps3), w=(yv_,))
                    P.op("gpsimd", lambda e, a=yg.ap[:, c0:c0 + Q], x=yv_.ap[:, 0:Q], z=zs.ap[:, c0:c0 + Q]: e.tensor_tensor(out=a, in0=x, in1=z, op=ALU.mult), r=(yv_, zs), w=(yg,))
                ps4 = PACC()
                P.op("tensor", lambda e, o=ps4.ap[:, 0:64], l=btk.ap[0:Q, :], r_=xdt_.ap[0:Q, 64:128]: e.matmul(o, lhsT=l, rhs=r_, start=True, stop=True), r=(btk, xdt_), w=(ps4,))
                P.op("vector", lambda e, a=Sf.ap, s=Eb.ap[:, c0 + Q - 1:c0 + Q], p=ps4.ap[:, 0:64]:
                     e.scalar_tensor_tensor(out=a, in0=a, scalar=s, in1=p, op0=ALU.mult, op1=ALU.add), r=(Sf, Eb, ps4), w=(Sf,))
                P.op("scalar", lambda e: e.copy(out=Sb.ap, in_=Sf.ap), r=(Sf,), w=(Sb,))
                if c == NCK - 1 and quarter:
                    slot, tl = tb["t0"] // CHq_, tb["t0"] % CHq_
                    if slot == NQ - 1:
                        store(yg, d["YS"][h * 64:(h + 1) * 64, 2 + tl:2 + tl + TT], yg.ap)
                    elif full:
                        store(yg, d["YS"][h * 64:(h + 1) * 64, 0:2], yg.ap[:, TT - 2:TT])
                elif c == NCK - 1:
                    for (dst_, c0_, c1_) in ydst(d, h * 64, (h + 1) * 64, tb["t0"], TT):
                        store(yg, dst_, yg.ap[:, c0_:c1_])

            chunks = [(h, t0, c) for h in range(NH) for t0 in range(0, L, TT) for c in range(NCK)]
            tiles_ = [(h, t0) for h in range(NH) for t0 in range(0, L, TT)]
            tidx = {t: i for i, t in enumerate(tiles_)}
            tbs = {}
            tbs[tiles_[0]] = prep(*tiles_[0])
            nxt = 1
            doneB = -1
            for i in range(len(chunks) + 1):
                curA = -1
                if i < len(chunks):
                    h, t0, c = chunks[i]
                    curA = tidx[(h, t0)]
                    stageA(i, tbs[(h, t0)], c, is_full(t0, c))
                if i >= 1:
                    h, t0, c = chunks[i - 1]
                    if t0 == 0 and c == 0:
                        load(Sf, Sf.ap, d["st0"][h])
                        P.op("vector", lambda e: e.tensor_copy(out=Sb.ap, in_=Sf.ap), r=(Sf,), w=(Sb,))
                    stageB(i - 1, tbs[(h, t0)], c, h, is_full(t0, c))
                    if c == NCK - 1:
                        doneB = tidx[(h, t0)]
                    if t0 + TT >= L and c == NCK - 1:
                        store(Sf, d["sst"][h], Sf.ap)
                while nxt < len(tiles_) and nxt - 2 <= doneB and nxt <= curA + 1:
                    tbs[tiles_[nxt]] = prep(*tiles_[nxt])
                    nxt += 1
        for d in (S if STOP >= 3 else []):
            phase3(d)
        barrier()
        A.off = base0

        def phase4(si, d):
            L, Lc, TK = d["L"], d["Lc"], d["TK"]
            base = A.off
            PL = min(128, L)
            JL = L // PL
            NKB = (TK + 127) // 128
            JK = NKB
            TKP = NKB * 128
            QT = min(512, L)
            fr = A.alloc([128, JL], F32, "fr")
            lfa = A.alloc([128, JK], F32, "lfa")
            Fc = A.alloc([128, JK], F32, "Fc")
            onesJ = A.alloc([128, JK], F32, "onesJ")
            hb = A.alloc([128, JK], BF16, "hb"); hf = A.alloc([128, JK], F32, "hf")
            r1 = A.alloc([128, JK], F32, "r1"); mb = A.alloc([128, JK], BF16, "mb"); mf = A.alloc([128, JK], F32, "mf")
            r2 = A.alloc([128, JK], F32, "r2"); lb = A.alloc([128, JK], BF16, "lb")
            sc6 = [A.alloc([128, JK], BF16, "sc6") for _ in range(6)]
            quarter = "Yq" in d
            CHq = L // NQ
            QW = (2 + CHq) if quarter else L
            Qa = A.alloc([70, QW], BF16, "Qa")
            Ka = A.alloc([70, TKP], BF16, "Ka")
            if quarter:
                G6 = A.alloc([3, QW], BF16, "G6")
                GK = A.alloc([3, TKP], BF16, "GK")
                Cg = A.alloc([128, NH * 10], mybir.dt.int32, "gidx")
                load(Cg, Cg.ap, d["gidx"])
                QQv = d["QQ_t"].ap().rearrange("r (j w) -> (r j) w", j=NQ)
                KQv = d["KQ_t"].ap().rearrange("r (j w) -> (r j) w", j=NQ + 1)
                VQv = d["VQ_t"].ap().rearrange("r (j w) -> (r j) w", j=NQ + 1)
                FAQv = d["FAQ_t"].ap().rearrange("r (j w) -> (r j) w", j=NQ)
                FAKv = d["FAK_t"].ap().rearrange("r (j w) -> (r j) w", j=NQ + 1)
                FAQ3 = d["FAQ_t"].ap().rearrange("r (j w) -> r j w", j=NQ)
                FAKf = d["FAK_t"].ap()

                def gath(tile, out_ap, src_v, col, npart, extra_r=()):
                    P.op("gpsimd", lambda g, a=out_ap, s_=src_v, ix=Cg.ap[0:npart, col:col + 1]:
                         g.indirect_dma_start(out=a, out_offset=None, in_=s_, in_offset=bass.IndirectOffsetOnAxis(ap=ix, axis=0)),
                         r=(Cg,) + tuple(extra_r), w=(tile,), dma=tile)
            Va = A.alloc([128, NKB, 65], BF16, "Va")
            vT = A.alloc([64, L], BF16, "vT")
            PT = [A.alloc([128, QT], BF16, "PT") for _ in range(3)]
            Lrow = A.alloc([65, QT], F32, "Lrow")
            rcp = A.alloc([64, QT], F32, "rcp")
            Yo = [A.alloc([64, QT], BF16, "Yo") for _ in range(2)]
            lfd = T(d["LFA"], "LFA%d" % si)
            fad = T(d["FA"], "FA%d" % si)
            P.op("vector", lambda e: e.memset(onesJ.ap, 1.0), w=(onesJ,))
            for h in range(NH):
                load(fr, fr.ap[0:PL, :], d["U"][FOFF + h, 3:3 + L].rearrange("(p j) -> p j", p=PL))
                P.op("scalar", lambda e, a=fr.ap[0:PL, :], b_=C_nfb.ap[0:PL, h:h + 1]: e.activation(out=a, in_=a, func=AF.Exp, bias=b_, scale=-1.0), r=(fr, C_nfb), w=(fr,))
                P.op("scalar", lambda e, a=fr.ap[0:PL, :]: e.activation(out=a, in_=a, func=AF.Ln, bias=1.0), r=(fr,), w=(fr,))
                P.op("vector", lambda e, a=fr.ap[0:PL, :]: e.tensor_scalar(out=a, in0=a, scalar1=-1.0, scalar2=None, op0=ALU.mult), r=(fr,), w=(fr,))
                store(fr, d["lf"][h, :].rearrange("(p j) -> p j", p=PL), fr.ap[0:PL, :])
                store(fr, d["LFA"][h, Lc:Lc + L].rearrange("(p j) -> p j", p=PL), fr.ap[0:PL, :], extra_w=(lfd,))
                if Lc:
                    d2d(lfd, d["LFA"][h, 0:Lc], d["cLF"][h, :], w=(lfd,))
                P.op("vector", lambda e: e.memset(lfa.ap, 0.0), w=(lfa,))
                full = TK // JK
                rem = TK - full * JK
                load(lfa, lfa.ap[0:full, :], d["LFA"][h, 0:full * JK].rearrange("(p j) -> p j", j=JK), extra_r=(lfd,))
                if rem:
                    load(lfa, lfa.ap[full:full + 1, 0:rem], d["LFA"][h:h + 1, full * JK:TK], extra_r=(lfd,))
                P.op("vector", lambda e: e.tensor_tensor_scan(out=Fc.ap, data0=onesJ.ap, data1=lfa.ap, initial=0.0, op0=ALU.mult, op1=ALU.add), r=(lfa, onesJ), w=(Fc,))
                ps = PS()
                P.op("tensor", lambda e, o=ps.ap[:, 0:2], r_=Fc.ap[:, JK - 1:JK]: e.matmul(o[:, 0:1], lhsT=C_low.ap, rhs=r_, start=True, stop=True), r=(C_low, Fc), w=(ps,))
                P.op("vector", lambda e, p=ps.ap[:, 0:1]: e.tensor_scalar(out=Fc.ap, in0=Fc.ap, scalar1=p, scalar2=None, op0=ALU.add), r=(ps, Fc), w=(Fc,))
                P.op("vector", lambda e: e.tensor_copy(out=hb.ap, in_=Fc.ap), r=(Fc,), w=(hb,))
                P.op("vector", lambda e: e.tensor_copy(out=hf.ap, in_=hb.ap), r=(hb,), w=(hf,))
                P.op("vector", lambda e: e.tensor_tensor(out=r1.ap, in0=Fc.ap, in1=hf.ap, op=ALU.subtract), r=(Fc, hf), w=(r1,))
                P.op("vector", lambda e: e.tensor_copy(out=mb.ap, in_=r1.ap), r=(r1,), w=(mb,))
                P.op("vector", lambda e: e.tensor_copy(out=mf.ap, in_=mb.ap), r=(mb,), w=(mf,))
                P.op("vector", lambda e: e.tensor_tensor(out=r2.ap, in0=r1.ap, in1=mf.ap, op=ALU.subtract), r=(r1, mf), w=(r2,))
                P.op("vector", lambda e: e.tensor_copy(out=lb.ap, in_=r2.ap), r=(r2,), w=(lb,))
                for i6, (src, sgn) in enumerate(((hb, 8.0), (mb, 8.0), (lb, 8.0), (hb, -8.0), (mb, -8.0), (lb, -8.0))):
                    P.op("vector", lambda e, a=sc6[i6].ap, s_=src.ap, v=sgn: e.tensor_scalar(out=a, in0=s_, scalar1=v, scalar2=None, op0=ALU.mult), r=(src,), w=(sc6[i6],))
                    if quarter:
                        PQ = 128 // NQ
                        if i6 < 3:
                            for jq in range(NQ):
                                store(sc6[i6], FAQ3[h * 3 + i6, jq, 2:2 + CHq].rearrange("(p j) -> p j", j=JK), sc6[i6].ap[jq * PQ:(jq + 1) * PQ, :], extra_w=(fad,))
                                if jq > 0:
                                    store(sc6[i6], FAQ3[h * 3 + i6:h * 3 + i6 + 1, jq, 0:2], sc6[i6].ap[jq * PQ - 1:jq * PQ, JK - 2:JK], extra_w=(fad,))
                        else:
                            store(sc6[i6], FAKf[h * 3 + i6 - 3, 0:TK].rearrange("(p j) -> p j", j=JK), sc6[i6].ap[0:full, :], extra_w=(fad,))
                        continue
                    store(sc6[i6], d["FA"][h, i6, 0:full * JK].rearrange("(p j) -> p j", j=JK), sc6[i6].ap[0:full, :], extra_w=(fad,))
                    if rem:
                        store(sc6[i6], d["FA"][h, i6:i6 + 1, full * JK:TK], sc6[i6].ap[full:full + 1, 0:rem], extra_w=(fad,))
                P.op("vector", lambda e: e.memset(Qa.ap[64:70, :], 1.0), w=(Qa,))
                P.op("vector", lambda e: e.memset(Ka.ap[64:70, :], 1.0), w=(Ka,))
                if quarter:
                    gb = h * 10
                    gath(Qa, Qa.ap[0:64, :], QQv, gb + 0, 64)
                    gath(G6, G6.ap[0:3, :], FAQv, gb + 5, 3, extra_r=(fad,))
                    P.op("sync", lambda g: g.dma_start(out=Qa.ap[64:67, :], in_=G6.ap[0:3, :]), r=(G6,), w=(Qa,), dma=Qa)
                    for sq_ in range(NQ):
                        gath(Ka, Ka.ap[0:64, sq_ * CHq:(sq_ + 1) * CHq], KQv, gb + 1 + sq_, 64)
                        gath(vT, vT.ap[0:64, sq_ * CHq:(sq_ + 1) * CHq], VQv, gb + 1 + sq_, 64)
                        gath(GK, GK.ap[0:3, sq_ * CHq:(sq_ + 1) * CHq], FAKv, gb + 6 + sq_, 3, extra_r=(fad,))
                    P.op("sync", lambda g: g.dma_start(out=Ka.ap[67:70, 0:TK], in_=GK.ap[0:3, 0:TK]), r=(GK,), w=(Ka,), dma=Ka)
                    P.op("vector", lambda e: e.memset(Va.ap[:, :, 64:65], 1.0), w=(Va,))
                else:
                    load(Qa, Qa.ap[0:64, :], d["U"][QOFF + h * 64:QOFF + (h + 1) * 64, 3:3 + L])
                    load(Qa, Qa.ap[64:67, :], d["FA"][h, 0:3, Lc:Lc + L], extra_r=(fad,))
                    if Lc:
                        load(Ka, Ka.ap[0:64, 0:Lc], d["cKT"][h])
                    load(Ka, Ka.ap[0:64, Lc:TK], d["U"][KOFF + h * 64:KOFF + (h + 1) * 64, 3:3 + L])
                    load(Ka, Ka.ap[67:70, 0:TK], d["FA"][h, 3:6, 0:TK], extra_r=(fad,))
                    P.op("vector", lambda e: e.memset(Va.ap[:, :, 64:65], 1.0), w=(Va,))
                    if Lc:
                        load(Va, Va.ap[:, 0:Lc // 128, 0:64], d["cV"][h].rearrange("(b p) d -> p b d", p=128))
                    load(vT, vT.ap, d["U"][VOFF + h * 64:VOFF + (h + 1) * 64, 3:3 + L])
                for kb in range(Lc // 128, NKB):
                    k0 = kb * 128 - Lc
                    kw = min(128, L - k0)
                    pb = PB()
                    P.op("tensor", lambda e, o=pb.ap[0:kw, 0:64], i_=vT.ap[:, k0:k0 + kw]: e.transpose(o, i_, C_identb.ap[0:64, 0:64]), r=(vT, C_identb), w=(pb,))
                    P.op("vector", lambda e, a=Va.ap[0:kw, kb, 0:64], p=pb.ap[0:kw, 0:64]: e.tensor_copy(out=a, in_=p), r=(pb,), w=(Va,))
                tasks = []
                if quarter:
                    QTq = min(512, CHq)
                    qtiles = [(0, 2, 3 * CHq - 2)] + [(2 + q_, QTq, 3 * CHq + q_) for q_ in range(0, CHq, QTq)]
                else:
                    qtiles = [(q_, QT, Lc + q_) for q_ in range(0, L, QT)]
                for qi, (q0, wq, qa0) in enumerate(qtiles):
                    last_kb = (qa0 + wq - 1) // 128
                    for kb in range(0, last_kb + 1):
                        kpos = kb * 128
                        kw = min(128, TK - kpos)
                        o_ = max(0, kpos - qa0)
                        tasks.append(dict(qi=qi, q0=q0, w=wq, kb=kb, kpos=kpos, kw=kw, o=o_, nq=wq - o_, moff=qa0 + o_ - kpos,
                                          diag=(kpos + kw - 1 > qa0 + o_), first=(kb == 0), last=(kb == last_kb)))
                LA = 2
                pos = {}
                for i in range(len(tasks) + LA):
                    if i < len(tasks):
                        t = tasks[i]
                        ps = PS()
                        t["ps"] = ps
                        kw, o_, kpos, q0, wq, mo = t["kw"], t["o"], t["kpos"], t["q0"], t["w"], t["moff"]
                        P.op("tensor", lambda e, o=ps.ap[0:kw, o_:wq], l=Ka.ap[:, kpos:kpos + kw], r_=Qa.ap[:, q0 + o_:q0 + wq], dg=t["diag"]:
                             e.matmul(o, lhsT=l, rhs=r_, start=True, stop=(not dg)), r=(Ka, Qa), w=(ps,))
                        if t["diag"]:
                            mw_ = min(kw - mo, t["nq"])
                            P.op("tensor", lambda e, o=ps.ap[0:kw, o_:o_ + mw_], l=C_identb.ap[0:kw, 0:kw], r_=C_maskb.ap[0:kw, mo:mo + mw_]:
                                 e.matmul(o, lhsT=l, rhs=r_, start=False, stop=True), r=(C_identb, C_maskb), w=(ps,))
                    j = i - LA
                    if j >= 0:
                        t = tasks[j]
                        ps = t["ps"]
                        kw, o_, kb, q0, qi, wq = t["kw"], t["o"], t["kb"], t["q0"], t["qi"], t["w"]
                        if t["first"]:
                            pos[qi] = PACC()
                        po = pos[qi]
                        pt = PT[j % 3]
                        P.op("scalar", lambda e, a=pt.ap[0:kw, o_:wq], p=ps.ap[0:kw, o_:wq]: e.activation(out=a, in_=p, func=AF.Exp, scale=0.125), r=(ps,), w=(pt,))
                        P.op("tensor", lambda e, o=po.ap[0:65, o_:wq], l=Va.ap[0:kw, kb, :], r_=pt.ap[0:kw, o_:wq], f=t["first"], la=t["last"]:
                             e.matmul(o, lhsT=l, rhs=r_, start=f, stop=la), r=(Va, pt), w=(po,))
                        if t["last"]:
                            P.op("scalar", lambda e, a=Lrow.ap[64:65, 0:wq], p=po.ap[64:65, 0:wq]: e.copy(out=a, in_=p), r=(po,), w=(Lrow,))
                            pl = PS()
                            P.op("tensor", lambda e, o=pl.ap[0:64, 0:wq], l=C_onesf.ap[64:65, 0:64], r_=Lrow.ap[64:65, 0:wq]: e.matmul(o, lhsT=l, rhs=r_, start=True, stop=True), r=(C_onesf, Lrow), w=(pl,))
                            P.op("vector", lambda e, a=rcp.ap[:, 0:wq], p=pl.ap[0:64, 0:wq]: e.tensor_scalar(out=a, in0=p, scalar1=1e-30, scalar2=None, op0=ALU.max), r=(pl,), w=(rcp,))
                            P.op("vector", lambda e, a=rcp.ap[:, 0:wq]: e.reciprocal(out=a, in_=a), r=(rcp,), w=(rcp,))
                            yo = Yo[qi % 2]
                            P.op("vector", lambda e, a=yo.ap[:, 0:wq], p=po.ap[0:64, 0:wq], r_=rcp.ap[:, 0:wq]: e.tensor_tensor(out=a, in0=p, in1=r_, op=ALU.mult), r=(po, rcp), w=(yo,))
                            if quarter:
                                store(yo, d["YF"][h * 64:(h + 1) * 64, q0:q0 + wq], yo.ap[:, 0:wq])
                            else:
                                for (dst_, c0_, c1_) in ydst(d, 512 + h * 64, 512 + (h + 1) * 64, q0, wq):
                                    store(yo, dst_, yo.ap[:, c0_:c1_])
        for si, d in (enumerate(S) if STOP >= 4 else []):
            phase4(si, d)
        barrier()
        A.off = base0

        base = A.off
        Wu = [A.alloc([128, 8, 128], BF16, "Wu") for _ in range(4)]
        Wo = [A.alloc([128, 8, 128], BF16, "Wo") for _ in range(2)]
        Wd = [A.alloc([128, 22, 128], BF16, "Wd") for _ in range(2)]
        NT = 512
        Yt2 = [A.alloc([128, 8, NT], F32, "Yt")] * 2
        Yn2 = [A.alloc([128, 8, NT], BF16, "Yn") for _ in range(2)]
        xt2 = [A.alloc([128, 8, NT], F32, "xt") for _ in range(2)]
        n22 = [A.alloc([128, 8, NT], BF16, "n2") for _ in range(2)]
        rs2 = [A.alloc([128, NT], F32, "rs5") for _ in range(2)]
        tctr = [0]
        mt = A.alloc([128, 22, NT], BF16, "mt")
        ug = [A.alloc([128, NT], F32, "ug") for _ in range(2)]
        uv = [A.alloc([128, NT], F32, "uv") for _ in range(2)]
        cg = [A.alloc([128, NT], F32, "cg")] * 2
        cv = [A.alloc([128, NT], F32, "cv")] * 2
        sg = [A.alloc([128, NT], F32, "sg")] * 2
        fpv = A.alloc([128, 44, 2], F32, "fpv")
        UL = A.alloc([128, 44, 2], F32, "UL")
        wi = [0, 0, 0]

        def rstd_from(ps_ap, out_ap, n, scale, rs):
            P.op("vector", lambda e: e.tensor_scalar(out=out_ap, in0=ps_ap, scalar1=scale, scalar2=EPS, op0=ALU.mult, op1=ALU.add), r=(cur_ps[0],), w=(rs,))
            P.op("scalar", lambda e: e.activation(out=out_ap, in_=out_ap, func=AF.Ln), r=(rs,), w=(rs,))
            P.op("scalar", lambda e: e.activation(out=out_ap, in_=out_ap, func=AF.Exp, scale=-0.5), r=(rs,), w=(rs,))

        cur_ps = [None]
        Yw = None
        for d in (S if STOP >= 5 else []):
            quarter = "Yq" in d
            L = d["L"] // NQ if quarter else d["L"]
            load(fpv, fpv.ap, d["fprev"])
            if quarter:
                Yw = A.alloc([128, 8, 2 + L], BF16, "Yw")
                for kt in range(4):
                    load(Yw, Yw.ap[:, kt, :], d["YS"][kt * 128:(kt + 1) * 128, :])
                for kt in range(4, 8):
                    load(Yw, Yw.ap[:, kt, :], d["YF"][(kt - 4) * 128:(kt - 3) * 128, :])
            step = NT - 2
            tiles = []
            t = 0
            while t < L:
                n = min(step, L - t)
                tiles.append((t, n))
                t += n
            def tile_body(t0, n, Yt, Yn, xt, n2, rs):
                sq = n2
                yo5 = Yt
                W = n + 2
                if quarter:
                    hl = 2
                    c_lo = 0
                    ysrc = lambda kt: Yw.ap[:, kt, t0:t0 + W]
                    ytile = Yw
                    load(xt, xt.ap[:, :, 0:W], d["xw"][:, t0:t0 + W].rearrange("(k p) t -> p k t", p=128))
                else:
                    hl = 2 if t0 > 0 else 0
                    c_lo = 2 - hl
                    if hl == 0:
                        P.op("vector", lambda e: e.memset(Yt.ap[:, :, 0:2], 0.0), w=(Yt,))
                        P.op("vector", lambda e: e.memset(xt.ap[:, :, 0:2], 0.0), w=(xt,))
                    load(Yt, Yt.ap[:, :, c_lo:W], d["Y"][:, t0 - hl:t0 + n].rearrange("(k p) t -> p k t", p=128))
                    load(xt, xt.ap[:, :, c_lo:W], d["xT"][:, t0 - hl:t0 + n].rearrange("(k p) t -> p k t", p=128))
                    ysrc = lambda kt: Yt.ap[:, kt, 0:W]
                    ytile = Yt
                for kt in range(4):
                    P.op("vector", lambda e, a=sq.ap[:, kt, 0:W], x=ysrc(kt): e.tensor_tensor(out=a, in0=x, in1=x, op=ALU.mult), r=(ytile,), w=(sq,))
                for g in range(2):
                    ps = PS()
                    cur_ps[0] = ps
                    for k in range(2):
                        P.op("tensor", lambda e, o=ps.ap[:, 0:W], r_=sq.ap[:, 2 * g + k, 0:W], k_=k: e.matmul(o, lhsT=C_onesb.ap, rhs=r_, start=(k_ == 0), stop=(k_ == 1)), r=(C_onesb, sq), w=(ps,))
                    rstd_from(ps.ap[:, 0:W], rs.ap[:, 0:W], W, 1.0 / 256, rs)
                    for k in range(2):
                        kt = 2 * g + k
                        P.op("vector", lambda e, a=Yn.ap[:, kt, 0:W], x=ysrc(kt), s=C_gS.ap[:, kt:kt + 1], r_=rs.ap[:, 0:W]:
                             e.scalar_tensor_tensor(out=a, in0=x, scalar=s, in1=r_, op0=ALU.mult, op1=ALU.mult), r=(ytile, C_gS, rs), w=(Yn,))
                for kt in range(4, 8):
                    P.op("scalar", lambda e, a=Yn.ap[:, kt, 0:W], x=ysrc(kt): e.copy(out=a, in_=x), r=(ytile,), w=(Yn,))
                for m in range(8):
                    wo = Wo[wi[0] % 2]
                    wi[0] += 1
                    load(wo, wo.ap, wob[m])
                    ps = PS()
                    for kt in range(8):
                        P.op("tensor", lambda e, o=ps.ap[:, 0:W], l=wo.ap[:, kt, :], r_=Yn.ap[:, kt, 0:W], k_=kt: e.matmul(o, lhsT=l, rhs=r_, start=(k_ == 0), stop=(k_ == 7)), r=(wo, Yn), w=(ps,))
                    P.op("vector", lambda e, a=xt.ap[:, m, 0:W], p=ps.ap[:, 0:W]: e.tensor_tensor(out=a, in0=a, in1=p, op=ALU.add), r=(xt, ps), w=(xt,))
                for kt in range(8):
                    P.op("vector", lambda e, a=sq.ap[:, kt, 0:W], x=xt.ap[:, kt, 0:W]: e.tensor_tensor(out=a, in0=x, in1=x, op=ALU.mult), r=(xt,), w=(sq,))
                ps = PS()
                cur_ps[0] = ps
                for kt in range(8):
                    P.op("tensor", lambda e, o=ps.ap[:, 0:W], r_=sq.ap[:, kt, 0:W], k_=kt: e.matmul(o, lhsT=C_onesb.ap, rhs=r_, start=(k_ == 0), stop=(k_ == 7)), r=(C_onesb, sq), w=(ps,))
                rstd_from(ps.ap[:, 0:W], rs.ap[:, 0:W], W, 1.0 / D, rs)
                for kt in range(8):
                    P.op("vector", lambda e, a=n2.ap[:, kt, 0:W], x=xt.ap[:, kt, 0:W], s=C_g2.ap[:, kt:kt + 1], r_=rs.ap[:, 0:W]:
                         e.scalar_tensor_tensor(out=a, in0=x, scalar=s, in1=r_, op0=ALU.mult, op1=ALU.mult), r=(xt, C_g2, rs), w=(n2,))
                for j in range(22):
                    bsel = j % 2
                    res = []
                    for (mi, ub, cbuf) in ((j, ug[bsel], cg[bsel]), (j + 22, uv[bsel], cv[bsel])):
                        wu = Wu[wi[2] % 4]
                        wi[2] += 1
                        load(wu, wu.ap, wub[mi])
                        ps = PS()
                        for kt in range(8):
                            P.op("tensor", lambda e, o=ps.ap[:, 0:W], l=wu.ap[:, kt, :], r_=n2.ap[:, kt, 0:W], k_=kt:
                                 e.matmul(o, lhsT=l, rhs=r_, start=(k_ == 0), stop=(k_ == 7)), r=(wu, n2), w=(ps,))
                        P.op("scalar", lambda e, a=ub.ap[:, 0:W], p=ps.ap[:, 0:W]: e.copy(out=a, in_=p), r=(ps,), w=(ub,))
                        if hl == 0:
                            P.op("gpsimd", lambda e, a=ub.ap[:, 0:2], s_=fpv.ap[:, mi, :]: e.tensor_copy(out=a, in_=s_), r=(fpv,), w=(ub,))
                        if t0 + n == L:
                            P.op("gpsimd", lambda e, a=UL.ap[:, mi, :], s_=ub.ap[:, W - 2:W]: e.tensor_copy(out=a, in_=s_), r=(ub,), w=(UL,))
                        P.op("scalar", lambda e, a=cbuf.ap[:, 0:n], u_=ub.ap[:, 2:W], s=C_fw.ap[:, mi, 2:3], b_=C_fb.ap[:, mi:mi + 1]:
                             e.activation(out=a, in_=u_, func=AF.Identity, bias=b_, scale=s), r=(ub, C_fw, C_fb), w=(cbuf,))
                        P.op("vector", lambda e, a=cbuf.ap[:, 0:n], u_=ub.ap[:, 1:W - 1], s=C_fw.ap[:, mi, 1:2]:
                             e.scalar_tensor_tensor(out=a, in0=u_, scalar=s, in1=a, op0=ALU.mult, op1=ALU.add), r=(ub, C_fw, cbuf), w=(cbuf,))
                        P.op("vector", lambda e, a=cbuf.ap[:, 0:n], u_=ub.ap[:, 0:W - 2], s=C_fw.ap[:, mi, 0:1]:
                             e.scalar_tensor_tensor(out=a, in0=u_, scalar=s, in1=a, op0=ALU.mult, op1=ALU.add), r=(ub, C_fw, cbuf), w=(cbuf,))
                    sgb = sg[bsel]
                    P.op("scalar", lambda e, a=sgb.ap[:, 0:n], i_=cg[bsel].ap[:, 0:n]: e.activation(out=a, in_=i_, func=AF.Silu), r=(cg[bsel],), w=(sgb,))
                    P.op("gpsimd", lambda e, a=mt.ap[:, j, 0:n], x=sgb.ap[:, 0:n], y=cv[bsel].ap[:, 0:n]: e.tensor_tensor(out=a, in0=x, in1=y, op=ALU.mult), r=(sgb, cv[bsel]), w=(mt,))
                for m in range(8):
                    wd = Wd[wi[1] % 2]
                    wi[1] += 1
                    load(wd, wd.ap, wdb[m])
                    ps = PS()
                    for kt in range(22):
                        P.op("tensor", lambda e, o=ps.ap[:, 0:n], l=wd.ap[:, kt, :], r_=mt.ap[:, kt, 0:n], k_=kt: e.matmul(o, lhsT=l, rhs=r_, start=(k_ == 0), stop=(k_ == 21)), r=(wd, mt), w=(ps,))
                    P.op("vector", lambda e, a=xt.ap[:, m, 2:W], p=ps.ap[:, 0:n]: e.tensor_tensor(out=a, in0=a, in1=p, op=ALU.add), r=(xt, ps), w=(xt,))
                for kt in range(8):
                    P.op("vector", lambda e, a=sq.ap[:, kt, 0:n], x=xt.ap[:, kt, 2:W]: e.tensor_tensor(out=a, in0=x, in1=x, op=ALU.mult), r=(xt,), w=(sq,))
                ps = PS()
                cur_ps[0] = ps
                for kt in range(8):
                    P.op("tensor", lambda e, o=ps.ap[:, 0:n], r_=sq.ap[:, kt, 0:n], k_=kt: e.matmul(o, lhsT=C_onesb.ap, rhs=r_, start=(k_ == 0), stop=(k_ == 7)), r=(C_onesb, sq), w=(ps,))
                rstd_from(ps.ap[:, 0:n], rs.ap[:, 0:n], n, 1.0 / D, rs)
                for kt in range(8):
                    P.op("vector", lambda e, a=yo5.ap[:, kt, 0:n], x=xt.ap[:, kt, 2:W], s=C_gF.ap[:, kt:kt + 1], r_=rs.ap[:, 0:n]:
                         e.scalar_tensor_tensor(out=a, in0=x, scalar=s, in1=r_, op0=ALU.mult, op1=ALU.mult), r=(xt, C_gF, rs), w=(yo5,))
                store(yo5, d["yT"][:, t0:t0 + n].rearrange("(k p) t -> p k t", p=128), yo5.ap[:, :, 0:n])
            for ti_, (t0, n) in enumerate(tiles):
                par = tctr[0] % 2
                tctr[0] += 1
                tile_body(t0, n, Yt2[par], Yn2[par], xt2[par], n22[par], rs2[par])
            store(UL, d["fcv"], UL.ap)
        A.off = base
        barrier()

        if os.environ.get("KNOSCHED") is None:
            P.reschedule()
        P.finalize()
        with nc.Block() as block:
            @block.sync
            def _(e):
                P.emit("sync", e)

            @block.scalar
            def _(e):
                P.emit("scalar", e)

            @block.vector
            def _(e):
                P.emit("vector", e)

            @block.gpsimd
            def _(e):
                P.emit("gpsimd", e)

            @block.tensor
            def _(e):
                P.emit("tensor", e)
    return nc, seqs


_CACHE = {}


def _consts():
    cst = np.zeros((128, 6, 128), np.float32)
    cst[:, 0, :] = np.eye(128, dtype=np.float32)
    s = np.arange(128)[:, None]
    l = np.arange(128)[None, :]
    cst[:, 1, :] = np.where(l >= s, 0.0, NEG)
    cst[:, 2, :] = (s < l).astype(np.float32)
    cst[:, 3, :] = 1.0
    cvec = np.zeros((128, 16), np.float32)
    cvec[0, 0] = 1.0
    cvec[1, 1] = 1.0
    cvec[1, 2] = -1.0
    cvec[:, 3] = EPS
    cvec[:, 4] = 1.0
    rmask = np.ones((128, 512), np.float32)
    rmask[:, ::128] = 0.0
    return cst, cvec, rmask


def _pk(v, n):
    return np.ascontiguousarray(np.asarray(v, np.float32).reshape(n, 128).T)


def kernel(x_prompt, x_sample, cache_fox_k, cache_fox_v, cache_fox_logf, state_ssd, state_ssd_conv,
           state_ffn_conv, norm1_g, w_in, ssd_conv_w, ssd_conv_b, ssd_dt_bias, ssd_a_log, ssd_d,
           ssd_norm_g, fox_f_bias, w_out, norm2_g, w_up, ffn_conv_w, ffn_conv_b, w_down, final_norm_g):
    f = lambda a: np.ascontiguousarray(np.asarray(a, dtype=np.float32))
    x_prompt = f(x_prompt); x_sample = f(x_sample)
    B, LP, _ = x_prompt.shape
    NSB, LS, _ = x_sample.shape
    LC = cache_fox_k.shape[2]
    ncore = 8
    nsamp = NSB // ncore
    key = (LP, nsamp, LS, LC)
    if key not in _CACHE:
        _CACHE[key] = build(LP, nsamp, LS, LC)
    nc, seqs = _CACHE[key]
    cst, cvec, rmask = _consts()
    rep = lambda v: np.ascontiguousarray(np.broadcast_to(np.asarray(v, np.float32)[None, :], (128, len(v))))
    common = {
        "w_in": f(w_in[0]), "w_out": f(w_out[0]), "w_up": f(w_up[0]), "w_down": f(w_down[0]),
        "g1": _pk(norm1_g[0], 8), "g2": _pk(norm2_g[0], 8), "gF": _pk(final_norm_g, 8), "gS": _pk(ssd_norm_g[0], 4),
        "cw": np.ascontiguousarray(f(ssd_conv_w[0]).T.reshape(8, 128, 4).transpose(1, 0, 2)),
        "cb": _pk(ssd_conv_b[0], 8),
        "fw": np.ascontiguousarray(f(ffn_conv_w[0]).T.reshape(44, 128, 3).transpose(1, 0, 2)),
        "fb": _pk(ffn_conv_b[0], 44),
        "dtb": rep(ssd_dt_bias[0]), "alog": rep(ssd_a_log[0]), "dsk": rep(ssd_d[0]), "fbias": rep(fox_f_bias[0]),
        "cst": cst, "cvec": cvec, "rmask": rmask, "bar_src": np.zeros((1, 16), np.float32),
    }
    cfk = f(cache_fox_k[0]); cfv = f(cache_fox_v[0]); cfl = f(cache_fox_logf[0])
    sst = f(state_ssd[0]); scv = f(state_ssd_conv[0]); sfc = f(state_ffn_conv[0])
    in_maps = []
    for c in range(ncore):
        m = dict(common)
        b = c % B
        m["xT_0"] = np.ascontiguousarray(x_prompt[b].T)
        rq = c // B
        CHq = LP // 4
        xwm = np.zeros((D, 2 + CHq), np.float32)
        if rq > 0:
            xwm[:, 0:2] = x_prompt[b][rq * CHq - 2:rq * CHq].T
        xwm[:, 2:] = x_prompt[b][rq * CHq:(rq + 1) * CHq].T
        m["xw"] = xwm
        pp = np.arange(128)
        m["qidx"] = np.stack([((kt * 128 + pp) * 4 + rq) for kt in range(8)], axis=1).astype(np.int32)
        m["zsrc"] = np.zeros((128, CHq), np.float32)
        fp_ = np.zeros((NH * 3, CHq), np.float32)
        fp_[0::3, :] = -240000.0
        m["fpad"] = fp_
        gi = np.zeros((128, NH * 10), np.int32)
        p64 = np.minimum(pp, 63)
        p3 = np.minimum(pp, 2)
        for h_ in range(NH):
            gi[:, h_ * 10 + 0] = (h_ * 64 + p64) * 4 + rq
            gi[:, h_ * 10 + 5] = (h_ * 3 + p3) * 4 + rq
            for sq_ in range(4):
                slot = sq_ - (3 - rq)
                if slot < 0:
                    slot = 4
                gi[:, h_ * 10 + 1 + sq_] = (h_ * 64 + p64) * 5 + slot
                gi[:, h_ * 10 + 6 + sq_] = (h_ * 3 + p3) * 5 + slot
        m["gidx"] = gi
        dp_ = np.full((NH, CHq), -100.0, np.float32)
        m["dpad"] = dp_
        si_ = np.zeros((128, NH * 20), np.int32)
        for h_ in range(NH):
            g_ = h_ // 4
            for sq_ in range(4):
                slot = sq_ - (3 - rq)
                if slot < 0:
                    slot = 4
                cb_ = h_ * 20 + sq_ * 5
                si_[:, cb_ + 0] = (h_ * 64 + (pp % 64)) * 5 + slot
                si_[:, cb_ + 1] = (512 + g_ * 128 + pp) * 5 + slot
                si_[:, cb_ + 2] = (768 + g_ * 128 + pp) * 5 + slot
                si_[:, cb_ + 3] = (h_ * 64 + p64) * 5 + slot
                si_[:, cb_ + 4] = h_ * 5 + slot
        m["sidx"] = si_
        m["st0_0"] = np.zeros((NH, 128, 64), np.float32)
        m["cprev_0"] = np.zeros((D, 3), np.float32)
        m["fprev_0"] = np.zeros((128, 44, 2), np.float32)
        for i in range(nsamp):
            s = c * nsamp + i
            n = "_%d" % (i + 1)
            m["xT" + n] = np.ascontiguousarray(x_sample[s].T)
            m["st0" + n] = np.ascontiguousarray(sst[s].transpose(0, 2, 1))
            m["cprev" + n] = np.ascontiguousarray(scv[s].T)
            m["fprev" + n] = np.ascontiguousarray(sfc[s].T.reshape(44, 128, 2).transpose(1, 0, 2))
            m["cKT" + n] = np.ascontiguousarray(cfk[s].transpose(1, 2, 0))
            m["cV" + n] = np.ascontiguousarray(cfv[s].transpose(1, 0, 2))
            m["cLF" + n] = np.ascontiguousarray(cfl[s].T)
        in_maps.append(m)
    res = run_bass_kernel_spmd(nc, in_maps, core_ids=list(range(ncore))).results

    def seq_out(r, i, L):
        n = "_%d" % i
        y = r["yT" + n].T if i > 0 else None
        k = r["kT" + n].T.reshape(L, NH, 64)
        v = r["vT" + n].T.reshape(L, NH, 64)
        lf = r["lf" + n].T
        ss = r["sst" + n].transpose(0, 2, 1)
        sc = r["scv" + n].T
        fc = r["fcv" + n].transpose(1, 0, 2).reshape(2 * DFF, 2).T
        return [np.ascontiguousarray(a, dtype=np.float32) if a is not None else None for a in (y, k, v, lf, ss, sc, fc)]
    pr = [seq_out(res[b], 0, LP) for b in range(B)]
    CHq = LP // 4
    for b in range(B):
        yfull = np.zeros((LP, D), np.float32)
        for rq in range(4):
            c = rq * B + b
            yfull[rq * CHq:(rq + 1) * CHq] = res[c]["yT_0"].T
        pr[b][0] = yfull
        pr[b][6] = np.ascontiguousarray(res[3 * B + b]["fcv_0"].transpose(1, 0, 2).reshape(2 * DFF, 2).T)
        pr[b][4] = np.ascontiguousarray(res[3 * B + b]["sst_0"].transpose(0, 2, 1))
    sm = [seq_out(res[s // nsamp], 1 + s % nsamp, LS) for s in range(NSB)]
    outs_p = [np.stack([p[j] for p in pr])[None] if j > 0 else np.stack([p[j] for p in pr]) for j in range(7)]
    outs_s = [np.stack([p[j] for p in sm])[None] if j > 0 else np.stack([p[j] for p in sm]) for j in range(7)]
    return tuple([outs_p[0], outs_s[0]] + outs_p[1:] + outs_s[1:])
```
